# Optimizing a Trainium2 kernel written in Bass

```python
import math
import jax, jax.numpy as jnp
from jax import lax
import numpy as np

D_MODEL = 1024
BATCH = 32
SEQ = 2048
DEPTH = 2

HEAD_DIM = 64
N_MIX_HEADS = D_MODEL // HEAD_DIM
CONV_WIDTH = D_MODEL // 4
CONV_GROUPS = CONV_WIDTH // HEAD_DIM
RET_HEADS = (N_MIX_HEADS - CONV_GROUPS) // 2
MOBA_HEADS = N_MIX_HEADS - CONV_GROUPS - RET_HEADS
RET_WIDTH = RET_HEADS * HEAD_DIM
MOBA_WIDTH = MOBA_HEADS * HEAD_DIM
IN_COLS = 4 * RET_WIDTH + 3 * MOBA_WIDTH + 3 * CONV_WIDTH
CONV_K = 3
RET_CHUNK = 128
MOBA_BLOCK = 256
MOBA_TOPK = 3
MOBA_Q_CHUNK = 16
REL_BUCKETS = 32
REL_MAX_DIST = 128
ROPE_BASE = 10000.0
D_FF = 256 * ((8 * D_MODEL // 3 + 255) // 256)
DN_ALPHA = (2.0 * DEPTH) ** 0.25
DN_BETA = (8.0 * DEPTH) ** -0.25
LN_EPS = 1e-5

kernel_name = "hybrid_retention_moba_shortconv_deepnorm"

F32 = jnp.float32


def layer_norm(x, g, b):
    xf = x.astype(F32)
    mu = jnp.mean(xf, axis=-1, keepdims=True)
    var = jnp.mean(jnp.square(xf - mu), axis=-1, keepdims=True)
    y = (xf - mu) * lax.rsqrt(var + LN_EPS)
    return (y * g.astype(F32) + b.astype(F32)).astype(x.dtype)


def causal_dwconv(x, w):
    K = w.shape[0]
    S = x.shape[1]
    xp = jnp.pad(x, ((0, 0), (K - 1, 0), (0, 0)))
    y = xp[:, 0:S] * w[0]
    for i in range(1, K):
        y = y + xp[:, i:i + S] * w[i]
    return y


def split_heads(t, n_heads):
    B, S, _ = t.shape
    return t.reshape(B, S, n_heads, HEAD_DIM).transpose(0, 2, 1, 3)


def merge_heads(t):
    B, H, S, dh = t.shape
    return t.transpose(0, 2, 1, 3).reshape(B, S, H * dh)


def rotary(t):
    S, dh = t.shape[2], t.shape[3]
    inv = ROPE_BASE ** (-jnp.arange(0, dh, 2, dtype=F32) / dh)
    ang = jnp.arange(S, dtype=F32)[:, None] * inv[None, :]
    cos, sin = jnp.cos(ang), jnp.sin(ang)
    t1, t2 = t[..., : dh // 2], t[..., dh // 2:]
    return jnp.concatenate([t1 * cos - t2 * sin, t1 * sin + t2 * cos], axis=-1)


def retention(q, k, v):
    B, H, S, dh = q.shape
    C = RET_CHUNK
    n = S // C
    log_g = jnp.log(1.0 - 2.0 ** (-5.0 - jnp.arange(H, dtype=F32)))
    idx = jnp.arange(C, dtype=F32)
    diff = idx[:, None] - idx[None, :]
    decay = jnp.where(diff >= 0, jnp.exp(jnp.maximum(diff, 0.0)[None] * log_g[:, None, None]), 0.0)
    xi = jnp.exp((idx + 1.0)[None, :] * log_g[:, None])
    zeta = jnp.exp((C - 1.0 - idx)[None, :] * log_g[:, None])
    g_chunk = jnp.exp(C * log_g)

    def to_chunks(t):
        return t.reshape(B, H, n, C, dh).transpose(2, 0, 1, 3, 4)

    def step(state, qkv):
        qc, kc, vc = qkv
        scores = jnp.einsum('bhcd,bhed->bhce', qc, kc) * decay[None]
        inner = jnp.einsum('bhce,bhev->bhcv', scores, vc)
        cross = jnp.einsum('bhcd,bhdv->bhcv', qc, state) * xi[None, :, :, None]
        state = state * g_chunk[None, :, None, None] + jnp.einsum(
            'bhcd,bhcv->bhdv', kc * zeta[None, :, :, None], vc)
        return state, inner + cross

    state0 = jnp.zeros((B, H, dh, dh), F32)
    _, out = lax.scan(step, state0, (to_chunks(q), to_chunks(k), to_chunks(v)))
    return out.transpose(1, 2, 0, 3, 4).reshape(B, H, S, dh)


def t5_bucket(dist):
    n = jnp.maximum(dist, 0)
    max_exact = REL_BUCKETS // 2
    nf = jnp.maximum(n, 1).astype(F32)
    large = max_exact + (jnp.log(nf / max_exact) / math.log(REL_MAX_DIST / max_exact)
                         * (REL_BUCKETS - max_exact)).astype(jnp.int32)
    large = jnp.minimum(large, REL_BUCKETS - 1)
    return jnp.where(n < max_exact, n, large)


def moba_attention(q, k, v, rel_bias):
    B, H, S, dh = q.shape
    L = MOBA_BLOCK
    nb = -(-S // L)
    pad = nb * L - S
    kp = jnp.pad(k, ((0, 0), (0, 0), (0, pad), (0, 0))).reshape(B, H, nb, L, dh)
    vp = jnp.pad(v, ((0, 0), (0, 0), (0, pad), (0, 0))).reshape(B, H, nb, L, dh)
    k_mean = jnp.mean(kp, axis=3)
    topk = min(MOBA_TOPK, nb)
    scale = dh ** -0.5
    rel_t = rel_bias.T.astype(F32)
    bi = jnp.arange(B)[:, None, None, None]
    hi = jnp.arange(H)[None, :, None, None]
    Qc = MOBA_Q_CHUNK
    nc = S // Qc
    q_chunks = q.reshape(B, H, nc, Qc, dh).transpose(2, 0, 1, 3, 4)

    def attend(args):
        qi, ci = args
        q_pos = ci * Qc + jnp.arange(Qc, dtype=jnp.int32)
        own = (ci * Qc) // L
        gate = jnp.einsum('bhqd,bhnd->bhqn', qi, k_mean)
        gate = jnp.where(jnp.arange(nb) < own, gate, -jnp.inf)
        _, sel = lax.top_k(gate, topk)
        sel_valid = jnp.arange(topk) < own
        kg = kp[bi, hi, sel]
        vg = vp[bi, hi, sel]
        k_pos_sel = sel[..., None] * L + jnp.arange(L, dtype=jnp.int32)
        b_sel = rel_t[hi[..., None], t5_bucket(q_pos[:, None, None] - k_pos_sel)]
        s_sel = jnp.einsum('bhqd,bhqjld->bhqjl', qi, kg) * scale + b_sel
        s_sel = jnp.where(sel_valid[:, None], s_sel, -jnp.inf)
        k_own = lax.dynamic_index_in_dim(kp, own, axis=2, keepdims=False)
        v_own = lax.dynamic_index_in_dim(vp, own, axis=2, keepdims=False)
        k_pos_own = own * L + jnp.arange(L, dtype=jnp.int32)
        dist_own = q_pos[:, None] - k_pos_own[None, :]
        b_own = rel_t[:, t5_bucket(dist_own)]
        s_own = jnp.einsum('bhqd,bhld->bhql', qi, k_own) * scale + b_own[None]
        s_own = jnp.where(dist_own >= 0, s_own, -jnp.inf)
        logits = jnp.concatenate([s_sel.reshape(B, H, Qc, topk * L), s_own], axis=-1)
        p = jax.nn.softmax(logits, axis=-1)
        p_sel = p[..., : topk * L].reshape(B, H, Qc, topk, L)
        p_own = p[..., topk * L:]
        return (jnp.einsum('bhqjl,bhqjld->bhqd', p_sel, vg)
                + jnp.einsum('bhql,bhld->bhqd', p_own, v_own))

    out = lax.map(attend, (q_chunks, jnp.arange(nc, dtype=jnp.int32)))
    return out.transpose(1, 2, 0, 3, 4).reshape(B, H, S, dh)


def head_group_norm(t):
    mu = jnp.mean(t, axis=-1, keepdims=True)
    var = jnp.mean(jnp.square(t - mu), axis=-1, keepdims=True)
    return (t - mu) * lax.rsqrt(var + LN_EPS)


def hybrid_mixer(h, w_in, conv_w, w_out, rel_bias):
    proj = h @ w_in
    o = np.cumsum([0, RET_WIDTH, RET_WIDTH, RET_WIDTH, RET_WIDTH,
                   MOBA_WIDTH, MOBA_WIDTH, MOBA_WIDTH, CONV_WIDTH, CONV_WIDTH, CONV_WIDTH])
    parts = [proj[..., int(o[i]):int(o[i + 1])] for i in range(10)]
    rq, rk, rv, rg, mq, mk, mv, cb, cc, ch = parts
    scale = HEAD_DIM ** -0.5
    q_r = rotary(split_heads(rq, RET_HEADS).astype(F32))
    k_r = rotary(split_heads(rk, RET_HEADS).astype(F32)) * scale
    v_r = split_heads(rv, RET_HEADS).astype(F32)
    y_ret = merge_heads(head_group_norm(retention(q_r, k_r, v_r))) * jax.nn.silu(rg.astype(F32))
    y_moba = merge_heads(moba_attention(split_heads(mq, MOBA_HEADS).astype(F32),
                                        split_heads(mk, MOBA_HEADS).astype(F32),
                                        split_heads(mv, MOBA_HEADS).astype(F32), rel_bias))
    y_conv = cb.astype(F32) * causal_dwconv(cc.astype(F32) * ch.astype(F32), conv_w.astype(F32))
    cat = jnp.concatenate([y_ret, y_moba, y_conv], axis=-1).astype(h.dtype)
    return cat @ w_out


def conv_ffn(h, w_up, ffn_conv_w, w_down):
    u = h @ w_up
    a, b = u[..., :D_FF], u[..., D_FF:]
    a = causal_dwconv(a, ffn_conv_w)
    return (jax.nn.gelu(a.astype(F32), approximate=False) * b.astype(F32)).astype(h.dtype) @ w_down


def setup_inputs(seed: int = 0) -> dict:
    key = jax.random.key(seed)
    ks = jax.random.split(key, 12)
    nrm = jax.random.normal
    x = nrm(ks[0], (BATCH, SEQ, D_MODEL), F32)
    w_in = nrm(ks[1], (DEPTH, D_MODEL, IN_COLS), F32) * D_MODEL ** -0.5
    conv_w = nrm(ks[2], (DEPTH, CONV_K, CONV_WIDTH), F32) * CONV_K ** -0.5
    w_out = nrm(ks[3], (DEPTH, D_MODEL, D_MODEL), F32) * (D_MODEL ** -0.5 * DN_BETA)
    ln1_g = 1.0 + 0.02 * nrm(ks[4], (DEPTH, D_MODEL), F32)
    ln1_b = 0.02 * nrm(ks[5], (DEPTH, D_MODEL), F32)
    w_up = nrm(ks[6], (DEPTH, D_MODEL, 2 * D_FF), F32) * D_MODEL ** -0.5
    ffn_conv_w = nrm(ks[7], (DEPTH, CONV_K, D_FF), F32) * CONV_K ** -0.5
    w_down = nrm(ks[8], (DEPTH, D_FF, D_MODEL), F32) * (D_FF ** -0.5 * DN_BETA)
    ln2_g = 1.0 + 0.02 * nrm(ks[9], (DEPTH, D_MODEL), F32)
    ln2_b = 0.02 * nrm(ks[10], (DEPTH, D_MODEL), F32)
    rel_bias = 0.2 * nrm(ks[11], (REL_BUCKETS, MOBA_HEADS), F32)
    return {"x": x, "w_in": w_in, "conv_w": conv_w, "w_out": w_out,
            "ln1_g": ln1_g, "ln1_b": ln1_b, "w_up": w_up, "ffn_conv_w": ffn_conv_w,
            "w_down": w_down, "ln2_g": ln2_g, "ln2_b": ln2_b, "rel_bias": rel_bias}


def reference(x, w_in, conv_w, w_out, ln1_g, ln1_b, w_up, ffn_conv_w, w_down, ln2_g, ln2_b, rel_bias):
    for l in range(DEPTH):
        mix = hybrid_mixer(x, w_in[l], conv_w[l], w_out[l], rel_bias)
        x = layer_norm(DN_ALPHA * x + mix, ln1_g[l], ln1_b[l])
        f = conv_ffn(x, w_up[l], ffn_conv_w[l], w_down[l])
        x = layer_norm(DN_ALPHA * x + f, ln2_g[l], ln2_b[l])
    return x
```

```python
import contextlib
import math
import numpy as np
import concourse.bass as bass
import concourse.mybir as mybir
from concourse.bass_utils import run_bass_kernel_spmd

F32 = mybir.dt.float32
BF16 = mybir.dt.bfloat16
AF = mybir.ActivationFunctionType
ALU = mybir.AluOpType
AX = mybir.AxisListType

N_CORES = 8
SEQ = 2048
D = 1024
NT = SEQ // 128
DEPTH = 2
IN_COLS = 3456
D_FF = 2816
ALPHA = (2.0 * DEPTH) ** 0.25
LN_EPS = 1e-5
NEG = -30000.0
BIG = 3.0e38

ENGS = ["pe", "act", "dve", "pool", "sp"]
SEM_CH = 12000


def I(name, *a, **k):
    return (name, a, k)


class Sched:
    def __init__(self, nc):
        self.nc = nc
        self.ops = {e: [] for e in ENGS}
        self.count = {e: 0 for e in ENGS}
        self.seen = {e: {} for e in ENGS}
        self.res = {}
        self.dma_cnt = {}

    def _deps(self, eng, reads, writes, skip_dma=None):
        deps = []
        for r in reads:
            st = self.res.get(r)
            if st and st[0] is not None:
                deps.append((st[0], True))
            if st and isinstance(r, tuple) and r[0] == "ps":
                for t in st[1]:
                    if t[1] != eng:
                        deps.append((t, True))
        for w in writes:
            st = self.res.get(w)
            if st:
                if st[0] is not None:
                    deps.append((st[0], False))
                for t in st[1]:
                    deps.append((t, False))
        seen = self.seen[eng]
        best = {}
        for tok, raw in deps:
            kind, key, val = tok
            if kind == "eng" and key == eng and not raw and eng == "pe":
                continue
            if kind == "dma" and key == skip_dma:
                continue
            k = (kind, key)
            if seen.get(k, 0) >= val:
                continue
            best[k] = max(best.get(k, 0), val)
        for k, v in best.items():
            seen[k] = v
        return [(k[0], k[1], v) for k, v in best.items()]

    def _commit(self, tok, reads, writes):
        for r in reads:
            st = self.res.setdefault(r, [None, []])
            st[1].append(tok)
        for w in writes:
            self.res[w] = [tok, []]

    def op(self, eng, fn, reads=(), writes=()):
        waits = self._deps(eng, reads, writes)
        self.count[eng] += 1
        tok = ("eng", eng, self.count[eng])
        self.ops[eng].append((waits, fn, tok))
        self._commit(tok, reads, writes)
        return tok

    def dma(self, q, sem, fn, reads=(), writes=()):
        waits = self._deps(q, reads, writes, skip_dma=sem)
        self.dma_cnt[sem] = self.dma_cnt.get(sem, 0) + 16
        tok = ("dma", sem, self.dma_cnt[sem])
        self.ops[q].append((waits, fn, tok))
        self._commit(tok, reads, writes)
        return tok

    def barrier(self):
        for e in ENGS:
            waits = []
            seen = self.seen[e]
            for e2 in ENGS:
                if e2 != e and self.count[e2] > seen.get(("eng", e2), 0):
                    waits.append(("eng", e2, self.count[e2]))
                    seen[("eng", e2)] = self.count[e2]
            for s, c in self.dma_cnt.items():
                if c > seen.get(("dma", s), 0):
                    waits.append(("dma", s, c))
                    seen[("dma", s)] = c
            if waits:
                self.ops[e].append((waits, None, None))
        self.res = {}

    def emit(self):
        nc = self.nc
        needed = {e: set() for e in ENGS}
        for e in ENGS:
            for waits, fn, tok in self.ops[e]:
                for kind, key, val in waits:
                    if kind == "eng":
                        needed[key].add(val)
        rank = {}
        for e in ENGS:
            rank[e] = {s: i + 1 for i, s in enumerate(sorted(needed[e]))}
        with contextlib.ExitStack() as es:
            esem = {}
            for e in ENGS:
                n = (len(rank[e]) + SEM_CH - 1) // SEM_CH
                esem[e] = [es.enter_context(nc.semaphore(f"s_{e}_{i}")) for i in range(max(n, 1))]
            dsem = {name: es.enter_context(nc.semaphore(f"d_{name}")) for name in self.dma_cnt}

            def lower(tok):
                kind, key, val = tok
                if kind == "eng":
                    r = rank[key][val]
                    return esem[key][(r - 1) // SEM_CH], (r - 1) % SEM_CH + 1
                return dsem[key], val

            def run(engobj, name):
                for waits, fn, tok in self.ops[name]:
                    for w in waits:
                        s, v = lower(w)
                        engobj.wait_ge(s, v)
                    if fn is None:
                        continue
                    ins = getattr(engobj, fn[0])(*fn[1], **fn[2])
                    if tok[0] == "eng":
                        if tok[2] in rank[name]:
                            s, v = lower(tok)
                            ins.then_inc(s, 1)
                    else:
                        s, v = lower(tok)
                        ins.then_inc(s, 16)

            with nc.Block() as block:
                @block.tensor
                def _(e):
                    run(e, "pe")

                @block.scalar
                def _(e):
                    run(e, "act")

                @block.vector
                def _(e):
                    run(e, "dve")

                @block.gpsimd
                def _(e):
                    run(e, "pool")

                @block.sync
                def _(e):
                    run(e, "sp")


class Arena:
    def __init__(self, t, words):
        self.t = t
        self.words = words
        self.off = 0

    def reset(self):
        self.off = 0

    def f32(self, n):
        assert self.off + n <= self.words, ("arena overflow", self.off + n, self.words)
        v = self.t[:, self.off:self.off + n]
        self.off += n
        return v

    def bf16(self, n):
        w = (n + 1) // 2
        assert self.off + w <= self.words, ("arena overflow", self.off + w, self.words)
        v = self.t[:, self.off:self.off + w].bitcast(BF16)
        self.off += w
        return v


SM_OWN = 0
SM_MASK = 128
SM_VSC = 256
SM_XI = 262
SM_ZETA = 268
SM_EPS = 274
SM_GC = 275
SM_W = 280


def _t5_bucket(dist):
    n = np.maximum(dist, 0)
    nf = np.maximum(n, 1).astype(np.float32)
    large = 16 + (np.log(nf / np.float32(16)) / np.float32(math.log(128 / 16)) * np.float32(16)).astype(np.int32)
    large = np.minimum(large, 31)
    return np.where(n < 16, n, large)


def _host_consts():
    p = np.arange(128)
    inv = (10000.0 ** (-np.arange(0, 64, 2, dtype=np.float32) / np.float32(64))).astype(np.float32)
    pos = (np.arange(NT)[None, :] * 128 + p[:, None]).astype(np.float32)
    ang = pos[:, :, None] * inv[None, None, :]
    cs = np.stack([np.concatenate([np.cos(ang), np.cos(ang)], axis=-1), np.concatenate([-np.sin(ang), np.sin(ang)], axis=-1)], axis=1).astype(np.float32)
    small = np.zeros((128, SM_W), np.float32)
    own = np.zeros((NT, 8), np.float32)
    for qt in range(NT):
        o = qt // 2
        own[qt, o] = BIG
        own[qt, o + 1:] = -BIG
    small[:, SM_OWN:SM_OWN + 128] = own.reshape(1, 128)
    e = p[:, None]; c = p[None, :]
    small[:, SM_MASK:SM_MASK + 128] = np.where(c >= e, 0.125, 0.0)
    g = 1.0 - 2.0 ** (-5.0 - np.arange(6))
    small[:, SM_VSC:SM_VSC + 6] = g[None, :] ** (-(p[:, None] + 1.0))
    small[:, SM_XI:SM_XI + 6] = g[None, :] ** (p[:, None] + 1.0)
    small[:, SM_ZETA:SM_ZETA + 6] = 0.125 * g[None, :] ** (127.0 - p[:, None])
    small[:, SM_EPS] = LN_EPS
    for j in range(3):
        small[0:64, SM_GC + j] = g[2 * j] ** 128.0
        small[64:128, SM_GC + j] = g[2 * j + 1] ** 128.0
    gC = [float(x) for x in g ** 128.0]
    ind = (np.arange(SEQ)[None, :] // 256 == np.arange(8)[:, None]).astype(np.float32)
    bidx_d = _t5_bucket(c - e)
    bidx_s = _t5_bucket(128 + c - e)
    causal = (c >= e)
    return cs, small, gC, ind, bidx_d, bidx_s, causal


class _Stop(Exception):
    pass


def build(nseq=4, depth=DEPTH, stop=None):
    cs_h, small_h, gC, ind_h, _, _, _ = _host_consts()
    nc = bass.Bass("TRN2", target_bir_lowering=False)
    x_d = nc.dram_tensor("x", [nseq, SEQ, D], F32, kind="ExternalInput").ap()
    w_in_d = nc.dram_tensor("w_in", [DEPTH, D, IN_COLS], F32, kind="ExternalInput").ap()
    w_out_d = nc.dram_tensor("w_out", [DEPTH, D, D], F32, kind="ExternalInput").ap()
    w_up_d = nc.dram_tensor("w_up", [DEPTH, D, 2 * D_FF], F32, kind="ExternalInput").ap()
    w_dn_d = nc.dram_tensor("w_down", [DEPTH, D_FF, D], F32, kind="ExternalInput").ap()
    lnp_d = nc.dram_tensor("lnp", [DEPTH, 4, 128, D], F32, kind="ExternalInput").ap()
    cw_d = nc.dram_tensor("cw", [DEPTH, 128, 6], F32, kind="ExternalInput").ap()
    fcw_d = nc.dram_tensor("fcw", [DEPTH, 128, 66], F32, kind="ExternalInput").ap()
    cs_d = nc.dram_tensor("cs", [128, 2 * NT * 64], F32, kind="ExternalInput").ap()
    bt_d = nc.dram_tensor("btile", [128, 6 * 2 * 128], F32, kind="ExternalInput").ap()
    cfar_d = nc.dram_tensor("cfar", [128, 6], F32, kind="ExternalInput").ap()
    small_d = nc.dram_tensor("small", [128, SM_W], F32, kind="ExternalInput").ap()
    ind_d = nc.dram_tensor("ind", [8, SEQ], F32, kind="ExternalInput").ap()
    out_d = nc.dram_tensor("out", [nseq, SEQ, D], F32, kind="ExternalOutput").ap()
    win_b = nc.dram_tensor("win_b", [DEPTH, D, IN_COLS], BF16).ap()
    wout_b = nc.dram_tensor("wout_b", [DEPTH, D, D], BF16).ap()
    wup_b = nc.dram_tensor("wup_b", [DEPTH, 11, 128, 8 * 512], BF16).ap()
    wdn_b = nc.dram_tensor("wdn_b", [DEPTH, D_FF, D], BF16).ap()
    ind_b = nc.dram_tensor("ind_b", [8, SEQ], BF16).ap()

    S = Sched(nc)

    def ck(tag):
        if stop == tag:
            raise _Stop()
    ARENA_W = 20992
    with contextlib.ExitStack() as es:
        def sb(name, shape, dt=F32):
            return es.enter_context(nc.sbuf_tensor("sb_" + name, shape, dt))

        xs = sb("xs", [128, NT, D])
        hT = sb("hT", [128, 8, SEQ], BF16)
        cs = sb("cs", [128, 2, NT, 64])
        bt = sb("bt", [128, 6, 2, 128])
        cfar = sb("cfar", [128, 6])
        small = sb("small", [128, SM_W])
        lnp = sb("lnp", [128, 2, D])
        cw = sb("cw", [128, 6])
        fcw = sb("fcw", [128, 66])
        ident = sb("ident", [128, 128], BF16)
        ones_f = sb("ones_f", [128, 64])
        xb = [sb("xb0", [128, D], BF16), sb("xb1", [128, D], BF16)]
        lnst = sb("lnst", [128, 4, 16])
        arena_t = sb("arena", [128, ARENA_W])
        A = Arena(arena_t, ARENA_W)
        pb = [es.enter_context(nc.psum_tensor(f"pb{i}", [128, 512], F32)) for i in range(8)]

        def pbb(i):
            return pb[i][:, :].bitcast(BF16)

        S.dma("sp", "c_cs", I("dma_start", out=cs[:].rearrange("p a t j -> p (a t j)"), in_=cs_d), writes=["cs"])
        S.dma("sp", "c_bt", I("dma_start", out=bt[:].rearrange("p h a q -> p (h a q)"), in_=bt_d), writes=["bt"])
        S.dma("sp", "c_cf", I("dma_start", out=cfar[:], in_=cfar_d), writes=["cfar"])
        S.dma("sp", "c_sm", I("dma_start", out=small[:], in_=small_d), writes=["small"])
        identf = A.f32(128)
        S.op("pool", I("memset", identf[:], 0.0), writes=["identf"])
        S.op("pool", I("affine_select", out=identf[:], in_=identf[:], pattern=[[-1, 128]],
                                              compare_op=ALU.not_equal, fill=1.0, base=0, channel_multiplier=1),
             reads=["identf"], writes=["identf"])
        S.op("dve", I("tensor_copy", out=ident[:], in_=identf[:]), reads=["identf"], writes=["ident"])
        S.op("pool", I("memset", ones_f[:], 1.0), writes=["ones_f"])
        for h in range(6):
            S.op("dve", I("tensor_scalar", out=bt[:, h, :, :], in0=bt[:, h, :, :], scalar1=cfar[:, h:h + 1], scalar2=None, op0=ALU.subtract),
                 reads=["bt", "cfar"], writes=["bt"])
        S.barrier()

        ownmask = small[:, SM_OWN:SM_OWN + 128].rearrange("p (t n) -> p t n", n=8)
        mask8 = small[:, SM_MASK:SM_MASK + 128]
        vsc1 = small[:, SM_VSC:SM_VSC + 6]
        xi = small[:, SM_XI:SM_XI + 6]
        zeta8 = small[:, SM_ZETA:SM_ZETA + 6]
        eps_t = small[:, SM_EPS:SM_EPS + 1]
        gct = small[:, SM_GC:SM_GC + 3]

        def bc3(ap2, n):
            a = ap2.shape[1]
            return ap2.unsqueeze(2).to_broadcast([128, a, n])

        def bcm(ap2, m):
            n = ap2.shape[1]
            return ap2.unsqueeze(1).to_broadcast([128, m, n])

        def wload(sem, dst3, scr2, r0, nrow_chunks, c0, c1, res):
            for k in range(nrow_chunks):
                S.dma("sp", sem, I("dma_start", out=dst3[:, k, :], in_=scr2[r0 + k * 128:r0 + (k + 1) * 128, c0:c1]), writes=[res])

        def emit_hT(t):
            b = t % 2
            S.op("act", I("activation", out=xb[b][:], in_=xs[:, t, :], func=AF.Copy),
                 reads=[("xs", t)], writes=[("xb", b)])
            p4 = pbb(4)
            for c in range(8):
                S.op("pe", I("transpose", out=p4[:, c * 128:(c + 1) * 128], in_=xb[b][:, c * 128:(c + 1) * 128],
                                                     identity=ident[:]),
                     reads=[("xb", b), "ident"], writes=[("ps", 4)])
            S.op("dve", I("tensor_copy", out=hT[:, :, t * 128:(t + 1) * 128],
                                                in_=p4.rearrange("p (c k) -> p c k", k=128)),
                 reads=[("ps", 4)], writes=[("hT", t)])

        def emit_ln_batch(tiles):
            n = len(tiles)
            for i, t in enumerate(tiles):
                S.op("dve", I("bn_stats", lnst[:, i, 0:6], xs[:, t, 0:512]), reads=[("xs", t)], writes=[("lnst0", i)])
                S.op("dve", I("bn_stats", lnst[:, i, 6:12], xs[:, t, 512:1024]), reads=[("xs", t)], writes=[("lnst1", i)])
                S.op("dve", I("bn_aggr", lnst[:, i, 12:14], lnst[:, i, 0:12]), reads=[("lnst0", i), ("lnst1", i)], writes=[("lnmv", i)])
            mvr = [("lnmv", i) for i in range(n)]
            S.op("act", I("activation", out=lnst[:, 0:n, 14], in_=lnst[:, 0:n, 13], func=AF.Ln, bias=eps_t, scale=1.0), reads=mvr, writes=["lnr"])
            S.op("act", I("activation", out=lnst[:, 0:n, 14], in_=lnst[:, 0:n, 14], func=AF.Exp, scale=-0.5), reads=["lnr"], writes=["lnr"])
            S.op("dve", I("scalar_tensor_tensor", out=lnst[:, 0:n, 15], in0=lnst[:, 0:n, 12], scalar=-1.0, in1=lnst[:, 0:n, 14], op0=ALU.mult, op1=ALU.mult),
                 reads=["lnr"] + mvr, writes=["lnn"])
            for i, t in enumerate(tiles):
                xt = xs[:, t, :]
                S.op("act", I("activation", out=xt, in_=xt, func=AF.Identity, bias=lnst[:, i, 15:16], scale=lnst[:, i, 14:15]),
                     reads=[("xs", t), "lnn", "lnr"], writes=[("xs", t)])
            for i, t in enumerate(tiles):
                xt = xs[:, t, :]
                S.op("dve", I("tensor_tensor", out=xt, in0=xt, in1=lnp[:, 0, :], op=ALU.mult), reads=[("xs", t), "lnp"], writes=[("xs", t)])
                S.op("dve", I("tensor_tensor", out=xt, in0=xt, in1=lnp[:, 1, :], op=ALU.add), reads=[("xs", t), "lnp"], writes=[("xs", t)])

        def load_lnp(l, gi):
            S.dma("sp", "lnp", I("dma_start", out=lnp[:, 0, :], in_=lnp_d[l, gi]),
                  writes=["lnp"])
            S.dma("sp", "lnp", I("dma_start", out=lnp[:, 1, :], in_=lnp_d[l, gi + 1]),
                  writes=["lnp"])

        def outproj_partial(t, lhs_fn, nk, wo3, first, lhs_reads):
            for hf in range(2):
                for j in range(nk):
                    S.op("pe", I("matmul", out=pb[6 + hf][:, 0:512], lhsT=lhs_fn(j),
                                                               rhs=wo3[:, j, hf * 512:(hf + 1) * 512],
                                                               start=(j == 0), stop=(j == nk - 1)),
                         reads=list(lhs_reads) + ["wo"], writes=[("ps", 6 + hf)])
            for hf in range(2):
                xsl = xs[:, t, hf * 512:(hf + 1) * 512]
                if first:
                    S.op("dve", I("scalar_tensor_tensor", out=xsl, in0=xsl, scalar=ALPHA, in1=pb[6 + hf][:, 0:512],
                                                                                op0=ALU.mult, op1=ALU.add),
                         reads=[("ps", 6 + hf), ("xs", t)], writes=[("xs", t)])
                else:
                    S.op("dve", I("tensor_tensor", out=xsl, in0=xsl, in1=pb[6 + hf][:, 0:512], op=ALU.add),
                         reads=[("ps", 6 + hf), ("xs", t)], writes=[("xs", t)])

        NSL, LEAD = 12, 6
        pf = [A.f32(768) for _ in range(NSL)]
        pbf = [A.bf16(768) for _ in range(NSL)]
        pieces = []
        for l in range(DEPTH):
            for src, dstb, nrows, ncols in ((w_in_d[l], win_b[l], D, IN_COLS), (w_out_d[l], wout_b[l], D, D)):
                for k in range(nrows // 128):
                    for p0 in range(0, ncols, 768):
                        n = min(768, ncols - p0)
                        rs = slice(k * 128, (k + 1) * 128)
                        pieces.append(([(0, n, src[rs, p0:p0 + n])], n, dstb[rs, p0:p0 + n]))
            for jp in range(11):
                for c in range(8):
                    rs = slice(c * 128, (c + 1) * 128)
                    pieces.append(([(0, 256, w_up_d[l][rs, jp * 256:(jp + 1) * 256]),
                                    (256, 256, w_up_d[l][rs, D_FF + jp * 256:D_FF + (jp + 1) * 256])], 512,
                                   wup_b[l, jp, :, c * 512:(c + 1) * 512]))
            for k in range(D_FF // 128):
                for p0 in range(0, D, 768):
                    n = min(768, D - p0)
                    rs = slice(k * 128, (k + 1) * 128)
                    pieces.append(([(0, n, w_dn_d[l][rs, p0:p0 + n])], n, wdn_b[l][rs, p0:p0 + n]))
        cast_eng = ["pool", "act", "dve"]
        npc = len(pieces)
        for i in range(npc + LEAD):
            if i < npc:
                sl = i % NSL
                for c0_, n_, src_ in pieces[i][0]:
                    S.dma("sp", f"pl{sl}", I("dma_start", out=pf[sl][:, c0_:c0_ + n_], in_=src_), writes=[("pf", sl)])
            j = i - LEAD
            if j >= 0:
                sl = j % NSL
                n_ = pieces[j][1]
                ce = cast_eng[j % 3]
                if ce == "act":
                    S.op("act", I("activation", out=pbf[sl][:, 0:n_], in_=pf[sl][:, 0:n_], func=AF.Copy), reads=[("pf", sl)], writes=[("pbf", sl)])
                else:
                    S.op(ce, I("tensor_copy", out=pbf[sl][:, 0:n_], in_=pf[sl][:, 0:n_]), reads=[("pf", sl)], writes=[("pbf", sl)])
                S.dma("sp", f"ps{sl}", I("dma_start", out=pieces[j][2], in_=pbf[sl][:, 0:n_]), reads=[("pbf", sl)])
        S.dma("sp", "pl0", I("dma_start", out=pf[0][64:72, 0:768], in_=ind_d[:, 0:768]), writes=[("pf", 0)])
        S.dma("sp", "pl1", I("dma_start", out=pf[1][64:72, 0:768], in_=ind_d[:, 768:1536]), writes=[("pf", 1)])
        S.dma("sp", "pl2", I("dma_start", out=pf[2][64:72, 0:512], in_=ind_d[:, 1536:2048]), writes=[("pf", 2)])
        for q_, (o_, n_) in enumerate(((0, 768), (768, 768), (1536, 512))):
            S.op("dve", I("tensor_copy", out=pbf[q_][64:72, 0:n_], in_=pf[q_][64:72, 0:n_]), reads=[("pf", q_)], writes=[("pbf", q_)])
            S.dma("sp", f"ps{q_}", I("dma_start", out=ind_b[:, o_:o_ + n_], in_=pbf[q_][64:72, 0:n_]), reads=[("pbf", q_)])
        S.barrier()

        try:
            for s in range(nseq):
                for t in range(NT):
                    S.dma("sp", f"xload{t}", I("dma_start", out=xs[:, t, :], in_=x_d[s, t * 128:(t + 1) * 128, :]),
                          writes=[("xs", t)])
                for t in range(NT):
                    emit_hT(t)
                if stop == "hT":
                    raise _Stop()
                for l in range(depth):
                    w_in = win_b[l]; w_out = wout_b[l]; w_dn = wdn_b[l]
                    S.barrier(); A.reset()
                    wR3 = A.bf16(8 * 1536).rearrange("p (c n) -> p c n", n=1536)
                    woR3 = A.bf16(3 * 1024).rearrange("p (c n) -> p c n", n=1024)
                    qrot = A.bf16(384)
                    krot = [A.bf16(384) for _ in range(2)]
                    vs1 = [A.bf16(384).rearrange("p (h d) -> p h d", d=64) for _ in range(2)]
                    vz = [A.bf16(384).rearrange("p (h d) -> p h d", d=64) for _ in range(2)]
                    qkT = [A.bf16(768) for _ in range(2)]
                    sg = [A.f32(384) for _ in range(2)]
                    scT = A.bf16(768).rearrange("p (par j c) -> p par j c", par=2, c=128)
                    ycat = A.bf16(384); catT = A.bf16(384)
                    stb = [A.bf16(192).rearrange("p (j v) -> p j v", v=64) for _ in range(2)]
                    stf = A.f32(192).rearrange("p (j v) -> p j v", v=64)
                    tq = [A.f32(384).rearrange("p (h d) -> p h d", d=64) for _ in range(4)]
                    eg = A.f32(384)
                    yr = A.f32(384).rearrange("p (h d) -> p h d", d=64)
                    sq = A.f32(384).rearrange("p (h d) -> p h d", d=64)
                    st6 = A.f32(48)
                    s1 = st6[:, 0:6]; s2 = st6[:, 8:14]; mean = st6[:, 16:22]; msq = st6[:, 24:30]; var = st6[:, 32:38]; rstd = st6[:, 40:46]
                    wload("wR", wR3, w_in, 0, 8, 0, 1536, "wR")
                    wload("wo", woR3, w_out, 0, 3, 0, 1024, "wo")
                    S.op("dve", I("memset", stf, 0.0), writes=["stf"] + [("stf", h) for h in range(6)])
                    S.op("dve", I("memset", stb[0], 0.0), writes=[("stb", 0)])

                    def r_proj(t):
                        tsl = slice(t * 128, (t + 1) * 128)
                        for c in range(8):
                            for g4 in range(4):
                                S.op("pe", I("matmul", out=pb[g4][:, 0:384], lhsT=hT[:, c, tsl], rhs=wR3[:, c, g4 * 384:(g4 + 1) * 384],
                                             start=(c == 0), stop=(c == 7)),
                                     reads=[("hT", t), "wR"], writes=[("ps", g4)])

                    def r_front_ew(t):
                        p = t % 2
                        cos2 = bcm(cs[:, 0, t, :], 6)
                        nsin = bcm(cs[:, 1, t, 0:32], 6); psin = bcm(cs[:, 1, t, 32:64], 6)
                        for bank, dst, nm, o in ((0, qrot, "qrot", 0), (1, krot[p], ("krot", p), 2)):
                            pq = pb[bank][:, 0:384].rearrange("p (h d) -> p h d", d=64)
                            ra = tq[o]; rb = tq[o + 1]
                            S.op("dve", I("tensor_tensor", out=ra, in0=pq, in1=cos2, op=ALU.mult), reads=[("ps", bank), "cs"], writes=[("tq", o)])
                            S.op("dve", I("tensor_tensor", out=rb[:, :, 0:32], in0=pq[:, :, 32:64], in1=nsin, op=ALU.mult), reads=[("ps", bank), "cs"], writes=[("tq", o + 1, 0)])
                            S.op("dve", I("tensor_tensor", out=rb[:, :, 32:64], in0=pq[:, :, 0:32], in1=psin, op=ALU.mult), reads=[("ps", bank), "cs"], writes=[("tq", o + 1, 1)])
                            S.op("dve", I("tensor_tensor", out=dst.rearrange("p (h d) -> p h d", d=64), in0=ra, in1=rb, op=ALU.add),
                                 reads=[("tq", o), ("tq", o + 1, 0), ("tq", o + 1, 1)], writes=[(nm, 1), (nm, 2)])
                        pv = pb[2][:, 0:384].rearrange("p (h d) -> p h d", d=64)
                        S.op("dve", I("tensor_tensor", out=vs1[p], in0=pv, in1=bc3(vsc1, 64), op=ALU.mult), reads=[("ps", 2), "small"], writes=[("vs1", p)])
                        S.op("dve", I("tensor_tensor", out=vz[p], in0=pv, in1=bc3(zeta8, 64), op=ALU.mult), reads=[("ps", 2), "small"], writes=[("vz", p)])
                        S.op("act", I("activation", out=eg, in_=pb[3][:, 0:384], func=AF.Exp, scale=-1.0), reads=[("ps", 3)], writes=["eg"])
                        S.op("act", I("activation", out=eg, in_=eg, func=AF.Identity, bias=ones_f[:, 0:1], scale=1.0), reads=["eg", "ones_f"], writes=["eg"])
                        S.op("dve", I("reciprocal", out=eg, in_=eg), reads=["eg"], writes=["eg"])
                        S.op("dve", I("tensor_tensor", out=sg[p], in0=pb[3][:, 0:384], in1=eg, op=ALU.mult), reads=[("ps", 3), "eg"], writes=[("sg", p)])

                    def r_front_tr(t):
                        p = t % 2
                        p4 = pbb(4)
                        for j in range(3):
                            S.op("pe", I("transpose", out=p4[:, j * 128:(j + 1) * 128], in_=qrot[:, j * 128:(j + 1) * 128], identity=ident[:]),
                                 reads=[("qrot", 1), ("qrot", 2), "ident"], writes=[("ps", 4)])
                            S.op("pe", I("transpose", out=p4[:, (3 + j) * 128:(4 + j) * 128], in_=krot[p][:, j * 128:(j + 1) * 128], identity=ident[:]),
                                 reads=[(("krot", p), 1), (("krot", p), 2), "ident"], writes=[("ps", 4)])
                        S.op("act", I("activation", out=qkT[p], in_=p4[:, 0:768], func=AF.Copy), reads=[("ps", 4)], writes=[("qkT", p)])

                    def r_scores(t):
                        p = t % 2
                        for h in range(6):
                            j, hf = h // 2, h % 2
                            pr = slice(hf * 64, hf * 64 + 64)
                            S.op("pe", I("matmul", out=pb[5 + hf][:, j * 128:(j + 1) * 128], lhsT=qkT[p][pr, (3 + j) * 128:(4 + j) * 128],
                                         rhs=qkT[p][pr, j * 128:(j + 1) * 128], start=True, stop=True),
                                 reads=[("qkT", p)], writes=[("ps", 5 + hf)])

                    def r_mask(t):
                        for hf in range(2):
                            S.op("dve", I("tensor_tensor", out=scT[:, hf, :, :], in0=pb[5 + hf][:, 0:384].rearrange("p (h c) -> p h c", c=128),
                                          in1=bcm(mask8, 3), op=ALU.mult),
                                 reads=[("ps", 5 + hf), "small"], writes=[("scT", hf)])

                    def r_o_ds(t):
                        p = t % 2
                        sb_cur = stb[t % 2]
                        for h in range(6):
                            j, hf = h // 2, h % 2
                            pr = slice(hf * 64, hf * 64 + 64)
                            S.op("pe", I("matmul", out=pb[5][:, h * 64:(h + 1) * 64], lhsT=scT[:, hf, j, :], rhs=vs1[p][:, h, :], start=True, stop=False),
                                 reads=[("scT", 0), ("scT", 1), ("vs1", p)], writes=[("ps", 5)])
                            S.op("pe", I("matmul", out=pb[5][:, h * 64:(h + 1) * 64], lhsT=qkT[p][pr, j * 128:(j + 1) * 128],
                                         rhs=sb_cur[pr, j, :], start=False, stop=True),
                                 reads=[("qkT", p), ("stb", t % 2)], writes=[("ps", 5)])
                        for h in range(6):
                            j, hf = h // 2, h % 2
                            S.op("pe", I("matmul", out=pb[7][hf * 64:hf * 64 + 64, j * 64:(j + 1) * 64], lhsT=krot[p][:, h * 64:(h + 1) * 64], rhs=vz[p][:, h, :],
                                         start=True, stop=True),
                                 reads=[(("krot", p), 1), (("krot", p), 2), ("vz", p)], writes=[("ps", 7)])

                    def r_state_norm(t):
                        p = t % 2
                        sb_nxt = stb[(t + 1) % 2]
                        for j in range(3):
                            S.op("dve", I("scalar_tensor_tensor", out=stf[:, j, :], in0=stf[:, j, :], scalar=gct[:, j:j + 1],
                                          in1=pb[7][:, j * 64:(j + 1) * 64], op0=ALU.mult, op1=ALU.add),
                                 reads=[("ps", 7), ("stf", 2 * j), "small"], writes=[("stf", 2 * j), ("stf", 2 * j + 1)])
                        S.op("act", I("activation", out=sb_nxt, in_=stf, func=AF.Copy),
                             reads=[("stf", h) for h in range(6)], writes=[("stb", (t + 1) % 2)])
                        po = pb[5][:, 0:384].rearrange("p (h d) -> p h d", d=64)
                        S.op("dve", I("tensor_tensor", out=yr, in0=po, in1=bc3(xi, 64), op=ALU.mult), reads=[("ps", 5), "small"], writes=["yr"])
                        S.op("dve", I("tensor_reduce", out=s1, in_=yr, axis=AX.X, op=ALU.add), reads=["yr"], writes=["s1"])
                        S.op("dve", I("tensor_tensor", out=sq, in0=yr, in1=yr, op=ALU.mult), reads=["yr"], writes=["sq"])
                        S.op("dve", I("tensor_reduce", out=s2, in_=sq, axis=AX.X, op=ALU.add), reads=["sq"], writes=["s2"])
                        S.op("dve", I("tensor_scalar", out=mean, in0=s1, scalar1=1.0 / 64, scalar2=None, op0=ALU.mult), reads=["s1"], writes=["mean"])
                        S.op("dve", I("tensor_tensor", out=msq, in0=mean, in1=mean, op=ALU.mult), reads=["mean"], writes=["msq"])
                        S.op("dve", I("tensor_scalar", out=var, in0=s2, scalar1=1.0 / 64, scalar2=None, op0=ALU.mult), reads=["s2"], writes=["var"])
                        S.op("dve", I("tensor_tensor", out=var, in0=var, in1=msq, op=ALU.subtract), reads=["var", "msq"], writes=["var"])
                        S.op("dve", I("tensor_scalar", out=var, in0=var, scalar1=0.0, scalar2=None, op0=ALU.max), reads=["var"], writes=["var"])
                        S.op("act", I("activation", out=rstd, in_=var, func=AF.Ln, bias=eps_t, scale=1.0), reads=["var", "small"], writes=["rstd"])
                        S.op("act", I("activation", out=rstd, in_=rstd, func=AF.Exp, scale=-0.5), reads=["rstd"], writes=["rstd"])
                        S.op("dve", I("tensor_tensor", out=yr, in0=yr, in1=bc3(mean, 64), op=ALU.subtract), reads=["yr", "mean", "s1", "sq"], writes=["yr"])
                        S.op("dve", I("tensor_tensor", out=yr, in0=yr, in1=bc3(rstd, 64), op=ALU.mult), reads=["yr", "rstd"], writes=["yr"])
                        S.op("dve", I("tensor_tensor", out=ycat, in0=yr.rearrange("p h d -> p (h d)"), in1=sg[p], op=ALU.mult), reads=["yr", ("sg", p)], writes=["ycat"])

                    def r_out(t):
                        p4 = pbb(4)
                        for j in range(3):
                            S.op("pe", I("transpose", out=p4[:, j * 128:(j + 1) * 128], in_=ycat[:, j * 128:(j + 1) * 128], identity=ident[:]),
                                 reads=["ycat", "ident"], writes=[("ps", 4)])
                        S.op("act", I("activation", out=catT, in_=p4[:, 0:384], func=AF.Copy), reads=[("ps", 4)], writes=["catT"])
                        outproj_partial(t, lambda j: catT[:, j * 128:(j + 1) * 128], 3, woR3, True, ["catT"])

                    for t in range(NT + 1):
                        if t < NT:
                            r_proj(t)
                        if t >= 1:
                            r_scores(t - 1)
                        if t < NT:
                            r_front_ew(t)
                        if t >= 1:
                            r_mask(t - 1)
                        if t < NT:
                            r_front_tr(t)
                        if t >= 1:
                            r_o_ds(t - 1)
                            r_state_norm(t - 1)
                            r_out(t - 1)

                    if stop == "R":
                        raise _Stop()
                    S.barrier(); A.reset()
                    wM3 = A.bf16(8 * 1152).rearrange("p (c n) -> p c n", n=1152)
                    woM3 = A.bf16(3 * 1024).rearrange("p (c n) -> p c n", n=1024)
                    KT = A.bf16(6 * SEQ).rearrange("p (h k) -> p h k", k=SEQ)
                    QT = A.bf16(6 * 512).rearrange("p (h k) -> p h k", k=512)
                    Va = A.bf16(NT * 6 * 65 + 2).rearrange("p (t h d) -> p t h d", h=6, d=65) if False else None
                    Vraw = A.bf16(NT * 6 * 66)
                    Va = Vraw.rearrange("p (t h d) -> p t h d", h=6, d=66)
                    NPT = 4
                    PT = [A.bf16(512) for _ in range(NPT)]
                    OTs = A.f32(512); rec = OTs
                    catM = A.bf16(3 * 512).rearrange("p (j k) -> p j k", k=512)
                    NM = [A.bf16(6 * 72).rearrange("p (h n) -> p h n", n=72) for _ in range(4)]
                    gsb = [A.f32(48).rearrange("p (h n) -> p h n", n=8) for _ in range(4)]
                    m8 = [A.f32(48).rearrange("p (h n) -> p h n", n=8) for _ in range(4)]
                    ge = [A.f32(48).rearrange("p (h n) -> p h n", n=8) for _ in range(4)]
                    km = A.f32(48).rearrange("p (h n) -> p h n", n=8)
                    kmT = A.bf16(48).rearrange("p (h n) -> p h n", n=8)
                    wload("wM", wM3, w_in, 0, 8, 1536, 2688, "wM")
                    wload("wo", woM3, w_out, 384, 3, 0, 1024, "wo")
                    for h in range(6):
                        S.dma("sp", f"ind{h}", I("dma_start", out=KT[64:72, h, :], in_=ind_b), writes=[("KTi", h)])
                    S.op("dve", I("memset", Vraw, 1.0), writes=["Va1"])
                    for i in range(4):
                        S.op("dve", I("memset", NM[i], 0.0), writes=[("NM", i)])
                    S.op("dve", I("memset", kmT, 0.0), writes=["kmT"])
                    S.op("dve", I("memset", km, 0.0), writes=["km"])
                    sidx = [0]; pidx = [0]; oidx = [0]
                    for G in range(4):
                        gsl = slice(G * 512, (G + 1) * 512)
                        hTr = [("hT", 4 * G + i) for i in range(4)]
                        for h in range(6):
                            for which in range(2):
                                bank = which
                                c0 = which * 384 + h * 64
                                for c in range(8):
                                    S.op("pe", I("matmul", out=pb[bank][0:64, 0:512], lhsT=wM3[:, c, c0:c0 + 64], rhs=hT[:, c, gsl],
                                                                                         start=(c == 0), stop=(c == 7)),
                                         reads=hTr + ["wM"], writes=[("ps", bank)])
                                if which == 0:
                                    S.op("act", I("activation", out=QT[0:64, h, :], in_=pb[0][0:64, 0:512], func=AF.Identity, scale=0.125),
                                         reads=[("ps", 0)], writes=[("QT", h)])
                                else:
                                    S.op("dve", I("tensor_copy", out=KT[0:64, h, gsl], in_=pb[1][0:64, 0:512]),
                                         reads=[("ps", 1)], writes=[("KT", h, G)])
                        for i in range(4):
                            tt = 4 * G + i
                            bank = 2 + (i % 2)
                            for c in range(8):
                                S.op("pe", I("matmul", out=pb[bank][:, 0:384], lhsT=hT[:, c, tt * 128:(tt + 1) * 128], rhs=wM3[:, c, 768:1152],
                                                                                     start=(c == 0), stop=(c == 7)),
                                     reads=[("hT", tt), "wM"], writes=[("ps", bank)])
                            S.op("act", I("activation", out=Va[:, tt, :, 0:64], in_=pb[bank][:, 0:384].rearrange("p (h d) -> p h d", d=64), func=AF.Copy),
                                 reads=[("ps", bank), "Va1"], writes=[("Va", tt)])
                        for h in range(6):
                            S.op("dve", I("tensor_reduce", out=km[0:64, h, 2 * G:2 * G + 2], in_=KT[0:64, h, gsl].rearrange("p (n k) -> p n k", k=256),
                                                                      axis=AX.X, op=ALU.add),
                                 reads=[("KT", h, G), "km"], writes=[("km", h)])
                        S.op("dve", I("tensor_scalar", out=kmT[0:64, :, 2 * G:2 * G + 2], in0=km[0:64, :, 2 * G:2 * G + 2], scalar1=1.0 / 256, scalar2=None, op0=ALU.mult),
                             reads=[("km", h) for h in range(6)] + ["kmT"], writes=["kmT"])
                        for i in range(4):
                            for h in range(6):
                                S.op("pe", I("matmul", out=pb[5][:, i * 48 + h * 8:i * 48 + (h + 1) * 8], lhsT=QT[0:64, h, i * 128:(i + 1) * 128], rhs=kmT[0:64, h, :],
                                             start=True, stop=True),
                                     reads=[("QT", h), "kmT"], writes=[("ps", 5)])
                        for i in range(4):
                            qt = 4 * G + i
                            nb = NM[i]
                            S.op("dve", I("tensor_tensor", out=gsb[i], in0=pb[5][:, i * 48:(i + 1) * 48].rearrange("p (h n) -> p h n", n=8), in1=bcm(ownmask[:, qt, :], 6), op=ALU.add),
                                 reads=[("ps", 5), "small"], writes=[("gsb", i)])
                            for h in range(6):
                                S.op("dve", I("max", out=m8[i][:, h, :], in_=gsb[i][:, h, :]), reads=[("gsb", i)], writes=[("m8", i, h)])
                            S.op("dve", I("tensor_tensor", out=ge[i], in0=gsb[i], in1=m8[i][:, :, 3:4].to_broadcast([128, 6, 8]), op=ALU.is_ge),
                                 reads=[("gsb", i)] + [("m8", i, h) for h in range(6)], writes=[("ge", i)])
                            S.op("dve", I("tensor_scalar", out=nb[:, :, 64:72], in0=ge[i], scalar1=-1.0, scalar2=-NEG, op0=ALU.add, op1=ALU.mult),
                                 reads=[("ge", i), ("NM", i)], writes=[("NM", i)])
                        for i in range(4):
                            nb = NM[i]
                            p4 = pbb(4)
                            for h in range(6):
                                S.op("pe", I("transpose", out=p4[0:72, h * 128:(h + 1) * 128], in_=nb[:, h, :], identity=ident[:]),
                                     reads=[("NM", i), "ident"], writes=[("ps", 4)])
                            S.op("act", I("activation", out=QT[64:72, :, i * 128:(i + 1) * 128], in_=p4[64:72, 0:768].rearrange("p (h k) -> p h k", k=128), func=AF.Copy),
                                 reads=[("ps", 4)] + [("QT", h) for h in range(6)], writes=[("QTn", i)])
                        nkt = 4 * G + 4
                        seq = [(h, kt) for h in range(6) for kt in range(nkt)]
                        LA = 3
                        slot = {}
                        obs = {}
                        for h in range(6):
                            obs[h] = 6 + (oidx[0] % 2); oidx[0] += 1
                        for idx in range(len(seq) + LA):
                            if idx < len(seq):
                                h, kt = seq[idx]
                                rel = kt - 4 * G
                                c0 = max(0, rel) * 128
                                sbk = sidx[0] % 4; sidx[0] += 1
                                ptb = pidx[0] % NPT; pidx[0] += 1
                                slot[idx] = ptb
                                Pt = PT[ptb]
                                S.op("pe", I("matmul", out=pb[sbk][:, c0:512], lhsT=KT[0:72, h, kt * 128:(kt + 1) * 128], rhs=QT[0:72, h, c0:512],
                                             start=True, stop=True),
                                     reads=[("KT", h, kt // 4), ("KTi", h), ("QT", h)] + [("QTn", i) for i in range(4)], writes=[("ps", sbk)])
                                if -1 <= rel <= 3:
                                    if rel == -1:
                                        a0, a1, bsl = 0, 128, bt[:, h, 1, :]
                                    elif rel == 3:
                                        a0, a1, bsl = 384, 512, bt[:, h, 0, :]
                                    else:
                                        a0, a1, bsl = rel * 128, rel * 128 + 256, bt[:, h, :, :].rearrange("p a q -> p (a q)")
                                    S.op("dve", I("tensor_tensor", out=pb[sbk][:, a0:a1], in0=pb[sbk][:, a0:a1], in1=bsl, op=ALU.add),
                                         reads=[("ps", sbk), "bt"], writes=[("ps", sbk)])
                                S.op("act", I("activation", out=Pt[:, c0:512], in_=pb[sbk][:, c0:512], func=AF.Exp, bias=cfar[:, h:h + 1], scale=1.0),
                                     reads=[("ps", sbk), "cfar", ("PT", ptb)], writes=[("PTf", ptb)])
                            jdx = idx - LA
                            if jdx >= 0:
                                h, kt = seq[jdx]
                                j, hf = h // 2, h % 2
                                ob = obs[h]
                                rel = kt - 4 * G
                                c0 = max(0, rel) * 128
                                ptb = slot[jdx]
                                Pt = PT[ptb]
                                S.op("pe", I("matmul", out=pb[ob][0:65, c0:512], lhsT=Va[:, kt, h, 0:65], rhs=Pt[:, c0:512],
                                             start=(kt == 0), stop=(kt == nkt - 1)),
                                     reads=[("Va", kt), "Va1", ("PTf", ptb)], writes=[("ps", ob), ("PT", ptb)])
                                if kt == nkt - 1:
                                    S.op("dve", I("reciprocal", out=rec[64:65, :], in_=pb[ob][64:65, 0:512]), reads=[("ps", ob)], writes=["rec"])
                                    S.op("pe", I("matmul", out=pb[5][0:64, 0:512], lhsT=ones_f[64:65, 0:64], rhs=rec[64:65, :], start=True, stop=True),
                                         reads=["rec", "ones_f"], writes=[("ps", 5)])
                                    S.op("act", I("activation", out=OTs[0:64, :], in_=pb[ob][0:64, 0:512], func=AF.Copy), reads=[("ps", ob)], writes=["OTs"])
                                    S.op("dve", I("tensor_tensor", out=catM[hf * 64:hf * 64 + 64, j, :], in0=OTs[0:64, :], in1=pb[5][0:64, 0:512], op=ALU.mult),
                                         reads=["OTs", ("ps", 5)], writes=[("catM", h)])
                        for i in range(4):
                            tt = 4 * G + i
                            outproj_partial(tt, lambda j, i=i: catM[:, j, i * 128:(i + 1) * 128], 3, woM3, False, [("catM", h) for h in range(6)])

                    if stop == "M":
                        raise _Stop()
                    S.barrier(); A.reset()
                    wC3 = A.bf16(8 * 768).rearrange("p (c n) -> p c n", n=768)
                    woC3 = A.bf16(2 * 1024).rearrange("p (c n) -> p c n", n=1024)
                    catC = A.bf16(2 * SEQ).rearrange("p (j k) -> p j k", k=SEQ)
                    pbuf = [A.f32(514) for _ in range(2)]
                    ccs = A.f32(512); acc = A.f32(512)
                    wload("wC", wC3, w_in, 0, 8, 2688, 3456, "wC")
                    wload("wo", woC3, w_out, 768, 2, 0, 1024, "wo")
                    S.dma("sp", "cw", I("dma_start", out=cw[:], in_=cw_d[l]), writes=["cw"])
                    load_lnp(l, 0)
                    for j in range(2):
                        S.op("dve", I("memset", pbuf[j][:, 0:2], 0.0), writes=[("pbuf", j)])
                    for G in range(4):
                        gsl = slice(G * 512, (G + 1) * 512)
                        hTr = [("hT", 4 * G + i) for i in range(4)]
                        for j in range(2):
                            for part in range(3):
                                c0 = part * 256 + j * 128
                                for c in range(8):
                                    S.op("pe", I("matmul", out=pb[part][:, 0:512], lhsT=wC3[:, c, c0:c0 + 128], rhs=hT[:, c, gsl],
                                                                                         start=(c == 0), stop=(c == 7)),
                                         reads=hTr + ["wC"], writes=[("ps", part)])
                            pbj = pbuf[j]
                            S.op("act", I("activation", out=ccs, in_=pb[1][:, 0:512], func=AF.Copy), reads=[("ps", 1)], writes=["ccs"])
                            S.op("dve", I("tensor_tensor", out=pbj[:, 2:514], in0=pb[2][:, 0:512], in1=ccs, op=ALU.mult),
                                 reads=[("ps", 2), "ccs", ("pbuf", j)], writes=[("pbufm", j)])
                            S.op("dve", I("tensor_scalar", out=acc, in0=pbj[:, 2:514], scalar1=cw[:, j * 3 + 2:j * 3 + 3], scalar2=None, op0=ALU.mult),
                                 reads=[("pbufm", j), "cw"], writes=["acc"])
                            S.op("dve", I("scalar_tensor_tensor", out=acc, in0=pbj[:, 1:513], scalar=cw[:, j * 3 + 1:j * 3 + 2], in1=acc, op0=ALU.mult, op1=ALU.add),
                                 reads=[("pbufm", j), ("pbuf", j), "acc", "cw"], writes=["acc"])
                            S.op("dve", I("scalar_tensor_tensor", out=acc, in0=pbj[:, 0:512], scalar=cw[:, j * 3:j * 3 + 1], in1=acc, op0=ALU.mult, op1=ALU.add),
                                 reads=[("pbufm", j), ("pbuf", j), "acc", "cw"], writes=["acc"])
                            S.op("dve", I("tensor_tensor", out=catC[:, j, gsl], in0=pb[0][:, 0:512], in1=acc, op=ALU.mult),
                                 reads=[("ps", 0), "acc"], writes=[("catC", j, G)])
                            S.op("dve", I("tensor_copy", out=pbj[:, 0:2], in_=pbj[:, 512:514]),
                                 reads=[("pbufm", j)], writes=[("pbuf", j)])
                        for i in range(4):
                            tt = 4 * G + i
                            outproj_partial(tt, lambda j, tt=tt: catC[:, j, tt * 128:(tt + 1) * 128], 2, woC3, False, [("catC", 0, G), ("catC", 1, G)])
                        emit_ln_batch([4 * G + i for i in range(4)])
                        for i in range(4):
                            emit_hT(4 * G + i)

                    if stop == "C":
                        raise _Stop()
                    S.barrier(); A.reset()
                    Gp = A.bf16(8 * SEQ).rearrange("p (j k) -> p j k", k=SEQ)
                    wD3 = A.bf16(8 * 1024).rearrange("p (j n) -> p j n", n=1024)
                    wUflat = [A.bf16(8 * 512) for _ in range(2)]
                    wU = [w_.rearrange("p (c n) -> p c n", n=512) for w_ in wUflat]
                    abuf = [A.f32(514) for _ in range(2)]
                    accs = [A.f32(512) for _ in range(2)]
                    gls = [A.f32(512) for _ in range(2)]
                    S.dma("sp", "fcw", I("dma_start", out=fcw[:], in_=fcw_d[l]), writes=["fcw"])
                    load_lnp(l, 2)
                    parts = [(0, 8), (8, 16), (16, 22)]
                    upc = [0]; itc = [0]; bai = [0]; bbi = [0]; dpi = [0]
                    for pi, (j0, j1) in enumerate(parts):
                        nj = j1 - j0
                        wload("wD", wD3, w_dn, j0 * 128, nj, 0, 1024, "wD")
                        its = []
                        for jp in range(j0 // 2, j1 // 2):
                            wb = upc[0] % 2; upc[0] += 1
                            for sub in range(2):
                                for G in range(4):
                                    its.append((jp, wb, sub, G))

                        def f_stage1(i):
                            jp, wb, sub, G = its[i]
                            wu = wU[wb]
                            if sub == 0 and G == 0:
                                nxt = [(jp2, wb2) for (jp2, wb2, s2, g2) in its if jp2 == jp + 1 and s2 == 0 and g2 == 0]
                                if nxt:
                                    S.dma("sp", f"wU{nxt[0][1]}", I("dma_start", out=wUflat[nxt[0][1]], in_=wup_b[l, nxt[0][0]]), writes=[("wU", nxt[0][1])])
                            jg = 2 * jp + sub
                            st = itc[0] % 2; itc[0] += 1
                            fset[i] = st
                            ab = abuf[st]; ac = accs[st]
                            gsl = slice(G * 512, (G + 1) * 512)
                            hTr = [("hT", 4 * G + q) for q in range(4)]
                            ba = bai[0] % 2; bai[0] += 1
                            bb = (2, 3, 5)[bbi[0] % 3]; bbi[0] += 1
                            fbb[i] = bb
                            for c in range(8):
                                S.op("pe", I("matmul", out=pb[ba][:, 0:512], lhsT=wu[:, c, sub * 128:(sub + 1) * 128], rhs=hT[:, c, gsl],
                                             start=(c == 0), stop=(c == 7)),
                                     reads=hTr + [("wU", wb)], writes=[("ps", ba)])
                            for c in range(8):
                                S.op("pe", I("matmul", out=pb[bb][:, 0:512], lhsT=wu[:, c, 256 + sub * 128:256 + (sub + 1) * 128], rhs=hT[:, c, gsl],
                                             start=(c == 0), stop=(c == 7)),
                                     reads=hTr + [("wU", wb)], writes=[("ps", bb)])
                            if G == 0:
                                S.op("pool", I("memset", ab[:, 0:2], 0.0), writes=[("abh", st)])
                            else:
                                S.op("pool", I("tensor_copy", out=ab[:, 0:2], in_=abuf[1 - st][:, 512:514]), reads=[("abm", 1 - st)], writes=[("abh", st)])
                            S.op("act", I("activation", out=ab[:, 2:514], in_=pb[ba][:, 0:512], func=AF.Copy),
                                 reads=[("ps", ba)], writes=[("abm", st)])
                            k3 = jg * 3
                            S.op("act", I("activation", out=ac, in_=pb[ba][:, 0:512], func=AF.Identity, scale=fcw[:, k3 + 2:k3 + 3]),
                                 reads=[("ps", ba), "fcw"], writes=[("acc", st)])
                            S.op("dve", I("scalar_tensor_tensor", out=ac, in0=ab[:, 1:513], scalar=fcw[:, k3 + 1:k3 + 2], in1=ac, op0=ALU.mult, op1=ALU.add),
                                 reads=[("abm", st), ("abh", st), ("acc", st), "fcw"], writes=[("acc", st)])
                            S.op("dve", I("scalar_tensor_tensor", out=ac, in0=ab[:, 0:512], scalar=fcw[:, k3:k3 + 1], in1=ac, op0=ALU.mult, op1=ALU.add),
                                 reads=[("abm", st), ("abh", st), ("acc", st), "fcw"], writes=[("acc", st)])

                        def f_stage2(i):
                            jp, wb, sub, G = its[i]
                            st = fset[i]; bb = fbb[i]
                            jj = 2 * jp + sub - j0
                            gsl = slice(G * 512, (G + 1) * 512)
                            S.op("act", I("activation", out=gls[st], in_=accs[st], func=AF.Gelu), reads=[("acc", st)], writes=[("gl", st)])
                            S.op("dve", I("tensor_tensor", out=Gp[:, jj, gsl], in0=pb[bb][:, 0:512], in1=gls[st], op=ALU.mult),
                                 reads=[("ps", bb), ("gl", st)], writes=[("Gp", jj, G)])

                        fset = {}; fbb = {}
                        S.dma("sp", f"wU{its[0][1]}", I("dma_start", out=wUflat[its[0][1]], in_=wup_b[l, its[0][0]]), writes=[("wU", its[0][1])])
                        for i in range(len(its) + 1):
                            if i < len(its):
                                f_stage1(i)
                            if i >= 1:
                                f_stage2(i - 1)
                        last = (pi == len(parts) - 1)
                        for t in range(NT):
                            G = t // 4
                            outproj_partial_reads = [("Gp", jj, G) for jj in range(nj)]
                            dbk = ((6, 7), (0, 1), (2, 3))[dpi[0] % 3]; dpi[0] += 1
                            for hf in range(2):
                                for jj in range(nj):
                                    S.op("pe", I("matmul", out=pb[dbk[hf]][:, 0:512], lhsT=Gp[:, jj, t * 128:(t + 1) * 128], rhs=wD3[:, jj, hf * 512:(hf + 1) * 512],
                                                 start=(jj == 0), stop=(jj == nj - 1)),
                                         reads=outproj_partial_reads + ["wD"], writes=[("ps", dbk[hf])])
                            for hf in range(2):
                                xsl = xs[:, t, hf * 512:(hf + 1) * 512]
                                if pi == 0:
                                    S.op("dve", I("scalar_tensor_tensor", out=xsl, in0=xsl, scalar=ALPHA, in1=pb[dbk[hf]][:, 0:512], op0=ALU.mult, op1=ALU.add),
                                         reads=[("ps", dbk[hf]), ("xs", t)], writes=[("xs", t)])
                                else:
                                    S.op("dve", I("tensor_tensor", out=xsl, in0=xsl, in1=pb[dbk[hf]][:, 0:512], op=ALU.add),
                                         reads=[("ps", dbk[hf]), ("xs", t)], writes=[("xs", t)])
                            if last and t % 4 == 3:
                                tl = [t - 3, t - 2, t - 1, t]
                                emit_ln_batch(tl)
                                for t2 in tl:
                                    if l == depth - 1:
                                        S.dma("sp", "ostore", I("dma_start", out=out_d[s, t2 * 128:(t2 + 1) * 128, :], in_=xs[:, t2, :]),
                                              reads=[("xs", t2)])
                                    else:
                                        emit_hT(t2)
                S.barrier()

        except _Stop:
            S.barrier()
            for t in range(NT):
                S.dma("sp", "ostore", I("dma_start", out=out_d[0, t * 128:(t + 1) * 128, :], in_=xs[:, t, :]), reads=[("xs", t)])
        S.barrier()
        S.emit()
    return nc


_NC_CACHE = {}


def _get_nc(nseq, depth):
    key = (nseq, depth)
    if key not in _NC_CACHE:
        _NC_CACHE[key] = build(nseq, depth)
    return _NC_CACHE[key]


def host_inputs(x_shard, w_in, conv_w, w_out, ln1_g, ln1_b, w_up, ffn_conv_w, w_down, ln2_g, ln2_b, rel_bias):
    cs_h, small_h, gC, ind_h, bidx_d, bidx_s, causal = _host_consts()
    f = lambda a: np.ascontiguousarray(a, dtype=np.float32)
    lnp = np.broadcast_to(np.stack([ln1_g, ln1_b, ln2_g, ln2_b], axis=1)[:, :, None, :], (DEPTH, 4, 128, D))
    cw = conv_w.reshape(DEPTH, 3, 2, 128).transpose(0, 3, 2, 1).reshape(DEPTH, 128, 6)
    fcw = ffn_conv_w.reshape(DEPTH, 3, 22, 128).transpose(0, 3, 2, 1).reshape(DEPTH, 128, 66)
    bd = np.where(causal[:, :, None], rel_bias[bidx_d], np.float32(NEG))
    bs = rel_bias[bidx_s]
    btile = np.stack([bd, bs], axis=0).transpose(1, 3, 0, 2).reshape(128, 6 * 2 * 128)
    cfar = np.broadcast_to(rel_bias[31][None, :], (128, 6))
    return {
        "x": f(x_shard), "w_in": f(w_in), "w_out": f(w_out), "w_up": f(w_up), "w_down": f(w_down),
        "lnp": f(lnp), "cw": f(cw), "fcw": f(fcw), "cs": f(cs_h.reshape(128, -1)), "btile": f(btile),
        "cfar": f(cfar), "small": f(small_h), "ind": f(ind_h),
    }


def kernel(x, w_in, conv_w, w_out, ln1_g, ln1_b, w_up, ffn_conv_w, w_down, ln2_g, ln2_b, rel_bias):
    args = [np.asarray(a) for a in (w_in, conv_w, w_out, ln1_g, ln1_b, w_up, ffn_conv_w, w_down, ln2_g, ln2_b, rel_bias)]
    x = np.asarray(x)
    B = x.shape[0]
    per = B // N_CORES
    nc = _get_nc(per, DEPTH)
    in_maps = []
    base = None
    for c in range(N_CORES):
        m = host_inputs(x[c * per:(c + 1) * per], *args) if base is None else dict(base, x=np.ascontiguousarray(x[c * per:(c + 1) * per], dtype=np.float32))
        if base is None:
            base = m
        in_maps.append(m)
    res = run_bass_kernel_spmd(nc, in_maps, core_ids=list(range(N_CORES)))
    return np.concatenate([np.asarray(r["out"]) for r in res.results], axis=0).astype(np.float32)
```

```python
import contextlib
import math
import numpy as np
import concourse.bass as bass
import concourse.mybir as mybir
from concourse.bass_utils import run_bass_kernel_spmd

F32 = mybir.dt.float32
BF16 = mybir.dt.bfloat16
AF = mybir.ActivationFunctionType
ALU = mybir.AluOpType
AX = mybir.AxisListType

N_CORES = 8
SEQ = 2048
D = 1024
NT = SEQ // 128
DEPTH = 2
IN_COLS = 3456
D_FF = 2816
ALPHA = (2.0 * DEPTH) ** 0.25
LN_EPS = 1e-5
NEG = -30000.0
BIG = 3.0e38

ENGS = ["pe", "act", "dve", "pool", "sp"]
SEM_CH = 12000


def I(name, *a, **k):
    return (name, a, k)


class Sched:
    def __init__(self, nc):
        self.nc = nc
        self.ops = {e: [] for e in ENGS}
        self.count = {e: 0 for e in ENGS}
        self.seen = {e: {} for e in ENGS}
        self.res = {}
        self.dma_cnt = {}

    def _deps(self, eng, reads, writes, skip_dma=None):
        deps = []
        for r in reads:
            st = self.res.get(r)
            if st and st[0] is not None:
                deps.append((st[0], True))
            if st and isinstance(r, tuple) and r[0] == "ps":
                for t in st[1]:
                    if t[1] != eng:
                        deps.append((t, True))
        for w in writes:
            st = self.res.get(w)
            if st:
                if st[0] is not None:
                    deps.append((st[0], False))
                for t in st[1]:
                    deps.append((t, False))
        seen = self.seen[eng]
        best = {}
        for tok, raw in deps:
            kind, key, val = tok
            if kind == "eng" and key == eng and not raw and eng == "pe":
                continue
            if kind == "dma" and key == skip_dma:
                continue
            k = (kind, key)
            if seen.get(k, 0) >= val:
                continue
            best[k] = max(best.get(k, 0), val)
        for k, v in best.items():
            seen[k] = v
        return [(k[0], k[1], v) for k, v in best.items()]

    def _commit(self, tok, reads, writes):
        for r in reads:
            st = self.res.setdefault(r, [None, []])
            st[1].append(tok)
        for w in writes:
            self.res[w] = [tok, []]

    def op(self, eng, fn, reads=(), writes=()):
        waits = self._deps(eng, reads, writes)
        self.count[eng] += 1
        tok = ("eng", eng, self.count[eng])
        self.ops[eng].append((waits, fn, tok))
        self._commit(tok, reads, writes)
        return tok

    def dma(self, q, sem, fn, reads=(), writes=()):
        waits = self._deps(q, reads, writes, skip_dma=sem)
        self.dma_cnt[sem] = self.dma_cnt.get(sem, 0) + 16
        tok = ("dma", sem, self.dma_cnt[sem])
        self.ops[q].append((waits, fn, tok))
        self._commit(tok, reads, writes)
        return tok

    def barrier(self):
        for e in ENGS:
            waits = []
            seen = self.seen[e]
            for e2 in ENGS:
                if e2 != e and self.count[e2] > seen.get(("eng", e2), 0):
                    waits.append(("eng", e2, self.count[e2]))
                    seen[("eng", e2)] = self.count[e2]
            for s, c in self.dma_cnt.items():
                if c > seen.get(("dma", s), 0):
                    waits.append(("dma", s, c))
                    seen[("dma", s)] = c
            if waits:
                self.ops[e].append((waits, None, None))
        self.res = {}

    def emit(self):
        nc = self.nc
        needed = {e: set() for e in ENGS}
        for e in ENGS:
            for waits, fn, tok in self.ops[e]:
                for kind, key, val in waits:
                    if kind == "eng":
                        needed[key].add(val)
        rank = {}
        for e in ENGS:
            rank[e] = {s: i + 1 for i, s in enumerate(sorted(needed[e]))}
        with contextlib.ExitStack() as es:
            esem = {}
            for e in ENGS:
                n = (len(rank[e]) + SEM_CH - 1) // SEM_CH
                esem[e] = [es.enter_context(nc.semaphore(f"s_{e}_{i}")) for i in range(max(n, 1))]
            dsem = {name: es.enter_context(nc.semaphore(f"d_{name}")) for name in self.dma_cnt}

            def lower(tok):
                kind, key, val = tok
                if kind == "eng":
                    r = rank[key][val]
                    return esem[key][(r - 1) // SEM_CH], (r - 1) % SEM_CH + 1
                return dsem[key], val

            def run(engobj, name):
                for waits, fn, tok in self.ops[name]:
                    for w in waits:
                        s, v = lower(w)
                        engobj.wait_ge(s, v)
                    if fn is None:
                        continue
                    ins = getattr(engobj, fn[0])(*fn[1], **fn[2])
                    if tok[0] == "eng":
                        if tok[2] in rank[name]:
                            s, v = lower(tok)
                            ins.then_inc(s, 1)
                    else:
                        s, v = lower(tok)
                        ins.then_inc(s, 16)

            with nc.Block() as block:
                @block.tensor
                def _(e):
                    run(e, "pe")

                @block.scalar
                def _(e):
                    run(e, "act")

                @block.vector
                def _(e):
                    run(e, "dve")

                @block.gpsimd
                def _(e):
                    run(e, "pool")

                @block.sync
                def _(e):
                    run(e, "sp")


class Arena:
    def __init__(self, t, words):
        self.t = t
        self.words = words
        self.off = 0

    def reset(self):
        self.off = 0

    def f32(self, n):
        assert self.off + n <= self.words, ("arena overflow", self.off + n, self.words)
        v = self.t[:, self.off:self.off + n]
        self.off += n
        return v

    def bf16(self, n):
        w = (n + 1) // 2
        assert self.off + w <= self.words, ("arena overflow", self.off + w, self.words)
        v = self.t[:, self.off:self.off + w].bitcast(BF16)
        self.off += w
        return v


SM_OWN = 0
SM_MASK = 128
SM_VSC = 256
SM_XI = 262
SM_ZETA = 268
SM_EPS = 274
SM_GC = 275
SM_W = 280


def _t5_bucket(dist):
    n = np.maximum(dist, 0)
    nf = np.maximum(n, 1).astype(np.float32)
    large = 16 + (np.log(nf / np.float32(16)) / np.float32(math.log(128 / 16)) * np.float32(16)).astype(np.int32)
    large = np.minimum(large, 31)
    return np.where(n < 16, n, large)


def _host_consts():
    p = np.arange(128)
    inv = (10000.0 ** (-np.arange(0, 64, 2, dtype=np.float32) / np.float32(64))).astype(np.float32)
    pos = (np.arange(NT)[None, :] * 128 + p[:, None]).astype(np.float32)
    ang = pos[:, :, None] * inv[None, None, :]
    cs = np.stack([np.concatenate([np.cos(ang), np.cos(ang)], axis=-1), np.concatenate([-np.sin(ang), np.sin(ang)], axis=-1)], axis=1).astype(np.float32)
    small = np.zeros((128, SM_W), np.float32)
    own = np.zeros((NT, 8), np.float32)
    for qt in range(NT):
        o = qt // 2
        own[qt, o] = BIG
        own[qt, o + 1:] = -BIG
    small[:, SM_OWN:SM_OWN + 128] = own.reshape(1, 128)
    e = p[:, None]; c = p[None, :]
    small[:, SM_MASK:SM_MASK + 128] = np.where(c >= e, 0.125, 0.0)
    g = 1.0 - 2.0 ** (-5.0 - np.arange(6))
    small[:, SM_VSC:SM_VSC + 6] = g[None, :] ** (-(p[:, None] + 1.0))
    small[:, SM_XI:SM_XI + 6] = g[None, :] ** (p[:, None] + 1.0)
    small[:, SM_ZETA:SM_ZETA + 6] = 0.125 * g[None, :] ** (127.0 - p[:, None])
    small[:, SM_EPS] = LN_EPS
    for j in range(3):
        small[0:64, SM_GC + j] = g[2 * j] ** 128.0
        small[64:128, SM_GC + j] = g[2 * j + 1] ** 128.0
    gC = [float(x) for x in g ** 128.0]
    ind = (np.arange(SEQ)[None, :] // 256 == np.arange(8)[:, None]).astype(np.float32)
    bidx_d = _t5_bucket(c - e)
    bidx_s = _t5_bucket(128 + c - e)
    causal = (c >= e)
    return cs, small, gC, ind, bidx_d, bidx_s, causal


class _Stop(Exception):
    pass


def build(nseq=4, depth=DEPTH, stop=None):
    cs_h, small_h, gC, ind_h, _, _, _ = _host_consts()
    nc = bass.Bass("TRN2", target_bir_lowering=False)
    x_d = nc.dram_tensor("x", [nseq, SEQ, D], F32, kind="ExternalInput").ap()
    w_in_d = nc.dram_tensor("w_in", [DEPTH, D, IN_COLS], F32, kind="ExternalInput").ap()
    w_out_d = nc.dram_tensor("w_out", [DEPTH, D, D], F32, kind="ExternalInput").ap()
    w_up_d = nc.dram_tensor("w_up", [DEPTH, D, 2 * D_FF], F32, kind="ExternalInput").ap()
    w_dn_d = nc.dram_tensor("w_down", [DEPTH, D_FF, D], F32, kind="ExternalInput").ap()
    lnp_d = nc.dram_tensor("lnp", [DEPTH, 4, 128, D], F32, kind="ExternalInput").ap()
    cw_d = nc.dram_tensor("cw", [DEPTH, 128, 6], F32, kind="ExternalInput").ap()
    fcw_d = nc.dram_tensor("fcw", [DEPTH, 128, 66], F32, kind="ExternalInput").ap()
    cs_d = nc.dram_tensor("cs", [128, 2 * NT * 64], F32, kind="ExternalInput").ap()
    bt_d = nc.dram_tensor("btile", [128, 6 * 2 * 128], F32, kind="ExternalInput").ap()
    cfar_d = nc.dram_tensor("cfar", [128, 6], F32, kind="ExternalInput").ap()
    small_d = nc.dram_tensor("small", [128, SM_W], F32, kind="ExternalInput").ap()
    ind_d = nc.dram_tensor("ind", [8, SEQ], F32, kind="ExternalInput").ap()
    out_d = nc.dram_tensor("out", [nseq, SEQ, D], F32, kind="ExternalOutput").ap()
    win_b = nc.dram_tensor("win_b", [DEPTH, D, IN_COLS], BF16).ap()
    wout_b = nc.dram_tensor("wout_b", [DEPTH, D, D], BF16).ap()
    wup_b = nc.dram_tensor("wup_b", [DEPTH, 11, 128, 8 * 512], BF16).ap()
    wdn_b = nc.dram_tensor("wdn_b", [DEPTH, D_FF, D], BF16).ap()
    ind_b = nc.dram_tensor("ind_b", [8, SEQ], BF16).ap()

    S = Sched(nc)

    def ck(tag):
        if stop == tag:
            raise _Stop()
    ARENA_W = 20992
    with contextlib.ExitStack() as es:
        def sb(name, shape, dt=F32):
            return es.enter_context(nc.sbuf_tensor("sb_" + name, shape, dt))

        xs = sb("xs", [128, NT, D])
        hT = sb("hT", [128, 8, SEQ], BF16)
        cs = sb("cs", [128, 2, NT, 64])
        bt = sb("bt", [128, 6, 2, 128])
        cfar = sb("cfar", [128, 6])
        small = sb("small", [128, SM_W])
        lnp = sb("lnp", [128, 2, D])
        cw = sb("cw", [128, 6])
        fcw = sb("fcw", [128, 66])
        ident = sb("ident", [128, 128], BF16)
        ones_f = sb("ones_f", [128, 64])
        xb = [sb("xb0", [128, D], BF16), sb("xb1", [128, D], BF16)]
        lnst = sb("lnst", [128, 4, 16])
        arena_t = sb("arena", [128, ARENA_W])
        A = Arena(arena_t, ARENA_W)
        pb = [es.enter_context(nc.psum_tensor(f"pb{i}", [128, 512], F32)) for i in range(8)]

        def pbb(i):
            return pb[i][:, :].bitcast(BF16)

        S.dma("sp", "c_cs", I("dma_start", out=cs[:].rearrange("p a t j -> p (a t j)"), in_=cs_d), writes=["cs"])
        S.dma("sp", "c_bt", I("dma_start", out=bt[:].rearrange("p h a q -> p (h a q)"), in_=bt_d), writes=["bt"])
        S.dma("sp", "c_cf", I("dma_start", out=cfar[:], in_=cfar_d), writes=["cfar"])
        S.dma("sp", "c_sm", I("dma_start", out=small[:], in_=small_d), writes=["small"])
        identf = A.f32(128)
        S.op("pool", I("memset", identf[:], 0.0), writes=["identf"])
        S.op("pool", I("affine_select", out=identf[:], in_=identf[:], pattern=[[-1, 128]],
                                              compare_op=ALU.not_equal, fill=1.0, base=0, channel_multiplier=1),
             reads=["identf"], writes=["identf"])
        S.op("dve", I("tensor_copy", out=ident[:], in_=identf[:]), reads=["identf"], writes=["ident"])
        S.op("pool", I("memset", ones_f[:], 1.0), writes=["ones_f"])
        for h in range(6):
            S.op("dve", I("tensor_scalar", out=bt[:, h, :, :], in0=bt[:, h, :, :], scalar1=cfar[:, h:h + 1], scalar2=None, op0=ALU.subtract),
                 reads=["bt", "cfar"], writes=["bt"])
        S.barrier()

        ownmask = small[:, SM_OWN:SM_OWN + 128].rearrange("p (t n) -> p t n", n=8)
        mask8 = small[:, SM_MASK:SM_MASK + 128]
        vsc1 = small[:, SM_VSC:SM_VSC + 6]
        xi = small[:, SM_XI:SM_XI + 6]
        zeta8 = small[:, SM_ZETA:SM_ZETA + 6]
        eps_t = small[:, SM_EPS:SM_EPS + 1]
        gct = small[:, SM_GC:SM_GC + 3]

        def bc3(ap2, n):
            a = ap2.shape[1]
            return ap2.unsqueeze(2).to_broadcast([128, a, n])

        def bcm(ap2, m):
            n = ap2.shape[1]
            return ap2.unsqueeze(1).to_broadcast([128, m, n])

        def wload(sem, dst3, scr2, r0, nrow_chunks, c0, c1, res):
            for k in range(nrow_chunks):
                S.dma("sp", sem, I("dma_start", out=dst3[:, k, :], in_=scr2[r0 + k * 128:r0 + (k + 1) * 128, c0:c1]), writes=[res])

        def emit_hT(t):
            b = t % 2
            S.op("act", I("activation", out=xb[b][:], in_=xs[:, t, :], func=AF.Copy),
                 reads=[("xs", t)], writes=[("xb", b)])
            p4 = pbb(4)
            for c in range(8):
                S.op("pe", I("transpose", out=p4[:, c * 128:(c + 1) * 128], in_=xb[b][:, c * 128:(c + 1) * 128],
                                                     identity=ident[:]),
                     reads=[("xb", b), "ident"], writes=[("ps", 4)])
            S.op("dve", I("tensor_copy", out=hT[:, :, t * 128:(t + 1) * 128],
                                                in_=p4.rearrange("p (c k) -> p c k", k=128)),
                 reads=[("ps", 4)], writes=[("hT", t)])

        def emit_ln_batch(tiles):
            n = len(tiles)
            for i, t in enumerate(tiles):
                S.op("dve", I("bn_stats", lnst[:, i, 0:6], xs[:, t, 0:512]), reads=[("xs", t)], writes=[("lnst0", i)])
                S.op("dve", I("bn_stats", lnst[:, i, 6:12], xs[:, t, 512:1024]), reads=[("xs", t)], writes=[("lnst1", i)])
                S.op("dve", I("bn_aggr", lnst[:, i, 12:14], lnst[:, i, 0:12]), reads=[("lnst0", i), ("lnst1", i)], writes=[("lnmv", i)])
            mvr = [("lnmv", i) for i in range(n)]
            S.op("act", I("activation", out=lnst[:, 0:n, 14], in_=lnst[:, 0:n, 13], func=AF.Ln, bias=eps_t, scale=1.0), reads=mvr, writes=["lnr"])
            S.op("act", I("activation", out=lnst[:, 0:n, 14], in_=lnst[:, 0:n, 14], func=AF.Exp, scale=-0.5), reads=["lnr"], writes=["lnr"])
            S.op("dve", I("scalar_tensor_tensor", out=lnst[:, 0:n, 15], in0=lnst[:, 0:n, 12], scalar=-1.0, in1=lnst[:, 0:n, 14], op0=ALU.mult, op1=ALU.mult),
                 reads=["lnr"] + mvr, writes=["lnn"])
            for i, t in enumerate(tiles):
                xt = xs[:, t, :]
                S.op("act", I("activation", out=xt, in_=xt, func=AF.Identity, bias=lnst[:, i, 15:16], scale=lnst[:, i, 14:15]),
                     reads=[("xs", t), "lnn", "lnr"], writes=[("xs", t)])
            for i, t in enumerate(tiles):
                xt = xs[:, t, :]
                S.op("dve", I("tensor_tensor", out=xt, in0=xt, in1=lnp[:, 0, :], op=ALU.mult), reads=[("xs", t), "lnp"], writes=[("xs", t)])
                S.op("dve", I("tensor_tensor", out=xt, in0=xt, in1=lnp[:, 1, :], op=ALU.add), reads=[("xs", t), "lnp"], writes=[("xs", t)])

        def load_lnp(l, gi):
            S.dma("sp", "lnp", I("dma_start", out=lnp[:, 0, :], in_=lnp_d[l, gi]),
                  writes=["lnp"])
            S.dma("sp", "lnp", I("dma_start", out=lnp[:, 1, :], in_=lnp_d[l, gi + 1]),
                  writes=["lnp"])

        def outproj_partial(t, lhs_fn, nk, wo3, first, lhs_reads):
            for hf in range(2):
                for j in range(nk):
                    S.op("pe", I("matmul", out=pb[6 + hf][:, 0:512], lhsT=lhs_fn(j),
                                                               rhs=wo3[:, j, hf * 512:(hf + 1) * 512],
                                                               start=(j == 0), stop=(j == nk - 1)),
                         reads=list(lhs_reads) + ["wo"], writes=[("ps", 6 + hf)])
            for hf in range(2):
                xsl = xs[:, t, hf * 512:(hf + 1) * 512]
                if first:
                    S.op("dve", I("scalar_tensor_tensor", out=xsl, in0=xsl, scalar=ALPHA, in1=pb[6 + hf][:, 0:512],
                                                                                op0=ALU.mult, op1=ALU.add),
                         reads=[("ps", 6 + hf), ("xs", t)], writes=[("xs", t)])
                else:
                    S.op("dve", I("tensor_tensor", out=xsl, in0=xsl, in1=pb[6 + hf][:, 0:512], op=ALU.add),
                         reads=[("ps", 6 + hf), ("xs", t)], writes=[("xs", t)])

        NSL, LEAD = 12, 6
        pf = [A.f32(768) for _ in range(NSL)]
        pbf = [A.bf16(768) for _ in range(NSL)]
        pieces = []
        for l in range(DEPTH):
            for src, dstb, nrows, ncols in ((w_in_d[l], win_b[l], D, IN_COLS), (w_out_d[l], wout_b[l], D, D)):
                for k in range(nrows // 128):
                    for p0 in range(0, ncols, 768):
                        n = min(768, ncols - p0)
                        rs = slice(k * 128, (k + 1) * 128)
                        pieces.append(([(0, n, src[rs, p0:p0 + n])], n, dstb[rs, p0:p0 + n]))
            for jp in range(11):
                for c in range(8):
                    rs = slice(c * 128, (c + 1) * 128)
                    pieces.append(([(0, 256, w_up_d[l][rs, jp * 256:(jp + 1) * 256]),
                                    (256, 256, w_up_d[l][rs, D_FF + jp * 256:D_FF + (jp + 1) * 256])], 512,
                                   wup_b[l, jp, :, c * 512:(c + 1) * 512]))
            for k in range(D_FF // 128):
                for p0 in range(0, D, 768):
                    n = min(768, D - p0)
                    rs = slice(k * 128, (k + 1) * 128)
                    pieces.append(([(0, n, w_dn_d[l][rs, p0:p0 + n])], n, wdn_b[l][rs, p0:p0 + n]))
        cast_eng = ["pool", "act", "dve"]
        npc = len(pieces)
        for i in range(npc + LEAD):
            if i < npc:
                sl = i % NSL
                for c0_, n_, src_ in pieces[i][0]:
                    S.dma("sp", f"pl{sl}", I("dma_start", out=pf[sl][:, c0_:c0_ + n_], in_=src_), writes=[("pf", sl)])
            j = i - LEAD
            if j >= 0:
                sl = j % NSL
                n_ = pieces[j][1]
                ce = cast_eng[j % 3]
                if ce == "act":
                    S.op("act", I("activation", out=pbf[sl][:, 0:n_], in_=pf[sl][:, 0:n_], func=AF.Copy), reads=[("pf", sl)], writes=[("pbf", sl)])
                else:
                    S.op(ce, I("tensor_copy", out=pbf[sl][:, 0:n_], in_=pf[sl][:, 0:n_]), reads=[("pf", sl)], writes=[("pbf", sl)])
                S.dma("sp", f"ps{sl}", I("dma_start", out=pieces[j][2], in_=pbf[sl][:, 0:n_]), reads=[("pbf", sl)])
        S.dma("sp", "pl0", I("dma_start", out=pf[0][64:72, 0:768], in_=ind_d[:, 0:768]), writes=[("pf", 0)])
        S.dma("sp", "pl1", I("dma_start", out=pf[1][64:72, 0:768], in_=ind_d[:, 768:1536]), writes=[("pf", 1)])
        S.dma("sp", "pl2", I("dma_start", out=pf[2][64:72, 0:512], in_=ind_d[:, 1536:2048]), writes=[("pf", 2)])
        for q_, (o_, n_) in enumerate(((0, 768), (768, 768), (1536, 512))):
            S.op("dve", I("tensor_copy", out=pbf[q_][64:72, 0:n_], in_=pf[q_][64:72, 0:n_]), reads=[("pf", q_)], writes=[("pbf", q_)])
            S.dma("sp", f"ps{q_}", I("dma_start", out=ind_b[:, o_:o_ + n_], in_=pbf[q_][64:72, 0:n_]), reads=[("pbf", q_)])
        S.barrier()

        try:
            for s in range(nseq):
                for t in range(NT):
                    S.dma("sp", f"xload{t}", I("dma_start", out=xs[:, t, :], in_=x_d[s, t * 128:(t + 1) * 128, :]),
                          writes=[("xs", t)])
                for t in range(NT):
                    emit_hT(t)
                if stop == "hT":
                    raise _Stop()
                for l in range(depth):
                    w_in = win_b[l]; w_out = wout_b[l]; w_dn = wdn_b[l]
                    S.barrier(); A.reset()
                    wR3 = A.bf16(8 * 1536).rearrange("p (c n) -> p c n", n=1536)
                    woR3 = A.bf16(3 * 1024).rearrange("p (c n) -> p c n", n=1024)
                    qrot = A.bf16(384)
                    krot = [A.bf16(384) for _ in range(2)]
                    vs1 = [A.bf16(384).rearrange("p (h d) -> p h d", d=64) for _ in range(2)]
                    vz = [A.bf16(384).rearrange("p (h d) -> p h d", d=64) for _ in range(2)]
                    qkT = [A.bf16(768) for _ in range(2)]
                    sg = [A.f32(384) for _ in range(2)]
                    scT = A.bf16(768).rearrange("p (par j c) -> p par j c", par=2, c=128)
                    ycat = [A.bf16(384) for _ in range(2)]; catT = A.bf16(384)
                    stb = [A.bf16(192).rearrange("p (j v) -> p j v", v=64) for _ in range(2)]
                    stf = A.f32(192).rearrange("p (j v) -> p j v", v=64)
                    tq = [A.f32(384).rearrange("p (h d) -> p h d", d=64) for _ in range(4)]
                    eg = A.f32(384)
                    yr = A.f32(384).rearrange("p (h d) -> p h d", d=64)
                    sq = A.f32(384).rearrange("p (h d) -> p h d", d=64)
                    st6 = A.f32(48)
                    s1 = st6[:, 0:6]; s2 = st6[:, 8:14]; mean = st6[:, 16:22]; msq = st6[:, 24:30]; var = st6[:, 32:38]; rstd = st6[:, 40:46]
                    wload("wR", wR3, w_in, 0, 8, 0, 1536, "wR")
                    wload("wo", woR3, w_out, 0, 3, 0, 1024, "wo")
                    S.op("dve", I("memset", stf, 0.0), writes=["stf"] + [("stf", h) for h in range(6)])
                    S.op("dve", I("memset", stb[0], 0.0), writes=[("stb", 0)])

                    def r_proj(t):
                        tsl = slice(t * 128, (t + 1) * 128)
                        for c in range(8):
                            for g4 in range(4):
                                S.op("pe", I("matmul", out=pb[g4][:, 0:384], lhsT=hT[:, c, tsl], rhs=wR3[:, c, g4 * 384:(g4 + 1) * 384],
                                             start=(c == 0), stop=(c == 7)),
                                     reads=[("hT", t), "wR"], writes=[("ps", g4)])

                    def r_front_ew(t):
                        p = t % 2
                        cos2 = bcm(cs[:, 0, t, :], 6)
                        nsin = bcm(cs[:, 1, t, 0:32], 6); psin = bcm(cs[:, 1, t, 32:64], 6)
                        for bank, dst, nm, o in ((0, qrot, "qrot", 0), (1, krot[p], ("krot", p), 2)):
                            pq = pb[bank][:, 0:384].rearrange("p (h d) -> p h d", d=64)
                            ra = tq[o]; rb = tq[o + 1]
                            S.op("dve", I("tensor_tensor", out=ra, in0=pq, in1=cos2, op=ALU.mult), reads=[("ps", bank), "cs"], writes=[("tq", o)])
                            S.op("dve", I("tensor_tensor", out=rb[:, :, 0:32], in0=pq[:, :, 32:64], in1=nsin, op=ALU.mult), reads=[("ps", bank), "cs"], writes=[("tq", o + 1, 0)])
                            S.op("dve", I("tensor_tensor", out=rb[:, :, 32:64], in0=pq[:, :, 0:32], in1=psin, op=ALU.mult), reads=[("ps", bank), "cs"], writes=[("tq", o + 1, 1)])
                            S.op("dve", I("tensor_tensor", out=dst.rearrange("p (h d) -> p h d", d=64), in0=ra, in1=rb, op=ALU.add),
                                 reads=[("tq", o), ("tq", o + 1, 0), ("tq", o + 1, 1)], writes=[(nm, 1), (nm, 2)])
                        pv = pb[2][:, 0:384].rearrange("p (h d) -> p h d", d=64)
                        S.op("dve", I("tensor_tensor", out=vs1[p], in0=pv, in1=bc3(vsc1, 64), op=ALU.mult), reads=[("ps", 2), "small"], writes=[("vs1", p)])
                        S.op("dve", I("tensor_tensor", out=vz[p], in0=pv, in1=bc3(zeta8, 64), op=ALU.mult), reads=[("ps", 2), "small"], writes=[("vz", p)])
                        S.op("act", I("activation", out=eg, in_=pb[3][:, 0:384], func=AF.Exp, scale=-1.0), reads=[("ps", 3)], writes=["eg"])
                        S.op("act", I("activation", out=eg, in_=eg, func=AF.Identity, bias=ones_f[:, 0:1], scale=1.0), reads=["eg", "ones_f"], writes=["eg"])
                        S.op("dve", I("reciprocal", out=eg, in_=eg), reads=["eg"], writes=["eg"])
                        S.op("dve", I("tensor_tensor", out=sg[p], in0=pb[3][:, 0:384], in1=eg, op=ALU.mult), reads=[("ps", 3), "eg"], writes=[("sg", p)])

                    def r_front_tr(t):
                        p = t % 2
                        p4 = pbb(4)
                        for j in range(3):
                            S.op("pe", I("transpose", out=p4[:, j * 128:(j + 1) * 128], in_=qrot[:, j * 128:(j + 1) * 128], identity=ident[:]),
                                 reads=[("qrot", 1), ("qrot", 2), "ident"], writes=[("ps", 4)])
                            S.op("pe", I("transpose", out=p4[:, (3 + j) * 128:(4 + j) * 128], in_=krot[p][:, j * 128:(j + 1) * 128], identity=ident[:]),
                                 reads=[(("krot", p), 1), (("krot", p), 2), "ident"], writes=[("ps", 4)])
                        S.op("act", I("activation", out=qkT[p], in_=p4[:, 0:768], func=AF.Copy), reads=[("ps", 4)], writes=[("qkT", p)])

                    def r_scores(t):
                        p = t % 2
                        for h in range(6):
                            j, hf = h // 2, h % 2
                            pr = slice(hf * 64, hf * 64 + 64)
                            S.op("pe", I("matmul", out=pb[5 + hf][:, j * 128:(j + 1) * 128], lhsT=qkT[p][pr, (3 + j) * 128:(4 + j) * 128],
                                         rhs=qkT[p][pr, j * 128:(j + 1) * 128], start=True, stop=True),
                                 reads=[("qkT", p)], writes=[("ps", 5 + hf)])

                    def r_mask(t):
                        for hf in range(2):
                            S.op("dve", I("tensor_tensor", out=scT[:, hf, :, :], in0=pb[5 + hf][:, 0:384].rearrange("p (h c) -> p h c", c=128),
                                          in1=bcm(mask8, 3), op=ALU.mult),
                                 reads=[("ps", 5 + hf), "small"], writes=[("scT", hf)])

                    def r_o_ds(t):
                        p = t % 2
                        sb_cur = stb[t % 2]
                        for h in range(6):
                            j, hf = h // 2, h % 2
                            pr = slice(hf * 64, hf * 64 + 64)
                            S.op("pe", I("matmul", out=pb[5][:, h * 64:(h + 1) * 64], lhsT=scT[:, hf, j, :], rhs=vs1[p][:, h, :], start=True, stop=False),
                                 reads=[("scT", 0), ("scT", 1), ("vs1", p)], writes=[("ps", 5)])
                            S.op("pe", I("matmul", out=pb[5][:, h * 64:(h + 1) * 64], lhsT=qkT[p][pr, j * 128:(j + 1) * 128],
                                         rhs=sb_cur[pr, j, :], start=False, stop=True),
                                 reads=[("qkT", p), ("stb", t % 2)], writes=[("ps", 5)])
                        for h in range(6):
                            j, hf = h // 2, h % 2
                            S.op("pe", I("matmul", out=pb[7][hf * 64:hf * 64 + 64, j * 64:(j + 1) * 64], lhsT=krot[p][:, h * 64:(h + 1) * 64], rhs=vz[p][:, h, :],
                                         start=True, stop=True),
                                 reads=[(("krot", p), 1), (("krot", p), 2), ("vz", p)], writes=[("ps", 7)])

                    def r_state_norm(t):
                        p = t % 2
                        sb_nxt = stb[(t + 1) % 2]
                        for j in range(3):
                            S.op("dve", I("scalar_tensor_tensor", out=stf[:, j, :], in0=stf[:, j, :], scalar=gct[:, j:j + 1],
                                          in1=pb[7][:, j * 64:(j + 1) * 64], op0=ALU.mult, op1=ALU.add),
                                 reads=[("ps", 7), ("stf", 2 * j), "small"], writes=[("stf", 2 * j), ("stf", 2 * j + 1)])
                        S.op("act", I("activation", out=sb_nxt, in_=stf, func=AF.Copy),
                             reads=[("stf", h) for h in range(6)], writes=[("stb", (t + 1) % 2)])
                        po = pb[5][:, 0:384].rearrange("p (h d) -> p h d", d=64)
                        S.op("dve", I("tensor_tensor", out=yr, in0=po, in1=bc3(xi, 64), op=ALU.mult), reads=[("ps", 5), "small"], writes=["yr"])
                        S.op("dve", I("tensor_reduce", out=s1, in_=yr, axis=AX.X, op=ALU.add), reads=["yr"], writes=["s1"])
                        S.op("dve", I("tensor_tensor", out=sq, in0=yr, in1=yr, op=ALU.mult), reads=["yr"], writes=["sq"])
                        S.op("dve", I("tensor_reduce", out=s2, in_=sq, axis=AX.X, op=ALU.add), reads=["sq"], writes=["s2"])
                        S.op("dve", I("tensor_scalar", out=mean, in0=s1, scalar1=1.0 / 64, scalar2=None, op0=ALU.mult), reads=["s1"], writes=["mean"])
                        S.op("dve", I("tensor_tensor", out=msq, in0=mean, in1=mean, op=ALU.mult), reads=["mean"], writes=["msq"])
                        S.op("dve", I("tensor_scalar", out=var, in0=s2, scalar1=1.0 / 64, scalar2=None, op0=ALU.mult), reads=["s2"], writes=["var"])
                        S.op("dve", I("tensor_tensor", out=var, in0=var, in1=msq, op=ALU.subtract), reads=["var", "msq"], writes=["var"])
                        S.op("dve", I("tensor_scalar", out=var, in0=var, scalar1=0.0, scalar2=None, op0=ALU.max), reads=["var"], writes=["var"])
                        S.op("act", I("activation", out=rstd, in_=var, func=AF.Ln, bias=eps_t, scale=1.0), reads=["var", "small"], writes=["rstd"])
                        S.op("act", I("activation", out=rstd, in_=rstd, func=AF.Exp, scale=-0.5), reads=["rstd"], writes=["rstd"])
                        S.op("dve", I("tensor_tensor", out=yr, in0=yr, in1=bc3(mean, 64), op=ALU.subtract), reads=["yr", "mean", "s1", "sq"], writes=["yr"])
                        S.op("dve", I("tensor_tensor", out=yr, in0=yr, in1=bc3(rstd, 64), op=ALU.mult), reads=["yr", "rstd"], writes=["yr"])
                        S.op("dve", I("tensor_tensor", out=ycat[p], in0=yr.rearrange("p h d -> p (h d)"), in1=sg[p], op=ALU.mult), reads=["yr", ("sg", p)], writes=[("ycat", p)])

                    def r_out(t):
                        p4 = pbb(4)
                        for j in range(3):
                            S.op("pe", I("transpose", out=p4[:, j * 128:(j + 1) * 128], in_=ycat[t % 2][:, j * 128:(j + 1) * 128], identity=ident[:]),
                                 reads=[("ycat", t % 2), "ident"], writes=[("ps", 4)])
                        S.op("act", I("activation", out=catT, in_=p4[:, 0:384], func=AF.Copy), reads=[("ps", 4)], writes=["catT"])
                        outproj_partial(t, lambda j: catT[:, j * 128:(j + 1) * 128], 3, woR3, True, ["catT"])

                    for t in range(NT + 2):
                        back = 1 <= t <= NT
                        if t < NT:
                            r_proj(t)
                        if back:
                            r_scores(t - 1)
                        if t < NT:
                            r_front_ew(t)
                        if back:
                            r_mask(t - 1)
                        if t < NT:
                            r_front_tr(t)
                        if t >= 2:
                            r_out(t - 2)
                        if back:
                            r_o_ds(t - 1)
                            r_state_norm(t - 1)

                    if stop == "R":
                        raise _Stop()
                    S.barrier(); A.reset()
                    wM3 = A.bf16(8 * 1152).rearrange("p (c n) -> p c n", n=1152)
                    woM3 = A.bf16(3 * 1024).rearrange("p (c n) -> p c n", n=1024)
                    KT = A.bf16(6 * SEQ).rearrange("p (h k) -> p h k", k=SEQ)
                    QT = A.bf16(6 * 512).rearrange("p (h k) -> p h k", k=512)
                    Va = A.bf16(NT * 6 * 65 + 2).rearrange("p (t h d) -> p t h d", h=6, d=65) if False else None
                    Vraw = A.bf16(NT * 6 * 66)
                    Va = Vraw.rearrange("p (t h d) -> p t h d", h=6, d=66)
                    NPT = 4
                    PT = [A.bf16(512) for _ in range(NPT)]
                    OTs = A.f32(512); rec = OTs
                    catM = A.bf16(3 * 512).rearrange("p (j k) -> p j k", k=512)
                    NM = [A.bf16(6 * 72).rearrange("p (h n) -> p h n", n=72) for _ in range(4)]
                    gsb = [A.f32(48).rearrange("p (h n) -> p h n", n=8) for _ in range(4)]
                    m8 = [A.f32(48).rearrange("p (h n) -> p h n", n=8) for _ in range(4)]
                    ge = [A.f32(48).rearrange("p (h n) -> p h n", n=8) for _ in range(4)]
                    km = A.f32(48).rearrange("p (h n) -> p h n", n=8)
                    kmT = A.bf16(48).rearrange("p (h n) -> p h n", n=8)
                    wload("wM", wM3, w_in, 0, 8, 1536, 2688, "wM")
                    wload("wo", woM3, w_out, 384, 3, 0, 1024, "wo")
                    for h in range(6):
                        S.dma("sp", f"ind{h}", I("dma_start", out=KT[64:72, h, :], in_=ind_b), writes=[("KTi", h)])
                    S.op("dve", I("memset", Vraw, 1.0), writes=["Va1"])
                    for i in range(4):
                        S.op("dve", I("memset", NM[i], 0.0), writes=[("NM", i)])
                    S.op("dve", I("memset", kmT, 0.0), writes=["kmT"])
                    S.op("dve", I("memset", km, 0.0), writes=["km"])
                    sidx = [0]; pidx = [0]; oidx = [0]
                    for G in range(4):
                        gsl = slice(G * 512, (G + 1) * 512)
                        hTr = [("hT", 4 * G + i) for i in range(4)]
                        for h in range(6):
                            for which in range(2):
                                bank = which
                                c0 = which * 384 + h * 64
                                for c in range(8):
                                    S.op("pe", I("matmul", out=pb[bank][0:64, 0:512], lhsT=wM3[:, c, c0:c0 + 64], rhs=hT[:, c, gsl],
                                                                                         start=(c == 0), stop=(c == 7)),
                                         reads=hTr + ["wM"], writes=[("ps", bank)])
                                if which == 0:
                                    S.op("act", I("activation", out=QT[0:64, h, :], in_=pb[0][0:64, 0:512], func=AF.Identity, scale=0.125),
                                         reads=[("ps", 0)], writes=[("QT", h)])
                                else:
                                    S.op("dve", I("tensor_copy", out=KT[0:64, h, gsl], in_=pb[1][0:64, 0:512]),
                                         reads=[("ps", 1)], writes=[("KT", h, G)])
                        for i in range(4):
                            tt = 4 * G + i
                            bank = 2 + (i % 2)
                            for c in range(8):
                                S.op("pe", I("matmul", out=pb[bank][:, 0:384], lhsT=hT[:, c, tt * 128:(tt + 1) * 128], rhs=wM3[:, c, 768:1152],
                                                                                     start=(c == 0), stop=(c == 7)),
                                     reads=[("hT", tt), "wM"], writes=[("ps", bank)])
                            S.op("act", I("activation", out=Va[:, tt, :, 0:64], in_=pb[bank][:, 0:384].rearrange("p (h d) -> p h d", d=64), func=AF.Copy),
                                 reads=[("ps", bank), "Va1"], writes=[("Va", tt)])
                        for h in range(6):
                            S.op("dve", I("tensor_reduce", out=km[0:64, h, 2 * G:2 * G + 2], in_=KT[0:64, h, gsl].rearrange("p (n k) -> p n k", k=256),
                                                                      axis=AX.X, op=ALU.add),
                                 reads=[("KT", h, G), "km"], writes=[("km", h)])
                        S.op("dve", I("tensor_scalar", out=kmT[0:64, :, 2 * G:2 * G + 2], in0=km[0:64, :, 2 * G:2 * G + 2], scalar1=1.0 / 256, scalar2=None, op0=ALU.mult),
                             reads=[("km", h) for h in range(6)] + ["kmT"], writes=["kmT"])
                        for i in range(4):
                            for h in range(6):
                                S.op("pe", I("matmul", out=pb[5][:, i * 48 + h * 8:i * 48 + (h + 1) * 8], lhsT=QT[0:64, h, i * 128:(i + 1) * 128], rhs=kmT[0:64, h, :],
                                             start=True, stop=True),
                                     reads=[("QT", h), "kmT"], writes=[("ps", 5)])
                        for i in range(4):
                            qt = 4 * G + i
                            nb = NM[i]
                            S.op("dve", I("tensor_tensor", out=gsb[i], in0=pb[5][:, i * 48:(i + 1) * 48].rearrange("p (h n) -> p h n", n=8), in1=bcm(ownmask[:, qt, :], 6), op=ALU.add),
                                 reads=[("ps", 5), "small"], writes=[("gsb", i)])
                            for h in range(6):
                                S.op("dve", I("max", out=m8[i][:, h, :], in_=gsb[i][:, h, :]), reads=[("gsb", i)], writes=[("m8", i, h)])
                            S.op("dve", I("tensor_tensor", out=ge[i], in0=gsb[i], in1=m8[i][:, :, 3:4].to_broadcast([128, 6, 8]), op=ALU.is_ge),
                                 reads=[("gsb", i)] + [("m8", i, h) for h in range(6)], writes=[("ge", i)])
                            S.op("dve", I("tensor_scalar", out=nb[:, :, 64:72], in0=ge[i], scalar1=-1.0, scalar2=-NEG, op0=ALU.add, op1=ALU.mult),
                                 reads=[("ge", i), ("NM", i)], writes=[("NM", i)])
                        for i in range(4):
                            nb = NM[i]
                            p4 = pbb(4)
                            for h in range(6):
                                S.op("pe", I("transpose", out=p4[0:72, h * 128:(h + 1) * 128], in_=nb[:, h, :], identity=ident[:]),
                                     reads=[("NM", i), "ident"], writes=[("ps", 4)])
                            S.op("act", I("activation", out=QT[64:72, :, i * 128:(i + 1) * 128], in_=p4[64:72, 0:768].rearrange("p (h k) -> p h k", k=128), func=AF.Copy),
                                 reads=[("ps", 4)] + [("QT", h) for h in range(6)], writes=[("QTn", i)])
                        nkt = 4 * G + 4
                        seq = [(h, kt) for h in range(6) for kt in range(nkt)]
                        LA = 3
                        slot = {}
                        obs = {}
                        for h in range(6):
                            obs[h] = 6 + (oidx[0] % 2); oidx[0] += 1
                        for idx in range(len(seq) + LA):
                            if idx < len(seq):
                                h, kt = seq[idx]
                                rel = kt - 4 * G
                                c0 = max(0, rel) * 128
                                sbk = sidx[0] % 4; sidx[0] += 1
                                ptb = pidx[0] % NPT; pidx[0] += 1
                                slot[idx] = ptb
                                Pt = PT[ptb]
                                S.op("pe", I("matmul", out=pb[sbk][:, c0:512], lhsT=KT[0:72, h, kt * 128:(kt + 1) * 128], rhs=QT[0:72, h, c0:512],
                                             start=True, stop=True),
                                     reads=[("KT", h, kt // 4), ("KTi", h), ("QT", h)] + [("QTn", i) for i in range(4)], writes=[("ps", sbk)])
                                if -1 <= rel <= 3:
                                    if rel == -1:
                                        a0, a1, bsl = 0, 128, bt[:, h, 1, :]
                                    elif rel == 3:
                                        a0, a1, bsl = 384, 512, bt[:, h, 0, :]
                                    else:
                                        a0, a1, bsl = rel * 128, rel * 128 + 256, bt[:, h, :, :].rearrange("p a q -> p (a q)")
                                    S.op("dve", I("tensor_tensor", out=pb[sbk][:, a0:a1], in0=pb[sbk][:, a0:a1], in1=bsl, op=ALU.add),
                                         reads=[("ps", sbk), "bt"], writes=[("ps", sbk)])
                                S.op("act", I("activation", out=Pt[:, c0:512], in_=pb[sbk][:, c0:512], func=AF.Exp, bias=cfar[:, h:h + 1], scale=1.0),
                                     reads=[("ps", sbk), "cfar", ("PT", ptb)], writes=[("PTf", ptb)])
                            jdx = idx - LA
                            if jdx >= 0:
                                h, kt = seq[jdx]
                                j, hf = h // 2, h % 2
                                ob = obs[h]
                                rel = kt - 4 * G
                                c0 = max(0, rel) * 128
                                ptb = slot[jdx]
                                Pt = PT[ptb]
                                S.op("pe", I("matmul", out=pb[ob][0:65, c0:512], lhsT=Va[:, kt, h, 0:65], rhs=Pt[:, c0:512],
                                             start=(kt == 0), stop=(kt == nkt - 1)),
                                     reads=[("Va", kt), "Va1", ("PTf", ptb)], writes=[("ps", ob), ("PT", ptb)])
                                if kt == nkt - 1:
                                    S.op("dve", I("reciprocal", out=rec[64:65, :], in_=pb[ob][64:65, 0:512]), reads=[("ps", ob)], writes=["rec"])
                                    S.op("pe", I("matmul", out=pb[5][0:64, 0:512], lhsT=ones_f[64:65, 0:64], rhs=rec[64:65, :], start=True, stop=True),
                                         reads=["rec", "ones_f"], writes=[("ps", 5)])
                                    S.op("act", I("activation", out=OTs[0:64, :], in_=pb[ob][0:64, 0:512], func=AF.Copy), reads=[("ps", ob)], writes=["OTs"])
                                    S.op("dve", I("tensor_tensor", out=catM[hf * 64:hf * 64 + 64, j, :], in0=OTs[0:64, :], in1=pb[5][0:64, 0:512], op=ALU.mult),
                                         reads=["OTs", ("ps", 5)], writes=[("catM", h)])
                        for i in range(4):
                            tt = 4 * G + i
                            outproj_partial(tt, lambda j, i=i: catM[:, j, i * 128:(i + 1) * 128], 3, woM3, False, [("catM", h) for h in range(6)])

                    if stop == "M":
                        raise _Stop()
                    S.barrier(); A.reset()
                    wC3 = A.bf16(8 * 768).rearrange("p (c n) -> p c n", n=768)
                    woC3 = A.bf16(2 * 1024).rearrange("p (c n) -> p c n", n=1024)
                    catC = A.bf16(2 * SEQ).rearrange("p (j k) -> p j k", k=SEQ)
                    pbuf = [A.f32(514) for _ in range(2)]
                    ccs = A.f32(512); acc = A.f32(512)
                    wload("wC", wC3, w_in, 0, 8, 2688, 3456, "wC")
                    wload("wo", woC3, w_out, 768, 2, 0, 1024, "wo")
                    S.dma("sp", "cw", I("dma_start", out=cw[:], in_=cw_d[l]), writes=["cw"])
                    load_lnp(l, 0)
                    for j in range(2):
                        S.op("dve", I("memset", pbuf[j][:, 0:2], 0.0), writes=[("pbuf", j)])
                    for G in range(4):
                        gsl = slice(G * 512, (G + 1) * 512)
                        hTr = [("hT", 4 * G + i) for i in range(4)]
                        for j in range(2):
                            for part in range(3):
                                c0 = part * 256 + j * 128
                                for c in range(8):
                                    S.op("pe", I("matmul", out=pb[part][:, 0:512], lhsT=wC3[:, c, c0:c0 + 128], rhs=hT[:, c, gsl],
                                                                                         start=(c == 0), stop=(c == 7)),
                                         reads=hTr + ["wC"], writes=[("ps", part)])
                            pbj = pbuf[j]
                            S.op("act", I("activation", out=ccs, in_=pb[1][:, 0:512], func=AF.Copy), reads=[("ps", 1)], writes=["ccs"])
                            S.op("dve", I("tensor_tensor", out=pbj[:, 2:514], in0=pb[2][:, 0:512], in1=ccs, op=ALU.mult),
                                 reads=[("ps", 2), "ccs", ("pbuf", j)], writes=[("pbufm", j)])
                            S.op("dve", I("tensor_scalar", out=acc, in0=pbj[:, 2:514], scalar1=cw[:, j * 3 + 2:j * 3 + 3], scalar2=None, op0=ALU.mult),
                                 reads=[("pbufm", j), "cw"], writes=["acc"])
                            S.op("dve", I("scalar_tensor_tensor", out=acc, in0=pbj[:, 1:513], scalar=cw[:, j * 3 + 1:j * 3 + 2], in1=acc, op0=ALU.mult, op1=ALU.add),
                                 reads=[("pbufm", j), ("pbuf", j), "acc", "cw"], writes=["acc"])
                            S.op("dve", I("scalar_tensor_tensor", out=acc, in0=pbj[:, 0:512], scalar=cw[:, j * 3:j * 3 + 1], in1=acc, op0=ALU.mult, op1=ALU.add),
                                 reads=[("pbufm", j), ("pbuf", j), "acc", "cw"], writes=["acc"])
                            S.op("dve", I("tensor_tensor", out=catC[:, j, gsl], in0=pb[0][:, 0:512], in1=acc, op=ALU.mult),
                                 reads=[("ps", 0), "acc"], writes=[("catC", j, G)])
                            S.op("dve", I("tensor_copy", out=pbj[:, 0:2], in_=pbj[:, 512:514]),
                                 reads=[("pbufm", j)], writes=[("pbuf", j)])
                        for i in range(4):
                            tt = 4 * G + i
                            outproj_partial(tt, lambda j, tt=tt: catC[:, j, tt * 128:(tt + 1) * 128], 2, woC3, False, [("catC", 0, G), ("catC", 1, G)])
                        emit_ln_batch([4 * G + i for i in range(4)])
                        for i in range(4):
                            emit_hT(4 * G + i)

                    if stop == "C":
                        raise _Stop()
                    S.barrier(); A.reset()
                    Gp = A.bf16(8 * SEQ).rearrange("p (j k) -> p j k", k=SEQ)
                    wD3 = A.bf16(8 * 1024).rearrange("p (j n) -> p j n", n=1024)
                    wUflat = [A.bf16(8 * 512) for _ in range(2)]
                    wU = [w_.rearrange("p (c n) -> p c n", n=512) for w_ in wUflat]
                    abuf = [A.f32(514) for _ in range(2)]
                    accs = [A.f32(512) for _ in range(2)]
                    gls = [A.f32(512) for _ in range(2)]
                    S.dma("sp", "fcw", I("dma_start", out=fcw[:], in_=fcw_d[l]), writes=["fcw"])
                    load_lnp(l, 2)
                    parts = [(0, 8), (8, 16), (16, 22)]
                    upc = [0]; itc = [0]; bai = [0]; bbi = [0]; dpi = [0]
                    for pi, (j0, j1) in enumerate(parts):
                        nj = j1 - j0
                        wload("wD", wD3, w_dn, j0 * 128, nj, 0, 1024, "wD")
                        its = []
                        for jp in range(j0 // 2, j1 // 2):
                            wb = upc[0] % 2; upc[0] += 1
                            for sub in range(2):
                                for G in range(4):
                                    its.append((jp, wb, sub, G))

                        def f_stage1(i):
                            jp, wb, sub, G = its[i]
                            wu = wU[wb]
                            if sub == 0 and G == 0:
                                nxt = [(jp2, wb2) for (jp2, wb2, s2, g2) in its if jp2 == jp + 1 and s2 == 0 and g2 == 0]
                                if nxt:
                                    S.dma("sp", f"wU{nxt[0][1]}", I("dma_start", out=wUflat[nxt[0][1]], in_=wup_b[l, nxt[0][0]]), writes=[("wU", nxt[0][1])])
                            jg = 2 * jp + sub
                            st = itc[0] % 2; itc[0] += 1
                            fset[i] = st
                            ab = abuf[st]; ac = accs[st]
                            gsl = slice(G * 512, (G + 1) * 512)
                            hTr = [("hT", 4 * G + q) for q in range(4)]
                            ba = bai[0] % 2; bai[0] += 1
                            bb = (2, 3, 5)[bbi[0] % 3]; bbi[0] += 1
                            fbb[i] = bb
                            for c in range(8):
                                S.op("pe", I("matmul", out=pb[ba][:, 0:512], lhsT=wu[:, c, sub * 128:(sub + 1) * 128], rhs=hT[:, c, gsl],
                                             start=(c == 0), stop=(c == 7)),
                                     reads=hTr + [("wU", wb)], writes=[("ps", ba)])
                            for c in range(8):
                                S.op("pe", I("matmul", out=pb[bb][:, 0:512], lhsT=wu[:, c, 256 + sub * 128:256 + (sub + 1) * 128], rhs=hT[:, c, gsl],
                                             start=(c == 0), stop=(c == 7)),
                                     reads=hTr + [("wU", wb)], writes=[("ps", bb)])
                            if G == 0:
                                S.op("pool", I("memset", ab[:, 0:2], 0.0), writes=[("abh", st)])
                            else:
                                S.op("pool", I("tensor_copy", out=ab[:, 0:2], in_=abuf[1 - st][:, 512:514]), reads=[("abm", 1 - st)], writes=[("abh", st)])
                            S.op("act", I("activation", out=ab[:, 2:514], in_=pb[ba][:, 0:512], func=AF.Copy),
                                 reads=[("ps", ba)], writes=[("abm", st)])
                            k3 = jg * 3
                            S.op("act", I("activation", out=ac, in_=pb[ba][:, 0:512], func=AF.Identity, scale=fcw[:, k3 + 2:k3 + 3]),
                                 reads=[("ps", ba), "fcw"], writes=[("acc", st)])
                            S.op("dve", I("scalar_tensor_tensor", out=ac, in0=ab[:, 1:513], scalar=fcw[:, k3 + 1:k3 + 2], in1=ac, op0=ALU.mult, op1=ALU.add),
                                 reads=[("abm", st), ("abh", st), ("acc", st), "fcw"], writes=[("acc", st)])
                            S.op("dve", I("scalar_tensor_tensor", out=ac, in0=ab[:, 0:512], scalar=fcw[:, k3:k3 + 1], in1=ac, op0=ALU.mult, op1=ALU.add),
                                 reads=[("abm", st), ("abh", st), ("acc", st), "fcw"], writes=[("acc", st)])

                        def f_stage2(i):
                            jp, wb, sub, G = its[i]
                            st = fset[i]; bb = fbb[i]
                            jj = 2 * jp + sub - j0
                            gsl = slice(G * 512, (G + 1) * 512)
                            S.op("act", I("activation", out=gls[st], in_=accs[st], func=AF.Gelu), reads=[("acc", st)], writes=[("gl", st)])
                            S.op("dve", I("tensor_tensor", out=Gp[:, jj, gsl], in0=pb[bb][:, 0:512], in1=gls[st], op=ALU.mult),
                                 reads=[("ps", bb), ("gl", st)], writes=[("Gp", jj, G)])

                        fset = {}; fbb = {}
                        S.dma("sp", f"wU{its[0][1]}", I("dma_start", out=wUflat[its[0][1]], in_=wup_b[l, its[0][0]]), writes=[("wU", its[0][1])])
                        for i in range(len(its) + 1):
                            if i < len(its):
                                f_stage1(i)
                            if i >= 1:
                                f_stage2(i - 1)
                        last = (pi == len(parts) - 1)
                        for t in range(NT):
                            G = t // 4
                            outproj_partial_reads = [("Gp", jj, G) for jj in range(nj)]
                            dbk = ((6, 7), (0, 1), (2, 3))[dpi[0] % 3]; dpi[0] += 1
                            for hf in range(2):
                                for jj in range(nj):
                                    S.op("pe", I("matmul", out=pb[dbk[hf]][:, 0:512], lhsT=Gp[:, jj, t * 128:(t + 1) * 128], rhs=wD3[:, jj, hf * 512:(hf + 1) * 512],
                                                 start=(jj == 0), stop=(jj == nj - 1)),
                                         reads=outproj_partial_reads + ["wD"], writes=[("ps", dbk[hf])])
                            for hf in range(2):
                                xsl = xs[:, t, hf * 512:(hf + 1) * 512]
                                if pi == 0:
                                    S.op("dve", I("scalar_tensor_tensor", out=xsl, in0=xsl, scalar=ALPHA, in1=pb[dbk[hf]][:, 0:512], op0=ALU.mult, op1=ALU.add),
                                         reads=[("ps", dbk[hf]), ("xs", t)], writes=[("xs", t)])
                                else:
                                    S.op("dve", I("tensor_tensor", out=xsl, in0=xsl, in1=pb[dbk[hf]][:, 0:512], op=ALU.add),
                                         reads=[("ps", dbk[hf]), ("xs", t)], writes=[("xs", t)])
                            if last and t % 4 == 3:
                                tl = [t - 3, t - 2, t - 1, t]
                                emit_ln_batch(tl)
                                for t2 in tl:
                                    if l == depth - 1:
                                        S.dma("sp", "ostore", I("dma_start", out=out_d[s, t2 * 128:(t2 + 1) * 128, :], in_=xs[:, t2, :]),
                                              reads=[("xs", t2)])
                                    else:
                                        emit_hT(t2)
                S.barrier()

        except _Stop:
            S.barrier()
            for t in range(NT):
                S.dma("sp", "ostore", I("dma_start", out=out_d[0, t * 128:(t + 1) * 128, :], in_=xs[:, t, :]), reads=[("xs", t)])
        S.barrier()
        S.emit()
    return nc


_NC_CACHE = {}


def _get_nc(nseq, depth):
    key = (nseq, depth)
    if key not in _NC_CACHE:
        _NC_CACHE[key] = build(nseq, depth)
    return _NC_CACHE[key]


def host_inputs(x_shard, w_in, conv_w, w_out, ln1_g, ln1_b, w_up, ffn_conv_w, w_down, ln2_g, ln2_b, rel_bias):
    cs_h, small_h, gC, ind_h, bidx_d, bidx_s, causal = _host_consts()
    f = lambda a: np.ascontiguousarray(a, dtype=np.float32)
    lnp = np.broadcast_to(np.stack([ln1_g, ln1_b, ln2_g, ln2_b], axis=1)[:, :, None, :], (DEPTH, 4, 128, D))
    cw = conv_w.reshape(DEPTH, 3, 2, 128).transpose(0, 3, 2, 1).reshape(DEPTH, 128, 6)
    fcw = ffn_conv_w.reshape(DEPTH, 3, 22, 128).transpose(0, 3, 2, 1).reshape(DEPTH, 128, 66)
    bd = np.where(causal[:, :, None], rel_bias[bidx_d], np.float32(NEG))
    bs = rel_bias[bidx_s]
    btile = np.stack([bd, bs], axis=0).transpose(1, 3, 0, 2).reshape(128, 6 * 2 * 128)
    cfar = np.broadcast_to(rel_bias[31][None, :], (128, 6))
    return {
        "x": f(x_shard), "w_in": f(w_in), "w_out": f(w_out), "w_up": f(w_up), "w_down": f(w_down),
        "lnp": f(lnp), "cw": f(cw), "fcw": f(fcw), "cs": f(cs_h.reshape(128, -1)), "btile": f(btile),
        "cfar": f(cfar), "small": f(small_h), "ind": f(ind_h),
    }


def kernel(x, w_in, conv_w, w_out, ln1_g, ln1_b, w_up, ffn_conv_w, w_down, ln2_g, ln2_b, rel_bias):
    args = [np.asarray(a) for a in (w_in, conv_w, w_out, ln1_g, ln1_b, w_up, ffn_conv_w, w_down, ln2_g, ln2_b, rel_bias)]
    x = np.asarray(x)
    B = x.shape[0]
    per = B // N_CORES
    nc = _get_nc(per, DEPTH)
    in_maps = []
    base = None
    for c in range(N_CORES):
        m = host_inputs(x[c * per:(c + 1) * per], *args) if base is None else dict(base, x=np.ascontiguousarray(x[c * per:(c + 1) * per], dtype=np.float32))
        if base is None:
            base = m
        in_maps.append(m)
    res = run_bass_kernel_spmd(nc, in_maps, core_ids=list(range(N_CORES)))
    return np.concatenate([np.asarray(r["out"]) for r in res.results], axis=0).astype(np.float32)
```

```python
import contextlib
import math
import numpy as np
import concourse.bass as bass
import concourse.mybir as mybir
from concourse.bass_utils import run_bass_kernel_spmd

F32 = mybir.dt.float32
BF16 = mybir.dt.bfloat16
AF = mybir.ActivationFunctionType
ALU = mybir.AluOpType
AX = mybir.AxisListType

N_CORES = 8
SEQ = 2048
D = 1024
NT = SEQ // 128
DEPTH = 2
IN_COLS = 3456
D_FF = 2816
ALPHA = (2.0 * DEPTH) ** 0.25
LN_EPS = 1e-5
NEG = -30000.0
BIG = 3.0e38

ENGS = ["pe", "act", "dve", "pool", "sp"]
SEM_CH = 12000


def I(name, *a, **k):
    return (name, a, k)


class Sched:
    def __init__(self, nc):
        self.nc = nc
        self.ops = {e: [] for e in ENGS}
        self.count = {e: 0 for e in ENGS}
        self.seen = {e: {} for e in ENGS}
        self.res = {}
        self.dma_cnt = {}

    def _deps(self, eng, reads, writes, skip_dma=None):
        deps = []
        for r in reads:
            st = self.res.get(r)
            if st and st[0] is not None:
                deps.append((st[0], True))
            if st and isinstance(r, tuple) and r[0] == "ps":
                for t in st[1]:
                    if t[1] != eng:
                        deps.append((t, True))
        for w in writes:
            st = self.res.get(w)
            if st:
                if st[0] is not None:
                    deps.append((st[0], False))
                for t in st[1]:
                    deps.append((t, False))
        seen = self.seen[eng]
        best = {}
        for tok, raw in deps:
            kind, key, val = tok
            if kind == "eng" and key == eng and not raw and eng == "pe":
                continue
            if kind == "dma" and key == skip_dma:
                continue
            k = (kind, key)
            if seen.get(k, 0) >= val:
                continue
            best[k] = max(best.get(k, 0), val)
        for k, v in best.items():
            seen[k] = v
        return [(k[0], k[1], v) for k, v in best.items()]

    def _commit(self, tok, reads, writes):
        for r in reads:
            st = self.res.setdefault(r, [None, []])
            st[1].append(tok)
        for w in writes:
            self.res[w] = [tok, []]

    def op(self, eng, fn, reads=(), writes=()):
        waits = self._deps(eng, reads, writes)
        self.count[eng] += 1
        tok = ("eng", eng, self.count[eng])
        self.ops[eng].append((waits, fn, tok))
        self._commit(tok, reads, writes)
        return tok

    def dma(self, q, sem, fn, reads=(), writes=()):
        waits = self._deps(q, reads, writes, skip_dma=sem)
        self.dma_cnt[sem] = self.dma_cnt.get(sem, 0) + 16
        tok = ("dma", sem, self.dma_cnt[sem])
        self.ops[q].append((waits, fn, tok))
        self._commit(tok, reads, writes)
        return tok

    def barrier(self):
        for e in ENGS:
            waits = []
            seen = self.seen[e]
            for e2 in ENGS:
                if e2 != e and self.count[e2] > seen.get(("eng", e2), 0):
                    waits.append(("eng", e2, self.count[e2]))
                    seen[("eng", e2)] = self.count[e2]
            for s, c in self.dma_cnt.items():
                if c > seen.get(("dma", s), 0):
                    waits.append(("dma", s, c))
                    seen[("dma", s)] = c
            if waits:
                self.ops[e].append((waits, None, None))
        self.res = {}

    def emit(self):
        nc = self.nc
        needed = {e: set() for e in ENGS}
        for e in ENGS:
            for waits, fn, tok in self.ops[e]:
                for kind, key, val in waits:
                    if kind == "eng":
                        needed[key].add(val)
        rank = {}
        for e in ENGS:
            rank[e] = {s: i + 1 for i, s in enumerate(sorted(needed[e]))}
        with contextlib.ExitStack() as es:
            esem = {}
            for e in ENGS:
                n = (len(rank[e]) + SEM_CH - 1) // SEM_CH
                esem[e] = [es.enter_context(nc.semaphore(f"s_{e}_{i}")) for i in range(max(n, 1))]
            dsem = {name: es.enter_context(nc.semaphore(f"d_{name}")) for name in self.dma_cnt}

            def lower(tok):
                kind, key, val = tok
                if kind == "eng":
                    r = rank[key][val]
                    return esem[key][(r - 1) // SEM_CH], (r - 1) % SEM_CH + 1
                return dsem[key], val

            def run(engobj, name):
                for waits, fn, tok in self.ops[name]:
                    for w in waits:
                        s, v = lower(w)
                        engobj.wait_ge(s, v)
                    if fn is None:
                        continue
                    ins = getattr(engobj, fn[0])(*fn[1], **fn[2])
                    if tok[0] == "eng":
                        if tok[2] in rank[name]:
                            s, v = lower(tok)
                            ins.then_inc(s, 1)
                    else:
                        s, v = lower(tok)
                        ins.then_inc(s, 16)

            with nc.Block() as block:
                @block.tensor
                def _(e):
                    run(e, "pe")

                @block.scalar
                def _(e):
                    run(e, "act")

                @block.vector
                def _(e):
                    run(e, "dve")

                @block.gpsimd
                def _(e):
                    run(e, "pool")

                @block.sync
                def _(e):
                    run(e, "sp")


class Arena:
    def __init__(self, t, words):
        self.t = t
        self.words = words
        self.off = 0

    def reset(self):
        self.off = 0

    def f32(self, n):
        assert self.off + n <= self.words, ("arena overflow", self.off + n, self.words)
        v = self.t[:, self.off:self.off + n]
        self.off += n
        return v

    def bf16(self, n):
        w = (n + 1) // 2
        assert self.off + w <= self.words, ("arena overflow", self.off + w, self.words)
        v = self.t[:, self.off:self.off + w].bitcast(BF16)
        self.off += w
        return v


SM_OWN = 0
SM_MASK = 128
SM_VSC = 256
SM_XI = 262
SM_ZETA = 268
SM_EPS = 274
SM_GC = 275
SM_W = 280


def _t5_bucket(dist):
    n = np.maximum(dist, 0)
    nf = np.maximum(n, 1).astype(np.float32)
    large = 16 + (np.log(nf / np.float32(16)) / np.float32(math.log(128 / 16)) * np.float32(16)).astype(np.int32)
    large = np.minimum(large, 31)
    return np.where(n < 16, n, large)


def _host_consts():
    p = np.arange(128)
    inv = (10000.0 ** (-np.arange(0, 64, 2, dtype=np.float32) / np.float32(64))).astype(np.float32)
    pos = (np.arange(NT)[None, :] * 128 + p[:, None]).astype(np.float32)
    ang = pos[:, :, None] * inv[None, None, :]
    cs = np.stack([np.concatenate([np.cos(ang), np.cos(ang)], axis=-1), np.concatenate([-np.sin(ang), np.sin(ang)], axis=-1)], axis=1).astype(np.float32)
    small = np.zeros((128, SM_W), np.float32)
    own = np.zeros((NT, 8), np.float32)
    for qt in range(NT):
        o = qt // 2
        own[qt, o] = BIG
        own[qt, o + 1:] = -BIG
    small[:, SM_OWN:SM_OWN + 128] = own.reshape(1, 128)
    e = p[:, None]; c = p[None, :]
    small[:, SM_MASK:SM_MASK + 128] = np.where(c >= e, 0.125, 0.0)
    g = 1.0 - 2.0 ** (-5.0 - np.arange(6))
    small[:, SM_VSC:SM_VSC + 6] = g[None, :] ** (-(p[:, None] + 1.0))
    small[:, SM_XI:SM_XI + 6] = g[None, :] ** (p[:, None] + 1.0)
    small[:, SM_ZETA:SM_ZETA + 6] = 0.125 * g[None, :] ** (127.0 - p[:, None])
    small[:, SM_EPS] = LN_EPS
    for j in range(3):
        small[0:64, SM_GC + j] = g[2 * j] ** 128.0
        small[64:128, SM_GC + j] = g[2 * j + 1] ** 128.0
    gC = [float(x) for x in g ** 128.0]
    ind = (np.arange(SEQ)[None, :] // 256 == np.arange(8)[:, None]).astype(np.float32)
    bidx_d = _t5_bucket(c - e)
    bidx_s = _t5_bucket(128 + c - e)
    causal = (c >= e)
    return cs, small, gC, ind, bidx_d, bidx_s, causal


class _Stop(Exception):
    pass


def build(nseq=4, depth=DEPTH, stop=None):
    cs_h, small_h, gC, ind_h, _, _, _ = _host_consts()
    nc = bass.Bass("TRN2", target_bir_lowering=False)
    x_d = nc.dram_tensor("x", [nseq, SEQ, D], F32, kind="ExternalInput").ap()
    w_in_d = nc.dram_tensor("w_in", [DEPTH, D, IN_COLS], F32, kind="ExternalInput").ap()
    w_out_d = nc.dram_tensor("w_out", [DEPTH, D, D], F32, kind="ExternalInput").ap()
    w_up_d = nc.dram_tensor("w_up", [DEPTH, D, 2 * D_FF], F32, kind="ExternalInput").ap()
    w_dn_d = nc.dram_tensor("w_down", [DEPTH, D_FF, D], F32, kind="ExternalInput").ap()
    lnp_d = nc.dram_tensor("lnp", [DEPTH, 4, 128, D], F32, kind="ExternalInput").ap()
    cw_d = nc.dram_tensor("cw", [DEPTH, 128, 6], F32, kind="ExternalInput").ap()
    fcw_d = nc.dram_tensor("fcw", [DEPTH, 128, 66], F32, kind="ExternalInput").ap()
    cs_d = nc.dram_tensor("cs", [128, 2 * NT * 64], F32, kind="ExternalInput").ap()
    bt_d = nc.dram_tensor("btile", [128, 6 * 2 * 128], F32, kind="ExternalInput").ap()
    cfar_d = nc.dram_tensor("cfar", [128, 6], F32, kind="ExternalInput").ap()
    small_d = nc.dram_tensor("small", [128, SM_W], F32, kind="ExternalInput").ap()
    ind_d = nc.dram_tensor("ind", [8, SEQ], F32, kind="ExternalInput").ap()
    out_d = nc.dram_tensor("out", [nseq, SEQ, D], F32, kind="ExternalOutput").ap()
    win_b = nc.dram_tensor("win_b", [DEPTH, D, IN_COLS], BF16).ap()
    wout_b = nc.dram_tensor("wout_b", [DEPTH, D, D], BF16).ap()
    wup_b = nc.dram_tensor("wup_b", [DEPTH, 11, 128, 8 * 512], BF16).ap()
    wdn_b = nc.dram_tensor("wdn_b", [DEPTH, D_FF, D], BF16).ap()
    ind_b = nc.dram_tensor("ind_b", [8, SEQ], BF16).ap()

    S = Sched(nc)

    def ck(tag):
        if stop == tag:
            raise _Stop()
    ARENA_W = 20992
    with contextlib.ExitStack() as es:
        def sb(name, shape, dt=F32):
            return es.enter_context(nc.sbuf_tensor("sb_" + name, shape, dt))

        xs = sb("xs", [128, NT, D])
        hT = sb("hT", [128, 8, SEQ], BF16)
        cs = sb("cs", [128, 2, NT, 64])
        bt = sb("bt", [128, 6, 2, 128])
        cfar = sb("cfar", [128, 6])
        small = sb("small", [128, SM_W])
        lnp = sb("lnp", [128, 2, D])
        cw = sb("cw", [128, 6])
        fcw = sb("fcw", [128, 66])
        ident = sb("ident", [128, 128], BF16)
        ones_f = sb("ones_f", [128, 64])
        xb = [sb("xb0", [128, D], BF16), sb("xb1", [128, D], BF16)]
        lnst = sb("lnst", [128, 4, 16])
        arena_t = sb("arena", [128, ARENA_W])
        A = Arena(arena_t, ARENA_W)
        pb = [es.enter_context(nc.psum_tensor(f"pb{i}", [128, 512], F32)) for i in range(8)]

        def pbb(i):
            return pb[i][:, :].bitcast(BF16)

        S.dma("sp", "c_cs", I("dma_start", out=cs[:].rearrange("p a t j -> p (a t j)"), in_=cs_d), writes=["cs"])
        S.dma("sp", "c_bt", I("dma_start", out=bt[:].rearrange("p h a q -> p (h a q)"), in_=bt_d), writes=["bt"])
        S.dma("sp", "c_cf", I("dma_start", out=cfar[:], in_=cfar_d), writes=["cfar"])
        S.dma("sp", "c_sm", I("dma_start", out=small[:], in_=small_d), writes=["small"])
        identf = A.f32(128)
        S.op("pool", I("memset", identf[:], 0.0), writes=["identf"])
        S.op("pool", I("affine_select", out=identf[:], in_=identf[:], pattern=[[-1, 128]],
                                              compare_op=ALU.not_equal, fill=1.0, base=0, channel_multiplier=1),
             reads=["identf"], writes=["identf"])
        S.op("dve", I("tensor_copy", out=ident[:], in_=identf[:]), reads=["identf"], writes=["ident"])
        S.op("pool", I("memset", ones_f[:], 1.0), writes=["ones_f"])
        for h in range(6):
            S.op("dve", I("tensor_scalar", out=bt[:, h, :, :], in0=bt[:, h, :, :], scalar1=cfar[:, h:h + 1], scalar2=None, op0=ALU.subtract),
                 reads=["bt", "cfar"], writes=["bt"])
        S.barrier()

        ownmask = small[:, SM_OWN:SM_OWN + 128].rearrange("p (t n) -> p t n", n=8)
        mask8 = small[:, SM_MASK:SM_MASK + 128]
        vsc1 = small[:, SM_VSC:SM_VSC + 6]
        xi = small[:, SM_XI:SM_XI + 6]
        zeta8 = small[:, SM_ZETA:SM_ZETA + 6]
        eps_t = small[:, SM_EPS:SM_EPS + 1]
        gct = small[:, SM_GC:SM_GC + 3]

        def bc3(ap2, n):
            a = ap2.shape[1]
            return ap2.unsqueeze(2).to_broadcast([128, a, n])

        def bcm(ap2, m):
            n = ap2.shape[1]
            return ap2.unsqueeze(1).to_broadcast([128, m, n])

        def wload(sem, dst3, scr2, r0, nrow_chunks, c0, c1, res):
            for k in range(nrow_chunks):
                S.dma("sp", sem, I("dma_start", out=dst3[:, k, :], in_=scr2[r0 + k * 128:r0 + (k + 1) * 128, c0:c1]), writes=[res])

        def emit_hT(t):
            b = t % 2
            S.op("act", I("activation", out=xb[b][:], in_=xs[:, t, :], func=AF.Copy),
                 reads=[("xs", t)], writes=[("xb", b)])
            p4 = pbb(4)
            for c in range(8):
                S.op("pe", I("transpose", out=p4[:, c * 128:(c + 1) * 128], in_=xb[b][:, c * 128:(c + 1) * 128],
                                                     identity=ident[:]),
                     reads=[("xb", b), "ident"], writes=[("ps", 4)])
            S.op("dve", I("tensor_copy", out=hT[:, :, t * 128:(t + 1) * 128],
                                                in_=p4.rearrange("p (c k) -> p c k", k=128)),
                 reads=[("ps", 4)], writes=[("hT", t)])

        def emit_ln_batch(tiles):
            n = len(tiles)
            for i, t in enumerate(tiles):
                S.op("dve", I("bn_stats", lnst[:, i, 0:6], xs[:, t, 0:512]), reads=[("xs", t)], writes=[("lnst0", i)])
                S.op("dve", I("bn_stats", lnst[:, i, 6:12], xs[:, t, 512:1024]), reads=[("xs", t)], writes=[("lnst1", i)])
                S.op("dve", I("bn_aggr", lnst[:, i, 12:14], lnst[:, i, 0:12]), reads=[("lnst0", i), ("lnst1", i)], writes=[("lnmv", i)])
            mvr = [("lnmv", i) for i in range(n)]
            S.op("act", I("activation", out=lnst[:, 0:n, 14], in_=lnst[:, 0:n, 13], func=AF.Ln, bias=eps_t, scale=1.0), reads=mvr, writes=["lnr"])
            S.op("act", I("activation", out=lnst[:, 0:n, 14], in_=lnst[:, 0:n, 14], func=AF.Exp, scale=-0.5), reads=["lnr"], writes=["lnr"])
            S.op("dve", I("scalar_tensor_tensor", out=lnst[:, 0:n, 15], in0=lnst[:, 0:n, 12], scalar=-1.0, in1=lnst[:, 0:n, 14], op0=ALU.mult, op1=ALU.mult),
                 reads=["lnr"] + mvr, writes=["lnn"])
            for i, t in enumerate(tiles):
                xt = xs[:, t, :]
                S.op("act", I("activation", out=xt, in_=xt, func=AF.Identity, bias=lnst[:, i, 15:16], scale=lnst[:, i, 14:15]),
                     reads=[("xs", t), "lnn", "lnr"], writes=[("xs", t)])
            for i, t in enumerate(tiles):
                xt = xs[:, t, :]
                S.op("dve", I("tensor_tensor", out=xt, in0=xt, in1=lnp[:, 0, :], op=ALU.mult), reads=[("xs", t), "lnp"], writes=[("xs", t)])
                S.op("dve", I("tensor_tensor", out=xt, in0=xt, in1=lnp[:, 1, :], op=ALU.add), reads=[("xs", t), "lnp"], writes=[("xs", t)])

        def load_lnp(l, gi):
            S.dma("sp", "lnp", I("dma_start", out=lnp[:, 0, :], in_=lnp_d[l, gi]),
                  writes=["lnp"])
            S.dma("sp", "lnp", I("dma_start", out=lnp[:, 1, :], in_=lnp_d[l, gi + 1]),
                  writes=["lnp"])

        def outproj_partial(t, lhs_fn, nk, wo3, first, lhs_reads, banks=(6, 7)):
            for hf in range(2):
                for j in range(nk):
                    S.op("pe", I("matmul", out=pb[banks[hf]][:, 0:512], lhsT=lhs_fn(j),
                                                               rhs=wo3[:, j, hf * 512:(hf + 1) * 512],
                                                               start=(j == 0), stop=(j == nk - 1)),
                         reads=list(lhs_reads) + ["wo"], writes=[("ps", banks[hf])])
            for hf in range(2):
                xsl = xs[:, t, hf * 512:(hf + 1) * 512]
                if first:
                    S.op("dve", I("scalar_tensor_tensor", out=xsl, in0=xsl, scalar=ALPHA, in1=pb[banks[hf]][:, 0:512],
                                                                                op0=ALU.mult, op1=ALU.add),
                         reads=[("ps", banks[hf]), ("xs", t)], writes=[("xs", t)])
                else:
                    S.op("dve", I("tensor_tensor", out=xsl, in0=xsl, in1=pb[banks[hf]][:, 0:512], op=ALU.add),
                         reads=[("ps", banks[hf]), ("xs", t)], writes=[("xs", t)])

        NSL, LEAD = 12, 6
        pf = [A.f32(768) for _ in range(NSL)]
        pbf = [A.bf16(768) for _ in range(NSL)]
        pieces = []
        for l in range(DEPTH):
            for src, dstb, nrows, ncols in ((w_in_d[l], win_b[l], D, IN_COLS), (w_out_d[l], wout_b[l], D, D)):
                for k in range(nrows // 128):
                    for p0 in range(0, ncols, 768):
                        n = min(768, ncols - p0)
                        rs = slice(k * 128, (k + 1) * 128)
                        pieces.append(([(0, n, src[rs, p0:p0 + n])], n, dstb[rs, p0:p0 + n]))
            for jp in range(11):
                for c in range(8):
                    rs = slice(c * 128, (c + 1) * 128)
                    pieces.append(([(0, 256, w_up_d[l][rs, jp * 256:(jp + 1) * 256]),
                                    (256, 256, w_up_d[l][rs, D_FF + jp * 256:D_FF + (jp + 1) * 256])], 512,
                                   wup_b[l, jp, :, c * 512:(c + 1) * 512]))
            for k in range(D_FF // 128):
                for p0 in range(0, D, 768):
                    n = min(768, D - p0)
                    rs = slice(k * 128, (k + 1) * 128)
                    pieces.append(([(0, n, w_dn_d[l][rs, p0:p0 + n])], n, wdn_b[l][rs, p0:p0 + n]))
        cast_eng = ["pool", "act", "dve"]
        npc = len(pieces)
        for i in range(npc + LEAD):
            if i < npc:
                sl = i % NSL
                for c0_, n_, src_ in pieces[i][0]:
                    S.dma("sp", f"pl{sl}", I("dma_start", out=pf[sl][:, c0_:c0_ + n_], in_=src_), writes=[("pf", sl)])
            j = i - LEAD
            if j >= 0:
                sl = j % NSL
                n_ = pieces[j][1]
                ce = cast_eng[j % 3]
                if ce == "act":
                    S.op("act", I("activation", out=pbf[sl][:, 0:n_], in_=pf[sl][:, 0:n_], func=AF.Copy), reads=[("pf", sl)], writes=[("pbf", sl)])
                else:
                    S.op(ce, I("tensor_copy", out=pbf[sl][:, 0:n_], in_=pf[sl][:, 0:n_]), reads=[("pf", sl)], writes=[("pbf", sl)])
                S.dma("sp", f"ps{sl}", I("dma_start", out=pieces[j][2], in_=pbf[sl][:, 0:n_]), reads=[("pbf", sl)])
        S.dma("sp", "pl0", I("dma_start", out=pf[0][64:72, 0:768], in_=ind_d[:, 0:768]), writes=[("pf", 0)])
        S.dma("sp", "pl1", I("dma_start", out=pf[1][64:72, 0:768], in_=ind_d[:, 768:1536]), writes=[("pf", 1)])
        S.dma("sp", "pl2", I("dma_start", out=pf[2][64:72, 0:512], in_=ind_d[:, 1536:2048]), writes=[("pf", 2)])
        for q_, (o_, n_) in enumerate(((0, 768), (768, 768), (1536, 512))):
            S.op("dve", I("tensor_copy", out=pbf[q_][64:72, 0:n_], in_=pf[q_][64:72, 0:n_]), reads=[("pf", q_)], writes=[("pbf", q_)])
            S.dma("sp", f"ps{q_}", I("dma_start", out=ind_b[:, o_:o_ + n_], in_=pbf[q_][64:72, 0:n_]), reads=[("pbf", q_)])
        S.barrier()

        try:
            for s in range(nseq):
                for t in range(NT):
                    S.dma("sp", f"xload{t}", I("dma_start", out=xs[:, t, :], in_=x_d[s, t * 128:(t + 1) * 128, :]),
                          writes=[("xs", t)])
                for t in range(NT):
                    emit_hT(t)
                if stop == "hT":
                    raise _Stop()
                for l in range(depth):
                    w_in = win_b[l]; w_out = wout_b[l]; w_dn = wdn_b[l]
                    S.barrier(); A.reset()
                    wR3 = A.bf16(8 * 1536).rearrange("p (c n) -> p c n", n=1536)
                    woR3 = A.bf16(3 * 1024).rearrange("p (c n) -> p c n", n=1024)
                    qrot = A.bf16(384)
                    krot = [A.bf16(384) for _ in range(2)]
                    vs1 = [A.bf16(384).rearrange("p (h d) -> p h d", d=64) for _ in range(2)]
                    vz = [A.bf16(384).rearrange("p (h d) -> p h d", d=64) for _ in range(2)]
                    qkT = [A.bf16(768) for _ in range(2)]
                    sg = [A.f32(384) for _ in range(2)]
                    scT = A.bf16(768).rearrange("p (par j c) -> p par j c", par=2, c=128)
                    ycat = [A.bf16(384) for _ in range(2)]; catT = A.bf16(384)
                    stb = [A.bf16(192).rearrange("p (j v) -> p j v", v=64) for _ in range(2)]
                    stf = A.f32(192).rearrange("p (j v) -> p j v", v=64)
                    tq = [A.f32(384).rearrange("p (h d) -> p h d", d=64) for _ in range(4)]
                    eg = A.f32(384)
                    yr = A.f32(384).rearrange("p (h d) -> p h d", d=64)
                    sq = A.f32(384).rearrange("p (h d) -> p h d", d=64)
                    st6 = A.f32(48)
                    s1 = st6[:, 0:6]; s2 = st6[:, 8:14]; mean = st6[:, 16:22]; msq = st6[:, 24:30]; var = st6[:, 32:38]; rstd = st6[:, 40:46]
                    wload("wR", wR3, w_in, 0, 8, 0, 1536, "wR")
                    wload("wo", woR3, w_out, 0, 3, 0, 1024, "wo")
                    S.op("dve", I("memset", stf, 0.0), writes=["stf"] + [("stf", h) for h in range(6)])
                    S.op("dve", I("memset", stb[0], 0.0), writes=[("stb", 0)])

                    def r_proj(t):
                        tsl = slice(t * 128, (t + 1) * 128)
                        for c in range(8):
                            for g4 in range(4):
                                S.op("pe", I("matmul", out=pb[g4][:, 0:384], lhsT=hT[:, c, tsl], rhs=wR3[:, c, g4 * 384:(g4 + 1) * 384],
                                             start=(c == 0), stop=(c == 7)),
                                     reads=[("hT", t), "wR"], writes=[("ps", g4)])

                    def r_front_ew(t):
                        p = t % 2
                        cos2 = bcm(cs[:, 0, t, :], 6)
                        nsin = bcm(cs[:, 1, t, 0:32], 6); psin = bcm(cs[:, 1, t, 32:64], 6)
                        for bank, dst, nm, o in ((0, qrot, "qrot", 0), (1, krot[p], ("krot", p), 2)):
                            pq = pb[bank][:, 0:384].rearrange("p (h d) -> p h d", d=64)
                            ra = tq[o]; rb = tq[o + 1]
                            S.op("dve", I("tensor_tensor", out=ra, in0=pq, in1=cos2, op=ALU.mult), reads=[("ps", bank), "cs"], writes=[("tq", o)])
                            S.op("dve", I("tensor_tensor", out=rb[:, :, 0:32], in0=pq[:, :, 32:64], in1=nsin, op=ALU.mult), reads=[("ps", bank), "cs"], writes=[("tq", o + 1, 0)])
                            S.op("dve", I("tensor_tensor", out=rb[:, :, 32:64], in0=pq[:, :, 0:32], in1=psin, op=ALU.mult), reads=[("ps", bank), "cs"], writes=[("tq", o + 1, 1)])
                            S.op("dve", I("tensor_tensor", out=dst.rearrange("p (h d) -> p h d", d=64), in0=ra, in1=rb, op=ALU.add),
                                 reads=[("tq", o), ("tq", o + 1, 0), ("tq", o + 1, 1)], writes=[(nm, 1), (nm, 2)])
                        pv = pb[2][:, 0:384].rearrange("p (h d) -> p h d", d=64)
                        S.op("dve", I("tensor_tensor", out=vs1[p], in0=pv, in1=bc3(vsc1, 64), op=ALU.mult), reads=[("ps", 2), "small"], writes=[("vs1", p)])
                        S.op("dve", I("tensor_tensor", out=vz[p], in0=pv, in1=bc3(zeta8, 64), op=ALU.mult), reads=[("ps", 2), "small"], writes=[("vz", p)])
                        S.op("act", I("activation", out=eg, in_=pb[3][:, 0:384], func=AF.Exp, scale=-1.0), reads=[("ps", 3)], writes=["eg"])
                        S.op("act", I("activation", out=eg, in_=eg, func=AF.Identity, bias=ones_f[:, 0:1], scale=1.0), reads=["eg", "ones_f"], writes=["eg"])
                        S.op("dve", I("reciprocal", out=eg, in_=eg), reads=["eg"], writes=["eg"])
                        S.op("dve", I("tensor_tensor", out=sg[p], in0=pb[3][:, 0:384], in1=eg, op=ALU.mult), reads=[("ps", 3), "eg"], writes=[("sg", p)])

                    def r_front_tr(t):
                        p = t % 2
                        p4 = pbb(4)
                        for j in range(3):
                            S.op("pe", I("transpose", out=p4[:, j * 128:(j + 1) * 128], in_=qrot[:, j * 128:(j + 1) * 128], identity=ident[:]),
                                 reads=[("qrot", 1), ("qrot", 2), "ident"], writes=[("ps", 4)])
                            S.op("pe", I("transpose", out=p4[:, (3 + j) * 128:(4 + j) * 128], in_=krot[p][:, j * 128:(j + 1) * 128], identity=ident[:]),
                                 reads=[(("krot", p), 1), (("krot", p), 2), "ident"], writes=[("ps", 4)])
                        S.op("act", I("activation", out=qkT[p], in_=p4[:, 0:768], func=AF.Copy), reads=[("ps", 4)], writes=[("qkT", p)])

                    def r_scores(t):
                        p = t % 2
                        for h in range(6):
                            j, hf = h // 2, h % 2
                            pr = slice(hf * 64, hf * 64 + 64)
                            S.op("pe", I("matmul", out=pb[5 + hf][:, j * 128:(j + 1) * 128], lhsT=qkT[p][pr, (3 + j) * 128:(4 + j) * 128],
                                         rhs=qkT[p][pr, j * 128:(j + 1) * 128], start=True, stop=True),
                                 reads=[("qkT", p)], writes=[("ps", 5 + hf)])

                    def r_mask(t):
                        for hf in range(2):
                            S.op("dve", I("tensor_tensor", out=scT[:, hf, :, :], in0=pb[5 + hf][:, 0:384].rearrange("p (h c) -> p h c", c=128),
                                          in1=bcm(mask8, 3), op=ALU.mult),
                                 reads=[("ps", 5 + hf), "small"], writes=[("scT", hf)])

                    def r_o_ds(t):
                        p = t % 2
                        sb_cur = stb[t % 2]
                        for h in range(6):
                            j, hf = h // 2, h % 2
                            pr = slice(hf * 64, hf * 64 + 64)
                            S.op("pe", I("matmul", out=pb[5][:, h * 64:(h + 1) * 64], lhsT=scT[:, hf, j, :], rhs=vs1[p][:, h, :], start=True, stop=False),
                                 reads=[("scT", 0), ("scT", 1), ("vs1", p)], writes=[("ps", 5)])
                            S.op("pe", I("matmul", out=pb[5][:, h * 64:(h + 1) * 64], lhsT=qkT[p][pr, j * 128:(j + 1) * 128],
                                         rhs=sb_cur[pr, j, :], start=False, stop=True),
                                 reads=[("qkT", p), ("stb", t % 2)], writes=[("ps", 5)])
                        for h in range(6):
                            j, hf = h // 2, h % 2
                            S.op("pe", I("matmul", out=pb[7][hf * 64:hf * 64 + 64, j * 64:(j + 1) * 64], lhsT=krot[p][:, h * 64:(h + 1) * 64], rhs=vz[p][:, h, :],
                                         start=True, stop=True),
                                 reads=[(("krot", p), 1), (("krot", p), 2), ("vz", p)], writes=[("ps", 7)])

                    def r_state_norm(t):
                        p = t % 2
                        sb_nxt = stb[(t + 1) % 2]
                        for j in range(3):
                            S.op("dve", I("scalar_tensor_tensor", out=stf[:, j, :], in0=stf[:, j, :], scalar=gct[:, j:j + 1],
                                          in1=pb[7][:, j * 64:(j + 1) * 64], op0=ALU.mult, op1=ALU.add),
                                 reads=[("ps", 7), ("stf", 2 * j), "small"], writes=[("stf", 2 * j), ("stf", 2 * j + 1)])
                        S.op("act", I("activation", out=sb_nxt, in_=stf, func=AF.Copy),
                             reads=[("stf", h) for h in range(6)], writes=[("stb", (t + 1) % 2)])
                        po = pb[5][:, 0:384].rearrange("p (h d) -> p h d", d=64)
                        S.op("dve", I("tensor_tensor", out=yr, in0=po, in1=bc3(xi, 64), op=ALU.mult), reads=[("ps", 5), "small"], writes=["yr"])
                        S.op("dve", I("tensor_reduce", out=s1, in_=yr, axis=AX.X, op=ALU.add), reads=["yr"], writes=["s1"])
                        S.op("dve", I("tensor_tensor", out=sq, in0=yr, in1=yr, op=ALU.mult), reads=["yr"], writes=["sq"])
                        S.op("dve", I("tensor_reduce", out=s2, in_=sq, axis=AX.X, op=ALU.add), reads=["sq"], writes=["s2"])
                        S.op("dve", I("tensor_scalar", out=mean, in0=s1, scalar1=1.0 / 64, scalar2=None, op0=ALU.mult), reads=["s1"], writes=["mean"])
                        S.op("dve", I("tensor_tensor", out=msq, in0=mean, in1=mean, op=ALU.mult), reads=["mean"], writes=["msq"])
                        S.op("dve", I("tensor_scalar", out=var, in0=s2, scalar1=1.0 / 64, scalar2=None, op0=ALU.mult), reads=["s2"], writes=["var"])
                        S.op("dve", I("tensor_tensor", out=var, in0=var, in1=msq, op=ALU.subtract), reads=["var", "msq"], writes=["var"])
                        S.op("dve", I("tensor_scalar", out=var, in0=var, scalar1=0.0, scalar2=None, op0=ALU.max), reads=["var"], writes=["var"])
                        S.op("act", I("activation", out=rstd, in_=var, func=AF.Ln, bias=eps_t, scale=1.0), reads=["var", "small"], writes=["rstd"])
                        S.op("act", I("activation", out=rstd, in_=rstd, func=AF.Exp, scale=-0.5), reads=["rstd"], writes=["rstd"])
                        S.op("dve", I("tensor_tensor", out=yr, in0=yr, in1=bc3(mean, 64), op=ALU.subtract), reads=["yr", "mean", "s1", "sq"], writes=["yr"])
                        S.op("dve", I("tensor_tensor", out=yr, in0=yr, in1=bc3(rstd, 64), op=ALU.mult), reads=["yr", "rstd"], writes=["yr"])
                        S.op("dve", I("tensor_tensor", out=ycat[p], in0=yr.rearrange("p h d -> p (h d)"), in1=sg[p], op=ALU.mult), reads=["yr", ("sg", p)], writes=[("ycat", p)])

                    def r_out(t):
                        p4 = pbb(4)
                        for j in range(3):
                            S.op("pe", I("transpose", out=p4[:, j * 128:(j + 1) * 128], in_=ycat[t % 2][:, j * 128:(j + 1) * 128], identity=ident[:]),
                                 reads=[("ycat", t % 2), "ident"], writes=[("ps", 4)])
                        S.op("act", I("activation", out=catT, in_=p4[:, 0:384], func=AF.Copy), reads=[("ps", 4)], writes=["catT"])
                        outproj_partial(t, lambda j: catT[:, j * 128:(j + 1) * 128], 3, woR3, True, ["catT"])

                    for t in range(NT + 2):
                        back = 1 <= t <= NT
                        if t < NT:
                            r_proj(t)
                        if back:
                            r_scores(t - 1)
                        if t < NT:
                            r_front_ew(t)
                        if back:
                            r_mask(t - 1)
                        if t < NT:
                            r_front_tr(t)
                        if t >= 2:
                            r_out(t - 2)
                        if back:
                            r_o_ds(t - 1)
                            r_state_norm(t - 1)

                    if stop == "R":
                        raise _Stop()
                    S.barrier(); A.reset()
                    wM3 = A.bf16(8 * 1152).rearrange("p (c n) -> p c n", n=1152)
                    woM3 = A.bf16(3 * 1024).rearrange("p (c n) -> p c n", n=1024)
                    KT = A.bf16(6 * SEQ).rearrange("p (h k) -> p h k", k=SEQ)
                    QT = A.bf16(6 * 512).rearrange("p (h k) -> p h k", k=512)
                    Va = A.bf16(NT * 6 * 65 + 2).rearrange("p (t h d) -> p t h d", h=6, d=65) if False else None
                    Vraw = A.bf16(NT * 6 * 66)
                    Va = Vraw.rearrange("p (t h d) -> p t h d", h=6, d=66)
                    NPT = 4
                    PT = [A.bf16(512) for _ in range(NPT)]
                    OTs = A.f32(512); rec = OTs
                    catM = A.bf16(3 * 512).rearrange("p (j k) -> p j k", k=512)
                    NM = [A.bf16(6 * 72).rearrange("p (h n) -> p h n", n=72) for _ in range(4)]
                    gsb = [A.f32(48).rearrange("p (h n) -> p h n", n=8) for _ in range(4)]
                    m8 = [A.f32(48).rearrange("p (h n) -> p h n", n=8) for _ in range(4)]
                    ge = [A.f32(48).rearrange("p (h n) -> p h n", n=8) for _ in range(4)]
                    km = A.f32(48).rearrange("p (h n) -> p h n", n=8)
                    kmT = A.bf16(48).rearrange("p (h n) -> p h n", n=8)
                    wload("wM", wM3, w_in, 0, 8, 1536, 2688, "wM")
                    wload("wo", woM3, w_out, 384, 3, 0, 1024, "wo")
                    for h in range(6):
                        S.dma("sp", f"ind{h}", I("dma_start", out=KT[64:72, h, :], in_=ind_b), writes=[("KTi", h)])
                    S.op("dve", I("memset", Vraw, 1.0), writes=["Va1"])
                    for i in range(4):
                        S.op("dve", I("memset", NM[i], 0.0), writes=[("NM", i)])
                    S.op("dve", I("memset", kmT, 0.0), writes=["kmT"])
                    S.op("dve", I("memset", km, 0.0), writes=["km"])
                    sidx = [0]; pidx = [0]; oidx = [0]
                    for G in range(4):
                        gsl = slice(G * 512, (G + 1) * 512)
                        hTr = [("hT", 4 * G + i) for i in range(4)]
                        for h in range(6):
                            for which in range(2):
                                bank = which
                                c0 = which * 384 + h * 64
                                for c in range(8):
                                    S.op("pe", I("matmul", out=pb[bank][0:64, 0:512], lhsT=wM3[:, c, c0:c0 + 64], rhs=hT[:, c, gsl],
                                                                                         start=(c == 0), stop=(c == 7)),
                                         reads=hTr + ["wM"], writes=[("ps", bank)])
                                if which == 0:
                                    S.op("act", I("activation", out=QT[0:64, h, :], in_=pb[0][0:64, 0:512], func=AF.Identity, scale=0.125),
                                         reads=[("ps", 0)], writes=[("QT", h)])
                                else:
                                    S.op("dve", I("tensor_copy", out=KT[0:64, h, gsl], in_=pb[1][0:64, 0:512]),
                                         reads=[("ps", 1)], writes=[("KT", h, G)])
                        for i in range(4):
                            tt = 4 * G + i
                            bank = 2 + (i % 2)
                            for c in range(8):
                                S.op("pe", I("matmul", out=pb[bank][:, 0:384], lhsT=hT[:, c, tt * 128:(tt + 1) * 128], rhs=wM3[:, c, 768:1152],
                                                                                     start=(c == 0), stop=(c == 7)),
                                     reads=[("hT", tt), "wM"], writes=[("ps", bank)])
                            S.op("act", I("activation", out=Va[:, tt, :, 0:64], in_=pb[bank][:, 0:384].rearrange("p (h d) -> p h d", d=64), func=AF.Copy),
                                 reads=[("ps", bank), "Va1"], writes=[("Va", tt)])
                        for h in range(6):
                            S.op("dve", I("tensor_reduce", out=km[0:64, h, 2 * G:2 * G + 2], in_=KT[0:64, h, gsl].rearrange("p (n k) -> p n k", k=256),
                                                                      axis=AX.X, op=ALU.add),
                                 reads=[("KT", h, G), "km"], writes=[("km", h)])
                        S.op("dve", I("tensor_scalar", out=kmT[0:64, :, 2 * G:2 * G + 2], in0=km[0:64, :, 2 * G:2 * G + 2], scalar1=1.0 / 256, scalar2=None, op0=ALU.mult),
                             reads=[("km", h) for h in range(6)] + ["kmT"], writes=["kmT"])
                        for i in range(4):
                            for h in range(6):
                                S.op("pe", I("matmul", out=pb[5][:, i * 48 + h * 8:i * 48 + (h + 1) * 8], lhsT=QT[0:64, h, i * 128:(i + 1) * 128], rhs=kmT[0:64, h, :],
                                             start=True, stop=True),
                                     reads=[("QT", h), "kmT"], writes=[("ps", 5)])
                        for i in range(4):
                            qt = 4 * G + i
                            nb = NM[i]
                            S.op("dve", I("tensor_tensor", out=gsb[i], in0=pb[5][:, i * 48:(i + 1) * 48].rearrange("p (h n) -> p h n", n=8), in1=bcm(ownmask[:, qt, :], 6), op=ALU.add),
                                 reads=[("ps", 5), "small"], writes=[("gsb", i)])
                            for h in range(6):
                                S.op("dve", I("max", out=m8[i][:, h, :], in_=gsb[i][:, h, :]), reads=[("gsb", i)], writes=[("m8", i, h)])
                            S.op("dve", I("tensor_tensor", out=ge[i], in0=gsb[i], in1=m8[i][:, :, 3:4].to_broadcast([128, 6, 8]), op=ALU.is_ge),
                                 reads=[("gsb", i)] + [("m8", i, h) for h in range(6)], writes=[("ge", i)])
                            S.op("dve", I("tensor_scalar", out=nb[:, :, 64:72], in0=ge[i], scalar1=-1.0, scalar2=-NEG, op0=ALU.add, op1=ALU.mult),
                                 reads=[("ge", i), ("NM", i)], writes=[("NM", i)])
                        for i in range(4):
                            nb = NM[i]
                            p4 = pbb(4)
                            for h in range(6):
                                S.op("pe", I("transpose", out=p4[0:72, h * 128:(h + 1) * 128], in_=nb[:, h, :], identity=ident[:]),
                                     reads=[("NM", i), "ident"], writes=[("ps", 4)])
                            S.op("act", I("activation", out=QT[64:72, :, i * 128:(i + 1) * 128], in_=p4[64:72, 0:768].rearrange("p (h k) -> p h k", k=128), func=AF.Copy),
                                 reads=[("ps", 4)] + [("QT", h) for h in range(6)], writes=[("QTn", i)])
                        nkt = 4 * G + 4
                        seq = [(h, kt) for h in range(6) for kt in range(nkt)]
                        LA = 3
                        slot = {}
                        pend = []
                        obs = {}
                        for h in range(6):
                            obs[h] = 6 + (oidx[0] % 2); oidx[0] += 1
                        for idx in range(len(seq) + LA):
                            if idx < len(seq):
                                h, kt = seq[idx]
                                rel = kt - 4 * G
                                c0 = max(0, rel) * 128
                                sbk = sidx[0] % 4; sidx[0] += 1
                                ptb = pidx[0] % NPT; pidx[0] += 1
                                slot[idx] = ptb
                                Pt = PT[ptb]
                                S.op("pe", I("matmul", out=pb[sbk][:, c0:512], lhsT=KT[0:72, h, kt * 128:(kt + 1) * 128], rhs=QT[0:72, h, c0:512],
                                             start=True, stop=True),
                                     reads=[("KT", h, kt // 4), ("KTi", h), ("QT", h)] + [("QTn", i) for i in range(4)], writes=[("ps", sbk)])
                                if -1 <= rel <= 3:
                                    if rel == -1:
                                        a0, a1, bsl = 0, 128, bt[:, h, 1, :]
                                    elif rel == 3:
                                        a0, a1, bsl = 384, 512, bt[:, h, 0, :]
                                    else:
                                        a0, a1, bsl = rel * 128, rel * 128 + 256, bt[:, h, :, :].rearrange("p a q -> p (a q)")
                                    S.op("dve", I("tensor_tensor", out=pb[sbk][:, a0:a1], in0=pb[sbk][:, a0:a1], in1=bsl, op=ALU.add),
                                         reads=[("ps", sbk), "bt"], writes=[("ps", sbk)])
                                S.op("act", I("activation", out=Pt[:, c0:512], in_=pb[sbk][:, c0:512], func=AF.Exp, bias=cfar[:, h:h + 1], scale=1.0),
                                     reads=[("ps", sbk), "cfar", ("PT", ptb)], writes=[("PTf", ptb)])
                            jdx = idx - LA
                            if jdx >= 0:
                                h, kt = seq[jdx]
                                j, hf = h // 2, h % 2
                                ob = obs[h]
                                rel = kt - 4 * G
                                c0 = max(0, rel) * 128
                                ptb = slot[jdx]
                                Pt = PT[ptb]
                                S.op("pe", I("matmul", out=pb[ob][0:65, c0:512], lhsT=Va[:, kt, h, 0:65], rhs=Pt[:, c0:512],
                                             start=(kt == 0), stop=(kt == nkt - 1)),
                                     reads=[("Va", kt), "Va1", ("PTf", ptb)], writes=[("ps", ob), ("PT", ptb)])
                                if kt == nkt - 1:
                                    S.op("dve", I("reciprocal", out=rec[64:65, :], in_=pb[ob][64:65, 0:512]), reads=[("ps", ob)], writes=["rec"])
                                    pend.append((idx + 2, h, ob))
                            while pend and (pend[0][0] <= idx or idx == len(seq) + LA - 1):
                                _, h, ob = pend.pop(0)
                                j, hf = h // 2, h % 2
                                S.op("pe", I("matmul", out=pb[5][0:64, 0:512], lhsT=ones_f[64:65, 0:64], rhs=rec[64:65, :], start=True, stop=True),
                                     reads=["rec", "ones_f"], writes=[("ps", 5)])
                                S.op("act", I("activation", out=OTs[0:64, :], in_=pb[ob][0:64, 0:512], func=AF.Copy), reads=[("ps", ob)], writes=["OTs"])
                                S.op("dve", I("tensor_tensor", out=catM[hf * 64:hf * 64 + 64, j, :], in0=OTs[0:64, :], in1=pb[5][0:64, 0:512], op=ALU.mult),
                                     reads=["OTs", ("ps", 5)], writes=[("catM", h)])
                        for i in range(4):
                            tt = 4 * G + i
                            outproj_partial(tt, lambda j, i=i: catM[:, j, i * 128:(i + 1) * 128], 3, woM3, False, [("catM", h) for h in range(6)],
                                            banks=((6, 7), (0, 1), (2, 3))[i % 3])

                    if stop == "M":
                        raise _Stop()
                    S.barrier(); A.reset()
                    wC3 = A.bf16(8 * 768).rearrange("p (c n) -> p c n", n=768)
                    woC3 = A.bf16(2 * 1024).rearrange("p (c n) -> p c n", n=1024)
                    catC = A.bf16(2 * SEQ).rearrange("p (j k) -> p j k", k=SEQ)
                    pbuf = [A.f32(514) for _ in range(2)]
                    ccs = A.f32(512); acc = A.f32(512)
                    wload("wC", wC3, w_in, 0, 8, 2688, 3456, "wC")
                    wload("wo", woC3, w_out, 768, 2, 0, 1024, "wo")
                    S.dma("sp", "cw", I("dma_start", out=cw[:], in_=cw_d[l]), writes=["cw"])
                    load_lnp(l, 0)
                    for j in range(2):
                        S.op("dve", I("memset", pbuf[j][:, 0:2], 0.0), writes=[("pbuf", j)])
                    for G in range(4):
                        gsl = slice(G * 512, (G + 1) * 512)
                        hTr = [("hT", 4 * G + i) for i in range(4)]
                        for j in range(2):
                            for part in range(3):
                                c0 = part * 256 + j * 128
                                for c in range(8):
                                    S.op("pe", I("matmul", out=pb[part][:, 0:512], lhsT=wC3[:, c, c0:c0 + 128], rhs=hT[:, c, gsl],
                                                                                         start=(c == 0), stop=(c == 7)),
                                         reads=hTr + ["wC"], writes=[("ps", part)])
                            pbj = pbuf[j]
                            S.op("act", I("activation", out=ccs, in_=pb[1][:, 0:512], func=AF.Copy), reads=[("ps", 1)], writes=["ccs"])
                            S.op("dve", I("tensor_tensor", out=pbj[:, 2:514], in0=pb[2][:, 0:512], in1=ccs, op=ALU.mult),
                                 reads=[("ps", 2), "ccs", ("pbuf", j)], writes=[("pbufm", j)])
                            S.op("dve", I("tensor_scalar", out=acc, in0=pbj[:, 2:514], scalar1=cw[:, j * 3 + 2:j * 3 + 3], scalar2=None, op0=ALU.mult),
                                 reads=[("pbufm", j), "cw"], writes=["acc"])
                            S.op("dve", I("scalar_tensor_tensor", out=acc, in0=pbj[:, 1:513], scalar=cw[:, j * 3 + 1:j * 3 + 2], in1=acc, op0=ALU.mult, op1=ALU.add),
                                 reads=[("pbufm", j), ("pbuf", j), "acc", "cw"], writes=["acc"])
                            S.op("dve", I("scalar_tensor_tensor", out=acc, in0=pbj[:, 0:512], scalar=cw[:, j * 3:j * 3 + 1], in1=acc, op0=ALU.mult, op1=ALU.add),
                                 reads=[("pbufm", j), ("pbuf", j), "acc", "cw"], writes=["acc"])
                            S.op("dve", I("tensor_tensor", out=catC[:, j, gsl], in0=pb[0][:, 0:512], in1=acc, op=ALU.mult),
                                 reads=[("ps", 0), "acc"], writes=[("catC", j, G)])
                            S.op("dve", I("tensor_copy", out=pbj[:, 0:2], in_=pbj[:, 512:514]),
                                 reads=[("pbufm", j)], writes=[("pbuf", j)])
                        for i in range(4):
                            tt = 4 * G + i
                            outproj_partial(tt, lambda j, tt=tt: catC[:, j, tt * 128:(tt + 1) * 128], 2, woC3, False, [("catC", 0, G), ("catC", 1, G)],
                                            banks=((6, 7), (3, 5))[i % 2])
                        emit_ln_batch([4 * G + i for i in range(4)])
                        for i in range(4):
                            emit_hT(4 * G + i)

                    if stop == "C":
                        raise _Stop()
                    S.barrier(); A.reset()
                    Gp = A.bf16(8 * SEQ).rearrange("p (j k) -> p j k", k=SEQ)
                    wD3 = A.bf16(8 * 1024).rearrange("p (j n) -> p j n", n=1024)
                    wUflat = [A.bf16(8 * 512) for _ in range(2)]
                    wU = [w_.rearrange("p (c n) -> p c n", n=512) for w_ in wUflat]
                    abuf = [A.f32(514) for _ in range(2)]
                    accs = [A.f32(512) for _ in range(2)]
                    gls = [A.f32(512) for _ in range(2)]
                    S.dma("sp", "fcw", I("dma_start", out=fcw[:], in_=fcw_d[l]), writes=["fcw"])
                    load_lnp(l, 2)
                    parts = [(0, 8), (8, 16), (16, 22)]
                    upc = [0]; itc = [0]; bai = [0]; bbi = [0]; dpi = [0]
                    for pi, (j0, j1) in enumerate(parts):
                        nj = j1 - j0
                        wload("wD", wD3, w_dn, j0 * 128, nj, 0, 1024, "wD")
                        its = []
                        for jp in range(j0 // 2, j1 // 2):
                            wb = upc[0] % 2; upc[0] += 1
                            for sub in range(2):
                                for G in range(4):
                                    its.append((jp, wb, sub, G))

                        def f_stage1(i):
                            jp, wb, sub, G = its[i]
                            wu = wU[wb]
                            if sub == 0 and G == 0:
                                nxt = [(jp2, wb2) for (jp2, wb2, s2, g2) in its if jp2 == jp + 1 and s2 == 0 and g2 == 0]
                                if nxt:
                                    S.dma("sp", f"wU{nxt[0][1]}", I("dma_start", out=wUflat[nxt[0][1]], in_=wup_b[l, nxt[0][0]]), writes=[("wU", nxt[0][1])])
                            jg = 2 * jp + sub
                            st = itc[0] % 2; itc[0] += 1
                            fset[i] = st
                            ab = abuf[st]; ac = accs[st]
                            gsl = slice(G * 512, (G + 1) * 512)
                            hTr = [("hT", 4 * G + q) for q in range(4)]
                            ba = bai[0] % 2; bai[0] += 1
                            bb = (2, 3, 5)[bbi[0] % 3]; bbi[0] += 1
                            fbb[i] = bb
                            for c in range(8):
                                S.op("pe", I("matmul", out=pb[ba][:, 0:512], lhsT=wu[:, c, sub * 128:(sub + 1) * 128], rhs=hT[:, c, gsl],
                                             start=(c == 0), stop=(c == 7)),
                                     reads=hTr + [("wU", wb)], writes=[("ps", ba)])
                            for c in range(8):
                                S.op("pe", I("matmul", out=pb[bb][:, 0:512], lhsT=wu[:, c, 256 + sub * 128:256 + (sub + 1) * 128], rhs=hT[:, c, gsl],
                                             start=(c == 0), stop=(c == 7)),
                                     reads=hTr + [("wU", wb)], writes=[("ps", bb)])
                            if G == 0:
                                S.op("pool", I("memset", ab[:, 0:2], 0.0), writes=[("abh", st)])
                            else:
                                S.op("pool", I("tensor_copy", out=ab[:, 0:2], in_=abuf[1 - st][:, 512:514]), reads=[("abm", 1 - st)], writes=[("abh", st)])
                            S.op("act", I("activation", out=ab[:, 2:514], in_=pb[ba][:, 0:512], func=AF.Copy),
                                 reads=[("ps", ba)], writes=[("abm", st)])
                            k3 = jg * 3
                            S.op("act", I("activation", out=ac, in_=pb[ba][:, 0:512], func=AF.Identity, scale=fcw[:, k3 + 2:k3 + 3]),
                                 reads=[("ps", ba), "fcw"], writes=[("acc", st)])
                            S.op("dve", I("scalar_tensor_tensor", out=ac, in0=ab[:, 1:513], scalar=fcw[:, k3 + 1:k3 + 2], in1=ac, op0=ALU.mult, op1=ALU.add),
                                 reads=[("abm", st), ("abh", st), ("acc", st), "fcw"], writes=[("acc", st)])
                            S.op("dve", I("scalar_tensor_tensor", out=ac, in0=ab[:, 0:512], scalar=fcw[:, k3:k3 + 1], in1=ac, op0=ALU.mult, op1=ALU.add),
                                 reads=[("abm", st), ("abh", st), ("acc", st), "fcw"], writes=[("acc", st)])

                        def f_stage2(i):
                            jp, wb, sub, G = its[i]
                            st = fset[i]; bb = fbb[i]
                            jj = 2 * jp + sub - j0
                            gsl = slice(G * 512, (G + 1) * 512)
                            S.op("act", I("activation", out=gls[st], in_=accs[st], func=AF.Gelu), reads=[("acc", st)], writes=[("gl", st)])
                            S.op("dve", I("tensor_tensor", out=Gp[:, jj, gsl], in0=pb[bb][:, 0:512], in1=gls[st], op=ALU.mult),
                                 reads=[("ps", bb), ("gl", st)], writes=[("Gp", jj, G)])

                        fset = {}; fbb = {}
                        S.dma("sp", f"wU{its[0][1]}", I("dma_start", out=wUflat[its[0][1]], in_=wup_b[l, its[0][0]]), writes=[("wU", its[0][1])])
                        for i in range(len(its) + 1):
                            if i < len(its):
                                f_stage1(i)
                            if i >= 1:
                                f_stage2(i - 1)
                        last = (pi == len(parts) - 1)
                        for t in range(NT):
                            G = t // 4
                            outproj_partial_reads = [("Gp", jj, G) for jj in range(nj)]
                            dbk = ((6, 7), (0, 1), (2, 3))[dpi[0] % 3]; dpi[0] += 1
                            for hf in range(2):
                                for jj in range(nj):
                                    S.op("pe", I("matmul", out=pb[dbk[hf]][:, 0:512], lhsT=Gp[:, jj, t * 128:(t + 1) * 128], rhs=wD3[:, jj, hf * 512:(hf + 1) * 512],
                                                 start=(jj == 0), stop=(jj == nj - 1)),
                                         reads=outproj_partial_reads + ["wD"], writes=[("ps", dbk[hf])])
                            for hf in range(2):
                                xsl = xs[:, t, hf * 512:(hf + 1) * 512]
                                if pi == 0:
                                    S.op("dve", I("scalar_tensor_tensor", out=xsl, in0=xsl, scalar=ALPHA, in1=pb[dbk[hf]][:, 0:512], op0=ALU.mult, op1=ALU.add),
                                         reads=[("ps", dbk[hf]), ("xs", t)], writes=[("xs", t)])
                                else:
                                    S.op("dve", I("tensor_tensor", out=xsl, in0=xsl, in1=pb[dbk[hf]][:, 0:512], op=ALU.add),
                                         reads=[("ps", dbk[hf]), ("xs", t)], writes=[("xs", t)])
                            if last and t % 4 == 3:
                                tl = [t - 3, t - 2, t - 1, t]
                                emit_ln_batch(tl)
                                for t2 in tl:
                                    if l == depth - 1:
                                        S.dma("sp", "ostore", I("dma_start", out=out_d[s, t2 * 128:(t2 + 1) * 128, :], in_=xs[:, t2, :]),
                                              reads=[("xs", t2)])
                                    else:
                                        emit_hT(t2)
                S.barrier()

        except _Stop:
            S.barrier()
            for t in range(NT):
                S.dma("sp", "ostore", I("dma_start", out=out_d[0, t * 128:(t + 1) * 128, :], in_=xs[:, t, :]), reads=[("xs", t)])
        S.barrier()
        S.emit()
    return nc


_NC_CACHE = {}


def _get_nc(nseq, depth):
    key = (nseq, depth)
    if key not in _NC_CACHE:
        _NC_CACHE[key] = build(nseq, depth)
    return _NC_CACHE[key]


def host_inputs(x_shard, w_in, conv_w, w_out, ln1_g, ln1_b, w_up, ffn_conv_w, w_down, ln2_g, ln2_b, rel_bias):
    cs_h, small_h, gC, ind_h, bidx_d, bidx_s, causal = _host_consts()
    f = lambda a: np.ascontiguousarray(a, dtype=np.float32)
    lnp = np.broadcast_to(np.stack([ln1_g, ln1_b, ln2_g, ln2_b], axis=1)[:, :, None, :], (DEPTH, 4, 128, D))
    cw = conv_w.reshape(DEPTH, 3, 2, 128).transpose(0, 3, 2, 1).reshape(DEPTH, 128, 6)
    fcw = ffn_conv_w.reshape(DEPTH, 3, 22, 128).transpose(0, 3, 2, 1).reshape(DEPTH, 128, 66)
    bd = np.where(causal[:, :, None], rel_bias[bidx_d], np.float32(NEG))
    bs = rel_bias[bidx_s]
    btile = np.stack([bd, bs], axis=0).transpose(1, 3, 0, 2).reshape(128, 6 * 2 * 128)
    cfar = np.broadcast_to(rel_bias[31][None, :], (128, 6))
    return {
        "x": f(x_shard), "w_in": f(w_in), "w_out": f(w_out), "w_up": f(w_up), "w_down": f(w_down),
        "lnp": f(lnp), "cw": f(cw), "fcw": f(fcw), "cs": f(cs_h.reshape(128, -1)), "btile": f(btile),
        "cfar": f(cfar), "small": f(small_h), "ind": f(ind_h),
    }


def kernel(x, w_in, conv_w, w_out, ln1_g, ln1_b, w_up, ffn_conv_w, w_down, ln2_g, ln2_b, rel_bias):
    args = [np.asarray(a) for a in (w_in, conv_w, w_out, ln1_g, ln1_b, w_up, ffn_conv_w, w_down, ln2_g, ln2_b, rel_bias)]
    x = np.asarray(x)
    B = x.shape[0]
    per = B // N_CORES
    nc = _get_nc(per, DEPTH)
    in_maps = []
    base = None
    for c in range(N_CORES):
        m = host_inputs(x[c * per:(c + 1) * per], *args) if base is None else dict(base, x=np.ascontiguousarray(x[c * per:(c + 1) * per], dtype=np.float32))
        if base is None:
            base = m
        in_maps.append(m)
    res = run_bass_kernel_spmd(nc, in_maps, core_ids=list(range(N_CORES)))
    return np.concatenate([np.asarray(r["out"]) for r in res.results], axis=0).astype(np.float32)
```

```python
import contextlib
import math
import numpy as np
import concourse.bass as bass
import concourse.mybir as mybir
from concourse.bass_utils import run_bass_kernel_spmd

F32 = mybir.dt.float32
BF16 = mybir.dt.bfloat16
AF = mybir.ActivationFunctionType
ALU = mybir.AluOpType
AX = mybir.AxisListType

N_CORES = 8
SEQ = 2048
D = 1024
NT = SEQ // 128
DEPTH = 2
IN_COLS = 3456
D_FF = 2816
ALPHA = (2.0 * DEPTH) ** 0.25
LN_EPS = 1e-5
NEG = -30000.0
BIG = 3.0e38

ENGS = ["pe", "act", "dve", "pool", "sp"]
SEM_CH = 12000


def I(name, *a, **k):
    return (name, a, k)


class Sched:
    def __init__(self, nc):
        self.nc = nc
        self.ops = {e: [] for e in ENGS}
        self.count = {e: 0 for e in ENGS}
        self.seen = {e: {} for e in ENGS}
        self.res = {}
        self.dma_cnt = {}

    def _deps(self, eng, reads, writes, skip_dma=None):
        deps = []
        for r in reads:
            st = self.res.get(r)
            if st and st[0] is not None:
                deps.append((st[0], True))
            if st and isinstance(r, tuple) and r[0] == "ps":
                for t in st[1]:
                    if t[1] != eng:
                        deps.append((t, True))
        for w in writes:
            st = self.res.get(w)
            if st:
                if st[0] is not None:
                    deps.append((st[0], False))
                for t in st[1]:
                    deps.append((t, False))
        seen = self.seen[eng]
        best = {}
        for tok, raw in deps:
            kind, key, val = tok
            if kind == "eng" and key == eng and not raw and eng == "pe":
                continue
            if kind == "dma" and key == skip_dma:
                continue
            k = (kind, key)
            if seen.get(k, 0) >= val:
                continue
            best[k] = max(best.get(k, 0), val)
        for k, v in best.items():
            seen[k] = v
        return [(k[0], k[1], v) for k, v in best.items()]

    def _commit(self, tok, reads, writes):
        for r in reads:
            st = self.res.setdefault(r, [None, []])
            st[1].append(tok)
        for w in writes:
            self.res[w] = [tok, []]

    def op(self, eng, fn, reads=(), writes=()):
        waits = self._deps(eng, reads, writes)
        self.count[eng] += 1
        tok = ("eng", eng, self.count[eng])
        self.ops[eng].append((waits, fn, tok))
        self._commit(tok, reads, writes)
        return tok

    def dma(self, q, sem, fn, reads=(), writes=()):
        waits = self._deps(q, reads, writes, skip_dma=sem)
        self.dma_cnt[sem] = self.dma_cnt.get(sem, 0) + 16
        tok = ("dma", sem, self.dma_cnt[sem])
        self.ops[q].append((waits, fn, tok))
        self._commit(tok, reads, writes)
        return tok

    def barrier(self):
        for e in ENGS:
            waits = []
            seen = self.seen[e]
            for e2 in ENGS:
                if e2 != e and self.count[e2] > seen.get(("eng", e2), 0):
                    waits.append(("eng", e2, self.count[e2]))
                    seen[("eng", e2)] = self.count[e2]
            for s, c in self.dma_cnt.items():
                if c > seen.get(("dma", s), 0):
                    waits.append(("dma", s, c))
                    seen[("dma", s)] = c
            if waits:
                self.ops[e].append((waits, None, None))
        self.res = {}

    def emit(self):
        nc = self.nc
        needed = {e: set() for e in ENGS}
        for e in ENGS:
            for waits, fn, tok in self.ops[e]:
                for kind, key, val in waits:
                    if kind == "eng":
                        needed[key].add(val)
        rank = {}
        for e in ENGS:
            rank[e] = {s: i + 1 for i, s in enumerate(sorted(needed[e]))}
        with contextlib.ExitStack() as es:
            esem = {}
            for e in ENGS:
                n = (len(rank[e]) + SEM_CH - 1) // SEM_CH
                esem[e] = [es.enter_context(nc.semaphore(f"s_{e}_{i}")) for i in range(max(n, 1))]
            dsem = {name: es.enter_context(nc.semaphore(f"d_{name}")) for name in self.dma_cnt}

            def lower(tok):
                kind, key, val = tok
                if kind == "eng":
                    r = rank[key][val]
                    return esem[key][(r - 1) // SEM_CH], (r - 1) % SEM_CH + 1
                return dsem[key], val

            def run(engobj, name):
                for waits, fn, tok in self.ops[name]:
                    for w in waits:
                        s, v = lower(w)
                        engobj.wait_ge(s, v)
                    if fn is None:
                        continue
                    ins = getattr(engobj, fn[0])(*fn[1], **fn[2])
                    if tok[0] == "eng":
                        if tok[2] in rank[name]:
                            s, v = lower(tok)
                            ins.then_inc(s, 1)
                    else:
                        s, v = lower(tok)
                        ins.then_inc(s, 16)

            with nc.Block() as block:
                @block.tensor
                def _(e):
                    run(e, "pe")

                @block.scalar
                def _(e):
                    run(e, "act")

                @block.vector
                def _(e):
                    run(e, "dve")

                @block.gpsimd
                def _(e):
                    run(e, "pool")

                @block.sync
                def _(e):
                    run(e, "sp")


class Arena:
    def __init__(self, t, words):
        self.t = t
        self.words = words
        self.off = 0

    def reset(self):
        self.off = 0

    def f32(self, n):
        assert self.off + n <= self.words, ("arena overflow", self.off + n, self.words)
        v = self.t[:, self.off:self.off + n]
        self.off += n
        return v

    def bf16(self, n):
        w = (n + 1) // 2
        assert self.off + w <= self.words, ("arena overflow", self.off + w, self.words)
        v = self.t[:, self.off:self.off + w].bitcast(BF16)
        self.off += w
        return v


SM_OWN = 0
SM_MASK = 128
SM_VSC = 256
SM_XI = 262
SM_ZETA = 268
SM_EPS = 274
SM_GC = 275
SM_W = 280


def _t5_bucket(dist):
    n = np.maximum(dist, 0)
    nf = np.maximum(n, 1).astype(np.float32)
    large = 16 + (np.log(nf / np.float32(16)) / np.float32(math.log(128 / 16)) * np.float32(16)).astype(np.int32)
    large = np.minimum(large, 31)
    return np.where(n < 16, n, large)


def _host_consts():
    p = np.arange(128)
    inv = (10000.0 ** (-np.arange(0, 64, 2, dtype=np.float32) / np.float32(64))).astype(np.float32)
    pos = (np.arange(NT)[None, :] * 128 + p[:, None]).astype(np.float32)
    ang = pos[:, :, None] * inv[None, None, :]
    cs = np.stack([np.concatenate([np.cos(ang), np.cos(ang)], axis=-1), np.concatenate([-np.sin(ang), np.sin(ang)], axis=-1)], axis=1).astype(np.float32)
    small = np.zeros((128, SM_W), np.float32)
    own = np.zeros((NT, 8), np.float32)
    for qt in range(NT):
        o = qt // 2
        own[qt, o] = BIG
        own[qt, o + 1:] = -BIG
    small[:, SM_OWN:SM_OWN + 128] = own.reshape(1, 128)
    e = p[:, None]; c = p[None, :]
    small[:, SM_MASK:SM_MASK + 128] = np.where(c >= e, 0.125, 0.0)
    g = 1.0 - 2.0 ** (-5.0 - np.arange(6))
    small[:, SM_VSC:SM_VSC + 6] = g[None, :] ** (-(p[:, None] + 1.0))
    small[:, SM_XI:SM_XI + 6] = g[None, :] ** (p[:, None] + 1.0)
    small[:, SM_ZETA:SM_ZETA + 6] = 0.125 * g[None, :] ** (127.0 - p[:, None])
    small[:, SM_EPS] = LN_EPS
    for j in range(3):
        small[0:64, SM_GC + j] = g[2 * j] ** 128.0
        small[64:128, SM_GC + j] = g[2 * j + 1] ** 128.0
    gC = [float(x) for x in g ** 128.0]
    ind = (np.arange(SEQ)[None, :] // 256 == np.arange(8)[:, None]).astype(np.float32)
    bidx_d = _t5_bucket(c - e)
    bidx_s = _t5_bucket(128 + c - e)
    causal = (c >= e)
    return cs, small, gC, ind, bidx_d, bidx_s, causal


class _Stop(Exception):
    pass


def build(nseq=4, depth=DEPTH, stop=None):
    cs_h, small_h, gC, ind_h, _, _, _ = _host_consts()
    nc = bass.Bass("TRN2", target_bir_lowering=False)
    x_d = nc.dram_tensor("x", [nseq, SEQ, D], F32, kind="ExternalInput").ap()
    w_in_d = nc.dram_tensor("w_in", [DEPTH, D, IN_COLS], F32, kind="ExternalInput").ap()
    w_out_d = nc.dram_tensor("w_out", [DEPTH, D, D], F32, kind="ExternalInput").ap()
    w_up_d = nc.dram_tensor("w_up", [DEPTH, D, 2 * D_FF], F32, kind="ExternalInput").ap()
    w_dn_d = nc.dram_tensor("w_down", [DEPTH, D_FF, D], F32, kind="ExternalInput").ap()
    lnp_d = nc.dram_tensor("lnp", [DEPTH, 4, 128, D], F32, kind="ExternalInput").ap()
    cw_d = nc.dram_tensor("cw", [DEPTH, 128, 6], F32, kind="ExternalInput").ap()
    fcw_d = nc.dram_tensor("fcw", [DEPTH, 128, 66], F32, kind="ExternalInput").ap()
    cs_d = nc.dram_tensor("cs", [128, 2 * NT * 64], F32, kind="ExternalInput").ap()
    bt_d = nc.dram_tensor("btile", [128, 6 * 2 * 128], F32, kind="ExternalInput").ap()
    cfar_d = nc.dram_tensor("cfar", [128, 6], F32, kind="ExternalInput").ap()
    small_d = nc.dram_tensor("small", [128, SM_W], F32, kind="ExternalInput").ap()
    ind_d = nc.dram_tensor("ind", [8, SEQ], F32, kind="ExternalInput").ap()
    out_d = nc.dram_tensor("out", [nseq, SEQ, D], F32, kind="ExternalOutput").ap()
    win_b = nc.dram_tensor("win_b", [DEPTH, D, IN_COLS], BF16).ap()
    wout_b = nc.dram_tensor("wout_b", [DEPTH, D, D], BF16).ap()
    wup_b = nc.dram_tensor("wup_b", [DEPTH, 11, 128, 8 * 512], BF16).ap()
    wdn_b = nc.dram_tensor("wdn_b", [DEPTH, D_FF, D], BF16).ap()
    ind_b = nc.dram_tensor("ind_b", [8, SEQ], BF16).ap()

    S = Sched(nc)

    def ck(tag):
        if stop == tag:
            raise _Stop()
    ARENA_W = 20992
    with contextlib.ExitStack() as es:
        def sb(name, shape, dt=F32):
            return es.enter_context(nc.sbuf_tensor("sb_" + name, shape, dt))

        xs = sb("xs", [128, NT, D])
        hT = sb("hT", [128, 8, SEQ], BF16)
        cs = sb("cs", [128, 2, NT, 64])
        bt = sb("bt", [128, 6, 2, 128])
        cfar = sb("cfar", [128, 6])
        small = sb("small", [128, SM_W])
        lnp = sb("lnp", [128, 2, D])
        cw = sb("cw", [128, 6])
        fcw = sb("fcw", [128, 66])
        ident = sb("ident", [128, 128], BF16)
        ones_f = sb("ones_f", [128, 64])
        xb = [sb("xb0", [128, D], BF16), sb("xb1", [128, D], BF16)]
        lnst = sb("lnst", [128, 4, 16])
        arena_t = sb("arena", [128, ARENA_W])
        A = Arena(arena_t, ARENA_W)
        pb = [es.enter_context(nc.psum_tensor(f"pb{i}", [128, 512], F32)) for i in range(8)]

        def pbb(i):
            return pb[i][:, :].bitcast(BF16)

        S.dma("sp", "c_cs", I("dma_start", out=cs[:].rearrange("p a t j -> p (a t j)"), in_=cs_d), writes=["cs"])
        S.dma("sp", "c_bt", I("dma_start", out=bt[:].rearrange("p h a q -> p (h a q)"), in_=bt_d), writes=["bt"])
        S.dma("sp", "c_cf", I("dma_start", out=cfar[:], in_=cfar_d), writes=["cfar"])
        S.dma("sp", "c_sm", I("dma_start", out=small[:], in_=small_d), writes=["small"])
        identf = A.f32(128)
        S.op("pool", I("memset", identf[:], 0.0), writes=["identf"])
        S.op("pool", I("affine_select", out=identf[:], in_=identf[:], pattern=[[-1, 128]],
                                              compare_op=ALU.not_equal, fill=1.0, base=0, channel_multiplier=1),
             reads=["identf"], writes=["identf"])
        S.op("dve", I("tensor_copy", out=ident[:], in_=identf[:]), reads=["identf"], writes=["ident"])
        S.op("pool", I("memset", ones_f[:], 1.0), writes=["ones_f"])
        for h in range(6):
            S.op("dve", I("tensor_scalar", out=bt[:, h, :, :], in0=bt[:, h, :, :], scalar1=cfar[:, h:h + 1], scalar2=None, op0=ALU.subtract),
                 reads=["bt", "cfar"], writes=["bt"])
        S.barrier()

        ownmask = small[:, SM_OWN:SM_OWN + 128].rearrange("p (t n) -> p t n", n=8)
        mask8 = small[:, SM_MASK:SM_MASK + 128]
        vsc1 = small[:, SM_VSC:SM_VSC + 6]
        xi = small[:, SM_XI:SM_XI + 6]
        zeta8 = small[:, SM_ZETA:SM_ZETA + 6]
        eps_t = small[:, SM_EPS:SM_EPS + 1]
        gct = small[:, SM_GC:SM_GC + 3]

        def bc3(ap2, n):
            a = ap2.shape[1]
            return ap2.unsqueeze(2).to_broadcast([128, a, n])

        def bcm(ap2, m):
            n = ap2.shape[1]
            return ap2.unsqueeze(1).to_broadcast([128, m, n])

        def wload(sem, dst3, scr2, r0, nrow_chunks, c0, c1, res):
            for k in range(nrow_chunks):
                S.dma("sp", sem, I("dma_start", out=dst3[:, k, :], in_=scr2[r0 + k * 128:r0 + (k + 1) * 128, c0:c1]), writes=[res])

        def emit_hT(t):
            b = t % 2
            S.op("act", I("activation", out=xb[b][:], in_=xs[:, t, :], func=AF.Copy),
                 reads=[("xs", t)], writes=[("xb", b)])
            p4 = pbb(4)
            for c in range(8):
                S.op("pe", I("transpose", out=p4[:, c * 128:(c + 1) * 128], in_=xb[b][:, c * 128:(c + 1) * 128],
                                                     identity=ident[:]),
                     reads=[("xb", b), "ident"], writes=[("ps", 4)])
            S.op("dve", I("tensor_copy", out=hT[:, :, t * 128:(t + 1) * 128],
                                                in_=p4.rearrange("p (c k) -> p c k", k=128)),
                 reads=[("ps", 4)], writes=[("hT", t)])

        def emit_ln_batch(tiles):
            n = len(tiles)
            for i, t in enumerate(tiles):
                S.op("dve", I("bn_stats", lnst[:, i, 0:6], xs[:, t, 0:512]), reads=[("xs", t)], writes=[("lnst0", i)])
                S.op("dve", I("bn_stats", lnst[:, i, 6:12], xs[:, t, 512:1024]), reads=[("xs", t)], writes=[("lnst1", i)])
                S.op("dve", I("bn_aggr", lnst[:, i, 12:14], lnst[:, i, 0:12]), reads=[("lnst0", i), ("lnst1", i)], writes=[("lnmv", i)])
            mvr = [("lnmv", i) for i in range(n)]
            S.op("act", I("activation", out=lnst[:, 0:n, 14], in_=lnst[:, 0:n, 13], func=AF.Ln, bias=eps_t, scale=1.0), reads=mvr, writes=["lnr"])
            S.op("act", I("activation", out=lnst[:, 0:n, 14], in_=lnst[:, 0:n, 14], func=AF.Exp, scale=-0.5), reads=["lnr"], writes=["lnr"])
            S.op("dve", I("scalar_tensor_tensor", out=lnst[:, 0:n, 15], in0=lnst[:, 0:n, 12], scalar=-1.0, in1=lnst[:, 0:n, 14], op0=ALU.mult, op1=ALU.mult),
                 reads=["lnr"] + mvr, writes=["lnn"])
            for i, t in enumerate(tiles):
                xt = xs[:, t, :]
                S.op("act", I("activation", out=xt, in_=xt, func=AF.Identity, bias=lnst[:, i, 15:16], scale=lnst[:, i, 14:15]),
                     reads=[("xs", t), "lnn", "lnr"], writes=[("xs", t)])
            for i, t in enumerate(tiles):
                xt = xs[:, t, :]
                S.op("dve", I("tensor_tensor", out=xt, in0=xt, in1=lnp[:, 0, :], op=ALU.mult), reads=[("xs", t), "lnp"], writes=[("xs", t)])
                S.op("dve", I("tensor_tensor", out=xt, in0=xt, in1=lnp[:, 1, :], op=ALU.add), reads=[("xs", t), "lnp"], writes=[("xs", t)])

        def load_lnp(l, gi):
            S.dma("sp", "lnp", I("dma_start", out=lnp[:, 0, :], in_=lnp_d[l, gi]),
                  writes=["lnp"])
            S.dma("sp", "lnp", I("dma_start", out=lnp[:, 1, :], in_=lnp_d[l, gi + 1]),
                  writes=["lnp"])

        def outproj_partial(t, lhs_fn, nk, wo3, first, lhs_reads, banks=(6, 7)):
            for hf in range(2):
                for j in range(nk):
                    S.op("pe", I("matmul", out=pb[banks[hf]][:, 0:512], lhsT=lhs_fn(j),
                                                               rhs=wo3[:, j, hf * 512:(hf + 1) * 512],
                                                               start=(j == 0), stop=(j == nk - 1)),
                         reads=list(lhs_reads) + ["wo"], writes=[("ps", banks[hf])])
            for hf in range(2):
                xsl = xs[:, t, hf * 512:(hf + 1) * 512]
                if first:
                    S.op("dve", I("scalar_tensor_tensor", out=xsl, in0=xsl, scalar=ALPHA, in1=pb[banks[hf]][:, 0:512],
                                                                                op0=ALU.mult, op1=ALU.add),
                         reads=[("ps", banks[hf]), ("xs", t)], writes=[("xs", t)])
                else:
                    S.op("dve", I("tensor_tensor", out=xsl, in0=xsl, in1=pb[banks[hf]][:, 0:512], op=ALU.add),
                         reads=[("ps", banks[hf]), ("xs", t)], writes=[("xs", t)])

        NSL, LEAD = 12, 6
        pf = [A.f32(768) for _ in range(NSL)]
        pbf = [A.bf16(768) for _ in range(NSL)]
        pieces = []
        for l in range(DEPTH):
            for src, dstb, nrows, ncols in ((w_in_d[l], win_b[l], D, IN_COLS), (w_out_d[l], wout_b[l], D, D)):
                for k in range(nrows // 128):
                    for p0 in range(0, ncols, 768):
                        n = min(768, ncols - p0)
                        rs = slice(k * 128, (k + 1) * 128)
                        pieces.append(([(0, n, src[rs, p0:p0 + n])], n, dstb[rs, p0:p0 + n]))
            for jp in range(11):
                for c in range(8):
                    rs = slice(c * 128, (c + 1) * 128)
                    pieces.append(([(0, 256, w_up_d[l][rs, jp * 256:(jp + 1) * 256]),
                                    (256, 256, w_up_d[l][rs, D_FF + jp * 256:D_FF + (jp + 1) * 256])], 512,
                                   wup_b[l, jp, :, c * 512:(c + 1) * 512]))
            for k in range(D_FF // 128):
                for p0 in range(0, D, 768):
                    n = min(768, D - p0)
                    rs = slice(k * 128, (k + 1) * 128)
                    pieces.append(([(0, n, w_dn_d[l][rs, p0:p0 + n])], n, wdn_b[l][rs, p0:p0 + n]))
        cast_eng = ["pool", "act", "dve"]
        npc = len(pieces)
        for i in range(npc + LEAD):
            if i < npc:
                sl = i % NSL
                for c0_, n_, src_ in pieces[i][0]:
                    S.dma("sp", f"pl{sl}", I("dma_start", out=pf[sl][:, c0_:c0_ + n_], in_=src_), writes=[("pf", sl)])
            j = i - LEAD
            if j >= 0:
                sl = j % NSL
                n_ = pieces[j][1]
                ce = cast_eng[j % 3]
                if ce == "act":
                    S.op("act", I("activation", out=pbf[sl][:, 0:n_], in_=pf[sl][:, 0:n_], func=AF.Copy), reads=[("pf", sl)], writes=[("pbf", sl)])
                else:
                    S.op(ce, I("tensor_copy", out=pbf[sl][:, 0:n_], in_=pf[sl][:, 0:n_]), reads=[("pf", sl)], writes=[("pbf", sl)])
                S.dma("sp", f"ps{sl}", I("dma_start", out=pieces[j][2], in_=pbf[sl][:, 0:n_]), reads=[("pbf", sl)])
        S.dma("sp", "pl0", I("dma_start", out=pf[0][64:72, 0:768], in_=ind_d[:, 0:768]), writes=[("pf", 0)])
        S.dma("sp", "pl1", I("dma_start", out=pf[1][64:72, 0:768], in_=ind_d[:, 768:1536]), writes=[("pf", 1)])
        S.dma("sp", "pl2", I("dma_start", out=pf[2][64:72, 0:512], in_=ind_d[:, 1536:2048]), writes=[("pf", 2)])
        for q_, (o_, n_) in enumerate(((0, 768), (768, 768), (1536, 512))):
            S.op("dve", I("tensor_copy", out=pbf[q_][64:72, 0:n_], in_=pf[q_][64:72, 0:n_]), reads=[("pf", q_)], writes=[("pbf", q_)])
            S.dma("sp", f"ps{q_}", I("dma_start", out=ind_b[:, o_:o_ + n_], in_=pbf[q_][64:72, 0:n_]), reads=[("pbf", q_)])
        S.barrier()

        try:
            for s in range(nseq):
                for t in range(NT):
                    S.dma("sp", f"xload{t}", I("dma_start", out=xs[:, t, :], in_=x_d[s, t * 128:(t + 1) * 128, :]),
                          writes=[("xs", t)])
                for t in range(NT):
                    emit_hT(t)
                if stop == "hT":
                    raise _Stop()
                for l in range(depth):
                    w_in = win_b[l]; w_out = wout_b[l]; w_dn = wdn_b[l]
                    S.barrier(); A.reset()
                    wR3 = A.bf16(8 * 1536).rearrange("p (c n) -> p c n", n=1536)
                    woR3 = A.bf16(3 * 1024).rearrange("p (c n) -> p c n", n=1024)
                    qrot = A.bf16(384)
                    krot = [A.bf16(384) for _ in range(2)]
                    vs1 = [A.bf16(384).rearrange("p (h d) -> p h d", d=64) for _ in range(2)]
                    vz = [A.bf16(384).rearrange("p (h d) -> p h d", d=64) for _ in range(2)]
                    qkT = [A.bf16(768) for _ in range(2)]
                    sg = [A.f32(384) for _ in range(2)]
                    scT = A.bf16(768).rearrange("p (par j c) -> p par j c", par=2, c=128)
                    ycat = [A.bf16(384) for _ in range(2)]; catT = A.bf16(384)
                    stb = [A.bf16(192).rearrange("p (j v) -> p j v", v=64) for _ in range(2)]
                    stf = A.f32(192).rearrange("p (j v) -> p j v", v=64)
                    tq = [A.f32(384).rearrange("p (h d) -> p h d", d=64) for _ in range(4)]
                    eg = A.f32(384)
                    yr = A.f32(384).rearrange("p (h d) -> p h d", d=64)
                    sq = A.f32(384).rearrange("p (h d) -> p h d", d=64)
                    st6 = A.f32(48)
                    s1 = st6[:, 0:6]; s2 = st6[:, 8:14]; mean = st6[:, 16:22]; msq = st6[:, 24:30]; var = st6[:, 32:38]; rstd = st6[:, 40:46]
                    wload("wR", wR3, w_in, 0, 8, 0, 1536, "wR")
                    wload("wo", woR3, w_out, 0, 3, 0, 1024, "wo")
                    S.op("dve", I("memset", stf, 0.0), writes=["stf"] + [("stf", h) for h in range(6)])
                    S.op("dve", I("memset", stb[0], 0.0), writes=[("stb", 0)])

                    def r_proj(t):
                        tsl = slice(t * 128, (t + 1) * 128)
                        for c in range(8):
                            for g4 in range(4):
                                S.op("pe", I("matmul", out=pb[g4][:, 0:384], lhsT=hT[:, c, tsl], rhs=wR3[:, c, g4 * 384:(g4 + 1) * 384],
                                             start=(c == 0), stop=(c == 7)),
                                     reads=[("hT", t), "wR"], writes=[("ps", g4)])

                    def r_front_ew(t):
                        p = t % 2
                        cos2 = bcm(cs[:, 0, t, :], 6)
                        nsin = bcm(cs[:, 1, t, 0:32], 6); psin = bcm(cs[:, 1, t, 32:64], 6)
                        for bank, dst, nm, o in ((0, qrot, "qrot", 0), (1, krot[p], ("krot", p), 2)):
                            pq = pb[bank][:, 0:384].rearrange("p (h d) -> p h d", d=64)
                            ra = tq[o]; rb = tq[o + 1]
                            S.op("dve", I("tensor_tensor", out=ra, in0=pq, in1=cos2, op=ALU.mult), reads=[("ps", bank), "cs"], writes=[("tq", o)])
                            S.op("dve", I("tensor_tensor", out=rb[:, :, 0:32], in0=pq[:, :, 32:64], in1=nsin, op=ALU.mult), reads=[("ps", bank), "cs"], writes=[("tq", o + 1, 0)])
                            S.op("dve", I("tensor_tensor", out=rb[:, :, 32:64], in0=pq[:, :, 0:32], in1=psin, op=ALU.mult), reads=[("ps", bank), "cs"], writes=[("tq", o + 1, 1)])
                            S.op("dve", I("tensor_tensor", out=dst.rearrange("p (h d) -> p h d", d=64), in0=ra, in1=rb, op=ALU.add),
                                 reads=[("tq", o), ("tq", o + 1, 0), ("tq", o + 1, 1)], writes=[(nm, 1), (nm, 2)])
                        pv = pb[2][:, 0:384].rearrange("p (h d) -> p h d", d=64)
                        S.op("dve", I("tensor_tensor", out=vs1[p], in0=pv, in1=bc3(vsc1, 64), op=ALU.mult), reads=[("ps", 2), "small"], writes=[("vs1", p)])
                        S.op("dve", I("tensor_tensor", out=vz[p], in0=pv, in1=bc3(zeta8, 64), op=ALU.mult), reads=[("ps", 2), "small"], writes=[("vz", p)])
                        S.op("act", I("activation", out=eg, in_=pb[3][:, 0:384], func=AF.Exp, scale=-1.0), reads=[("ps", 3)], writes=["eg"])
                        S.op("act", I("activation", out=eg, in_=eg, func=AF.Identity, bias=ones_f[:, 0:1], scale=1.0), reads=["eg", "ones_f"], writes=["eg"])
                        S.op("dve", I("reciprocal", out=eg, in_=eg), reads=["eg"], writes=["eg"])
                        S.op("dve", I("tensor_tensor", out=sg[p], in0=pb[3][:, 0:384], in1=eg, op=ALU.mult), reads=[("ps", 3), "eg"], writes=[("sg", p)])

                    def r_front_tr(t):
                        p = t % 2
                        p4 = pbb(4)
                        for j in range(3):
                            S.op("pe", I("transpose", out=p4[:, j * 128:(j + 1) * 128], in_=qrot[:, j * 128:(j + 1) * 128], identity=ident[:]),
                                 reads=[("qrot", 1), ("qrot", 2), "ident"], writes=[("ps", 4)])
                            S.op("pe", I("transpose", out=p4[:, (3 + j) * 128:(4 + j) * 128], in_=krot[p][:, j * 128:(j + 1) * 128], identity=ident[:]),
                                 reads=[(("krot", p), 1), (("krot", p), 2), "ident"], writes=[("ps", 4)])
                        S.op("act", I("activation", out=qkT[p], in_=p4[:, 0:768], func=AF.Copy), reads=[("ps", 4)], writes=[("qkT", p)])

                    def r_scores(t):
                        p = t % 2
                        for h in range(6):
                            j, hf = h // 2, h % 2
                            pr = slice(hf * 64, hf * 64 + 64)
                            S.op("pe", I("matmul", out=pb[5 + hf][:, j * 128:(j + 1) * 128], lhsT=qkT[p][pr, (3 + j) * 128:(4 + j) * 128],
                                         rhs=qkT[p][pr, j * 128:(j + 1) * 128], start=True, stop=True),
                                 reads=[("qkT", p)], writes=[("ps", 5 + hf)])

                    def r_mask(t):
                        for hf in range(2):
                            S.op("dve", I("tensor_tensor", out=scT[:, hf, :, :], in0=pb[5 + hf][:, 0:384].rearrange("p (h c) -> p h c", c=128),
                                          in1=bcm(mask8, 3), op=ALU.mult),
                                 reads=[("ps", 5 + hf), "small"], writes=[("scT", hf)])

                    def r_o_ds(t):
                        p = t % 2
                        sb_cur = stb[t % 2]
                        for h in range(6):
                            j, hf = h // 2, h % 2
                            pr = slice(hf * 64, hf * 64 + 64)
                            S.op("pe", I("matmul", out=pb[5][:, h * 64:(h + 1) * 64], lhsT=scT[:, hf, j, :], rhs=vs1[p][:, h, :], start=True, stop=False),
                                 reads=[("scT", 0), ("scT", 1), ("vs1", p)], writes=[("ps", 5)])
                            S.op("pe", I("matmul", out=pb[5][:, h * 64:(h + 1) * 64], lhsT=qkT[p][pr, j * 128:(j + 1) * 128],
                                         rhs=sb_cur[pr, j, :], start=False, stop=True),
                                 reads=[("qkT", p), ("stb", t % 2)], writes=[("ps", 5)])
                        for h in range(6):
                            j, hf = h // 2, h % 2
                            S.op("pe", I("matmul", out=pb[7][hf * 64:hf * 64 + 64, j * 64:(j + 1) * 64], lhsT=krot[p][:, h * 64:(h + 1) * 64], rhs=vz[p][:, h, :],
                                         start=True, stop=True),
                                 reads=[(("krot", p), 1), (("krot", p), 2), ("vz", p)], writes=[("ps", 7)])

                    def r_state_norm(t):
                        p = t % 2
                        sb_nxt = stb[(t + 1) % 2]
                        for j in range(3):
                            S.op("dve", I("scalar_tensor_tensor", out=stf[:, j, :], in0=stf[:, j, :], scalar=gct[:, j:j + 1],
                                          in1=pb[7][:, j * 64:(j + 1) * 64], op0=ALU.mult, op1=ALU.add),
                                 reads=[("ps", 7), ("stf", 2 * j), "small"], writes=[("stf", 2 * j), ("stf", 2 * j + 1)])
                        S.op("act", I("activation", out=sb_nxt, in_=stf, func=AF.Copy),
                             reads=[("stf", h) for h in range(6)], writes=[("stb", (t + 1) % 2)])
                        po = pb[5][:, 0:384].rearrange("p (h d) -> p h d", d=64)
                        S.op("dve", I("tensor_tensor", out=yr, in0=po, in1=bc3(xi, 64), op=ALU.mult), reads=[("ps", 5), "small"], writes=["yr"])
                        S.op("dve", I("tensor_reduce", out=s1, in_=yr, axis=AX.X, op=ALU.add), reads=["yr"], writes=["s1"])
                        S.op("dve", I("tensor_tensor", out=sq, in0=yr, in1=yr, op=ALU.mult), reads=["yr"], writes=["sq"])
                        S.op("dve", I("tensor_reduce", out=s2, in_=sq, axis=AX.X, op=ALU.add), reads=["sq"], writes=["s2"])
                        S.op("dve", I("tensor_scalar", out=mean, in0=s1, scalar1=1.0 / 64, scalar2=None, op0=ALU.mult), reads=["s1"], writes=["mean"])
                        S.op("dve", I("tensor_tensor", out=msq, in0=mean, in1=mean, op=ALU.mult), reads=["mean"], writes=["msq"])
                        S.op("dve", I("tensor_scalar", out=var, in0=s2, scalar1=1.0 / 64, scalar2=None, op0=ALU.mult), reads=["s2"], writes=["var"])
                        S.op("dve", I("tensor_tensor", out=var, in0=var, in1=msq, op=ALU.subtract), reads=["var", "msq"], writes=["var"])
                        S.op("dve", I("tensor_scalar", out=var, in0=var, scalar1=0.0, scalar2=None, op0=ALU.max), reads=["var"], writes=["var"])
                        S.op("act", I("activation", out=rstd, in_=var, func=AF.Ln, bias=eps_t, scale=1.0), reads=["var", "small"], writes=["rstd"])
                        S.op("act", I("activation", out=rstd, in_=rstd, func=AF.Exp, scale=-0.5), reads=["rstd"], writes=["rstd"])
                        S.op("dve", I("tensor_tensor", out=yr, in0=yr, in1=bc3(mean, 64), op=ALU.subtract), reads=["yr", "mean", "s1", "sq"], writes=["yr"])
                        S.op("dve", I("tensor_tensor", out=yr, in0=yr, in1=bc3(rstd, 64), op=ALU.mult), reads=["yr", "rstd"], writes=["yr"])
                        S.op("dve", I("tensor_tensor", out=ycat[p], in0=yr.rearrange("p h d -> p (h d)"), in1=sg[p], op=ALU.mult), reads=["yr", ("sg", p)], writes=[("ycat", p)])

                    def r_out(t):
                        p4 = pbb(4)
                        for j in range(3):
                            S.op("pe", I("transpose", out=p4[:, j * 128:(j + 1) * 128], in_=ycat[t % 2][:, j * 128:(j + 1) * 128], identity=ident[:]),
                                 reads=[("ycat", t % 2), "ident"], writes=[("ps", 4)])
                        S.op("act", I("activation", out=catT, in_=p4[:, 0:384], func=AF.Copy), reads=[("ps", 4)], writes=["catT"])
                        outproj_partial(t, lambda j: catT[:, j * 128:(j + 1) * 128], 3, woR3, True, ["catT"])

                    for t in range(NT + 2):
                        back = 1 <= t <= NT
                        if t < NT:
                            r_proj(t)
                        if back:
                            r_scores(t - 1)
                        if t < NT:
                            r_front_ew(t)
                        if back:
                            r_mask(t - 1)
                        if t < NT:
                            r_front_tr(t)
                        if t >= 2:
                            r_out(t - 2)
                        if back:
                            r_o_ds(t - 1)
                            r_state_norm(t - 1)

                    if stop == "R":
                        raise _Stop()
                    S.barrier(); A.reset()
                    wM3 = A.bf16(8 * 1152).rearrange("p (c n) -> p c n", n=1152)
                    woM3 = A.bf16(3 * 1024).rearrange("p (c n) -> p c n", n=1024)
                    KT = A.bf16(6 * SEQ).rearrange("p (h k) -> p h k", k=SEQ)
                    QT = A.bf16(6 * 512).rearrange("p (h k) -> p h k", k=512)
                    Va = A.bf16(NT * 6 * 65 + 2).rearrange("p (t h d) -> p t h d", h=6, d=65) if False else None
                    Vraw = A.bf16(NT * 6 * 66)
                    Va = Vraw.rearrange("p (t h d) -> p t h d", h=6, d=66)
                    NPT = 4
                    PT = [A.bf16(512) for _ in range(NPT)]
                    OTs = A.f32(512); rec = OTs
                    catM = A.bf16(3 * 512).rearrange("p (j k) -> p j k", k=512)
                    NM = [A.bf16(6 * 72).rearrange("p (h n) -> p h n", n=72) for _ in range(4)]
                    gsb = [A.f32(48).rearrange("p (h n) -> p h n", n=8) for _ in range(4)]
                    m8 = [A.f32(48).rearrange("p (h n) -> p h n", n=8) for _ in range(4)]
                    ge = [A.f32(48).rearrange("p (h n) -> p h n", n=8) for _ in range(4)]
                    km = A.f32(48).rearrange("p (h n) -> p h n", n=8)
                    kmT = A.bf16(48).rearrange("p (h n) -> p h n", n=8)
                    wload("wM", wM3, w_in, 0, 8, 1536, 2688, "wM")
                    wload("wo", woM3, w_out, 384, 3, 0, 1024, "wo")
                    for h in range(6):
                        S.dma("sp", f"ind{h}", I("dma_start", out=KT[64:72, h, :], in_=ind_b), writes=[("KTi", h)])
                    S.op("dve", I("memset", Vraw, 1.0), writes=["Va1"])
                    for i in range(4):
                        S.op("dve", I("memset", NM[i], 0.0), writes=[("NM", i)])
                    S.op("dve", I("memset", kmT, 0.0), writes=["kmT"])
                    S.op("dve", I("memset", km, 0.0), writes=["km"])
                    sidx = [0]; pidx = [0]; oidx = [0]
                    for G in range(4):
                        gsl = slice(G * 512, (G + 1) * 512)
                        hTr = [("hT", 4 * G + i) for i in range(4)]
                        for h in range(6):
                            for which in range(2):
                                bank = which
                                c0 = which * 384 + h * 64
                                for c in range(8):
                                    S.op("pe", I("matmul", out=pb[bank][0:64, 0:512], lhsT=wM3[:, c, c0:c0 + 64], rhs=hT[:, c, gsl],
                                                                                         start=(c == 0), stop=(c == 7)),
                                         reads=hTr + ["wM"], writes=[("ps", bank)])
                                if which == 0:
                                    S.op("act", I("activation", out=QT[0:64, h, :], in_=pb[0][0:64, 0:512], func=AF.Identity, scale=0.125),
                                         reads=[("ps", 0)], writes=[("QT", h)])
                                else:
                                    S.op("dve", I("tensor_copy", out=KT[0:64, h, gsl], in_=pb[1][0:64, 0:512]),
                                         reads=[("ps", 1)], writes=[("KT", h, G)])
                        for i in range(4):
                            tt = 4 * G + i
                            bank = 2 + (i % 2)
                            for c in range(8):
                                S.op("pe", I("matmul", out=pb[bank][:, 0:384], lhsT=hT[:, c, tt * 128:(tt + 1) * 128], rhs=wM3[:, c, 768:1152],
                                                                                     start=(c == 0), stop=(c == 7)),
                                     reads=[("hT", tt), "wM"], writes=[("ps", bank)])
                            S.op("act", I("activation", out=Va[:, tt, :, 0:64], in_=pb[bank][:, 0:384].rearrange("p (h d) -> p h d", d=64), func=AF.Copy),
                                 reads=[("ps", bank), "Va1"], writes=[("Va", tt)])
                        for h in range(6):
                            S.op("dve", I("tensor_reduce", out=km[0:64, h, 2 * G:2 * G + 2], in_=KT[0:64, h, gsl].rearrange("p (n k) -> p n k", k=256),
                                                                      axis=AX.X, op=ALU.add),
                                 reads=[("KT", h, G), "km"], writes=[("km", h)])
                        S.op("dve", I("tensor_scalar", out=kmT[0:64, :, 2 * G:2 * G + 2], in0=km[0:64, :, 2 * G:2 * G + 2], scalar1=1.0 / 256, scalar2=None, op0=ALU.mult),
                             reads=[("km", h) for h in range(6)] + ["kmT"], writes=["kmT"])
                        for i in range(4):
                            for h in range(6):
                                S.op("pe", I("matmul", out=pb[5][:, i * 48 + h * 8:i * 48 + (h + 1) * 8], lhsT=QT[0:64, h, i * 128:(i + 1) * 128], rhs=kmT[0:64, h, :],
                                             start=True, stop=True),
                                     reads=[("QT", h), "kmT"], writes=[("ps", 5)])
                        for i in range(4):
                            qt = 4 * G + i
                            nb = NM[i]
                            S.op("dve", I("tensor_tensor", out=gsb[i], in0=pb[5][:, i * 48:(i + 1) * 48].rearrange("p (h n) -> p h n", n=8), in1=bcm(ownmask[:, qt, :], 6), op=ALU.add),
                                 reads=[("ps", 5), "small"], writes=[("gsb", i)])
                            for h in range(6):
                                S.op("dve", I("max", out=m8[i][:, h, :], in_=gsb[i][:, h, :]), reads=[("gsb", i)], writes=[("m8", i, h)])
                            S.op("dve", I("tensor_tensor", out=ge[i], in0=gsb[i], in1=m8[i][:, :, 3:4].to_broadcast([128, 6, 8]), op=ALU.is_ge),
                                 reads=[("gsb", i)] + [("m8", i, h) for h in range(6)], writes=[("ge", i)])
                            S.op("dve", I("tensor_scalar", out=nb[:, :, 64:72], in0=ge[i], scalar1=-1.0, scalar2=-NEG, op0=ALU.add, op1=ALU.mult),
                                 reads=[("ge", i), ("NM", i)], writes=[("NM", i)])
                        for i in range(4):
                            nb = NM[i]
                            p4 = pbb(4)
                            for h in range(6):
                                S.op("pe", I("transpose", out=p4[0:72, h * 128:(h + 1) * 128], in_=nb[:, h, :], identity=ident[:]),
                                     reads=[("NM", i), "ident"], writes=[("ps", 4)])
                            S.op("act", I("activation", out=QT[64:72, :, i * 128:(i + 1) * 128], in_=p4[64:72, 0:768].rearrange("p (h k) -> p h k", k=128), func=AF.Copy),
                                 reads=[("ps", 4)] + [("QT", h) for h in range(6)], writes=[("QTn", i)])
                        nkt = 4 * G + 4
                        seq = [(h, kt) for h in range(6) for kt in range(nkt)]
                        LA = 3
                        slot = {}
                        pend = []
                        obs = {}
                        for h in range(6):
                            obs[h] = 6 + (oidx[0] % 2); oidx[0] += 1
                        for idx in range(len(seq) + LA):
                            if idx < len(seq):
                                h, kt = seq[idx]
                                rel = kt - 4 * G
                                c0 = max(0, rel) * 128
                                sbk = sidx[0] % 4; sidx[0] += 1
                                ptb = pidx[0] % NPT; pidx[0] += 1
                                slot[idx] = ptb
                                Pt = PT[ptb]
                                S.op("pe", I("matmul", out=pb[sbk][:, c0:512], lhsT=KT[0:72, h, kt * 128:(kt + 1) * 128], rhs=QT[0:72, h, c0:512],
                                             start=True, stop=True),
                                     reads=[("KT", h, kt // 4), ("KTi", h), ("QT", h)] + [("QTn", i) for i in range(4)], writes=[("ps", sbk)])
                                if -1 <= rel <= 3:
                                    if rel == -1:
                                        a0, a1, bsl = 0, 128, bt[:, h, 1, :]
                                    elif rel == 3:
                                        a0, a1, bsl = 384, 512, bt[:, h, 0, :]
                                    else:
                                        a0, a1, bsl = rel * 128, rel * 128 + 256, bt[:, h, :, :].rearrange("p a q -> p (a q)")
                                    S.op("dve", I("tensor_tensor", out=pb[sbk][:, a0:a1], in0=pb[sbk][:, a0:a1], in1=bsl, op=ALU.add),
                                         reads=[("ps", sbk), "bt"], writes=[("ps", sbk)])
                                S.op("act", I("activation", out=Pt[:, c0:512], in_=pb[sbk][:, c0:512], func=AF.Exp, bias=cfar[:, h:h + 1], scale=1.0),
                                     reads=[("ps", sbk), "cfar", ("PT", ptb)], writes=[("PTf", ptb)])
                            jdx = idx - LA
                            if jdx >= 0:
                                h, kt = seq[jdx]
                                j, hf = h // 2, h % 2
                                ob = obs[h]
                                rel = kt - 4 * G
                                c0 = max(0, rel) * 128
                                ptb = slot[jdx]
                                Pt = PT[ptb]
                                S.op("pe", I("matmul", out=pb[ob][0:65, c0:512], lhsT=Va[:, kt, h, 0:65], rhs=Pt[:, c0:512],
                                             start=(kt == 0), stop=(kt == nkt - 1)),
                                     reads=[("Va", kt), "Va1", ("PTf", ptb)], writes=[("ps", ob), ("PT", ptb)])
                                if kt == nkt - 1:
                                    S.op("dve", I("reciprocal", out=rec[64:65, :], in_=pb[ob][64:65, 0:512]), reads=[("ps", ob)], writes=["rec"])
                                    pend.append((idx + 2, h, ob))
                            while pend and (pend[0][0] <= idx or idx == len(seq) + LA - 1):
                                _, h, ob = pend.pop(0)
                                j, hf = h // 2, h % 2
                                S.op("pe", I("matmul", out=pb[5][0:64, 0:512], lhsT=ones_f[64:65, 0:64], rhs=rec[64:65, :], start=True, stop=True),
                                     reads=["rec", "ones_f"], writes=[("ps", 5)])
                                S.op("act", I("activation", out=OTs[0:64, :], in_=pb[ob][0:64, 0:512], func=AF.Copy), reads=[("ps", ob)], writes=["OTs"])
                                S.op("dve", I("tensor_tensor", out=catM[hf * 64:hf * 64 + 64, j, :], in0=OTs[0:64, :], in1=pb[5][0:64, 0:512], op=ALU.mult),
                                     reads=["OTs", ("ps", 5)], writes=[("catM", h)])
                        for i in range(4):
                            tt = 4 * G + i
                            outproj_partial(tt, lambda j, i=i: catM[:, j, i * 128:(i + 1) * 128], 3, woM3, False, [("catM", h) for h in range(6)],
                                            banks=((6, 7), (0, 1), (2, 3))[i % 3])

                    if stop == "M":
                        raise _Stop()
                    S.barrier(); A.reset()
                    wC3 = A.bf16(8 * 768).rearrange("p (c n) -> p c n", n=768)
                    woC3 = A.bf16(2 * 1024).rearrange("p (c n) -> p c n", n=1024)
                    catC = A.bf16(2 * SEQ).rearrange("p (j k) -> p j k", k=SEQ)
                    pbuf = [A.f32(514) for _ in range(2)]
                    ccsj = [A.f32(512) for _ in range(2)]; accj = [A.f32(512) for _ in range(2)]
                    wload("wC", wC3, w_in, 0, 8, 2688, 3456, "wC")
                    wload("wo", woC3, w_out, 768, 2, 0, 1024, "wo")
                    S.dma("sp", "cw", I("dma_start", out=cw[:], in_=cw_d[l]), writes=["cw"])
                    load_lnp(l, 0)
                    for j in range(2):
                        S.op("dve", I("memset", pbuf[j][:, 0:2], 0.0), writes=[("pbuf", j)])
                    def c_conv(G):
                        gsl = slice(G * 512, (G + 1) * 512)
                        hTr = [("hT", 4 * G + i) for i in range(4)]
                        for j in range(2):
                            bk = ((0, 1, 2), (3, 5, 7))[j]
                            for part in range(3):
                                c0 = part * 256 + j * 128
                                for c in range(8):
                                    S.op("pe", I("matmul", out=pb[bk[part]][:, 0:512], lhsT=wC3[:, c, c0:c0 + 128], rhs=hT[:, c, gsl],
                                                 start=(c == 0), stop=(c == 7)),
                                         reads=hTr + ["wC"], writes=[("ps", bk[part])])
                        for j in range(2):
                            bk = ((0, 1, 2), (3, 5, 7))[j]
                            pbj = pbuf[j]; acj = accj[j]
                            S.op("act", I("activation", out=ccsj[j], in_=pb[bk[1]][:, 0:512], func=AF.Copy), reads=[("ps", bk[1])], writes=[("ccs", j)])
                            S.op("dve", I("tensor_tensor", out=pbj[:, 2:514], in0=pb[bk[2]][:, 0:512], in1=ccsj[j], op=ALU.mult),
                                 reads=[("ps", bk[2]), ("ccs", j), ("pbuf", j)], writes=[("pbufm", j)])
                            S.op("dve", I("tensor_scalar", out=acj, in0=pbj[:, 2:514], scalar1=cw[:, j * 3 + 2:j * 3 + 3], scalar2=None, op0=ALU.mult),
                                 reads=[("pbufm", j), "cw"], writes=[("acc", j)])
                            S.op("dve", I("scalar_tensor_tensor", out=acj, in0=pbj[:, 1:513], scalar=cw[:, j * 3 + 1:j * 3 + 2], in1=acj, op0=ALU.mult, op1=ALU.add),
                                 reads=[("pbufm", j), ("pbuf", j), ("acc", j), "cw"], writes=[("acc", j)])
                            S.op("dve", I("scalar_tensor_tensor", out=acj, in0=pbj[:, 0:512], scalar=cw[:, j * 3:j * 3 + 1], in1=acj, op0=ALU.mult, op1=ALU.add),
                                 reads=[("pbufm", j), ("pbuf", j), ("acc", j), "cw"], writes=[("acc", j)])
                            S.op("dve", I("tensor_tensor", out=catC[:, j, gsl], in0=pb[bk[0]][:, 0:512], in1=acj, op=ALU.mult),
                                 reads=[("ps", bk[0]), ("acc", j)], writes=[("catC", j, G)])
                            S.op("dve", I("tensor_copy", out=pbj[:, 0:2], in_=pbj[:, 512:514]),
                                 reads=[("pbufm", j)], writes=[("pbuf", j)])

                    def c_tail(G):
                        for i in range(4):
                            tt = 4 * G + i
                            outproj_partial(tt, lambda j, tt=tt: catC[:, j, tt * 128:(tt + 1) * 128], 2, woC3, False, [("catC", 0, G), ("catC", 1, G)],
                                            banks=(6, 4))
                        emit_ln_batch([4 * G + i for i in range(4)])
                        for i in range(4):
                            emit_hT(4 * G + i)

                    for G in range(5):
                        if G < 4:
                            c_conv(G)
                        if G >= 1:
                            c_tail(G - 1)

                    if stop == "C":
                        raise _Stop()
                    S.barrier(); A.reset()
                    Gp = A.bf16(8 * SEQ).rearrange("p (j k) -> p j k", k=SEQ)
                    wD3 = A.bf16(8 * 1024).rearrange("p (j n) -> p j n", n=1024)
                    wUflat = [A.bf16(8 * 512) for _ in range(2)]
                    wU = [w_.rearrange("p (c n) -> p c n", n=512) for w_ in wUflat]
                    abuf = [A.f32(514) for _ in range(2)]
                    accs = [A.f32(512) for _ in range(2)]
                    gls = [A.f32(512) for _ in range(2)]
                    S.dma("sp", "fcw", I("dma_start", out=fcw[:], in_=fcw_d[l]), writes=["fcw"])
                    load_lnp(l, 2)
                    parts = [(0, 8), (8, 16), (16, 22)]
                    upc = [0]; itc = [0]; bai = [0]; bbi = [0]; dpi = [0]
                    for pi, (j0, j1) in enumerate(parts):
                        nj = j1 - j0
                        wload("wD", wD3, w_dn, j0 * 128, nj, 0, 1024, "wD")
                        its = []
                        for jp in range(j0 // 2, j1 // 2):
                            wb = upc[0] % 2; upc[0] += 1
                            for sub in range(2):
                                for G in range(4):
                                    its.append((jp, wb, sub, G))

                        def f_stage1(i):
                            jp, wb, sub, G = its[i]
                            wu = wU[wb]
                            if sub == 0 and G == 0:
                                nxt = [(jp2, wb2) for (jp2, wb2, s2, g2) in its if jp2 == jp + 1 and s2 == 0 and g2 == 0]
                                if nxt:
                                    S.dma("sp", f"wU{nxt[0][1]}", I("dma_start", out=wUflat[nxt[0][1]], in_=wup_b[l, nxt[0][0]]), writes=[("wU", nxt[0][1])])
                            jg = 2 * jp + sub
                            st = itc[0] % 2; itc[0] += 1
                            fset[i] = st
                            ab = abuf[st]; ac = accs[st]
                            gsl = slice(G * 512, (G + 1) * 512)
                            hTr = [("hT", 4 * G + q) for q in range(4)]
                            ba = bai[0] % 2; bai[0] += 1
                            bb = (2, 3, 5)[bbi[0] % 3]; bbi[0] += 1
                            fbb[i] = bb
                            for c in range(8):
                                S.op("pe", I("matmul", out=pb[ba][:, 0:512], lhsT=wu[:, c, sub * 128:(sub + 1) * 128], rhs=hT[:, c, gsl],
                                             start=(c == 0), stop=(c == 7)),
                                     reads=hTr + [("wU", wb)], writes=[("ps", ba)])
                            for c in range(8):
                                S.op("pe", I("matmul", out=pb[bb][:, 0:512], lhsT=wu[:, c, 256 + sub * 128:256 + (sub + 1) * 128], rhs=hT[:, c, gsl],
                                             start=(c == 0), stop=(c == 7)),
                                     reads=hTr + [("wU", wb)], writes=[("ps", bb)])
                            if G == 0:
                                S.op("pool", I("memset", ab[:, 0:2], 0.0), writes=[("abh", st)])
                            else:
                                S.op("pool", I("tensor_copy", out=ab[:, 0:2], in_=abuf[1 - st][:, 512:514]), reads=[("abm", 1 - st)], writes=[("abh", st)])
                            S.op("act", I("activation", out=ab[:, 2:514], in_=pb[ba][:, 0:512], func=AF.Copy),
                                 reads=[("ps", ba)], writes=[("abm", st)])
                            k3 = jg * 3
                            S.op("act", I("activation", out=ac, in_=pb[ba][:, 0:512], func=AF.Identity, scale=fcw[:, k3 + 2:k3 + 3]),
                                 reads=[("ps", ba), "fcw"], writes=[("acc", st)])
                            S.op("dve", I("scalar_tensor_tensor", out=ac, in0=ab[:, 1:513], scalar=fcw[:, k3 + 1:k3 + 2], in1=ac, op0=ALU.mult, op1=ALU.add),
                                 reads=[("abm", st), ("abh", st), ("acc", st), "fcw"], writes=[("acc", st)])
                            S.op("dve", I("scalar_tensor_tensor", out=ac, in0=ab[:, 0:512], scalar=fcw[:, k3:k3 + 1], in1=ac, op0=ALU.mult, op1=ALU.add),
                                 reads=[("abm", st), ("abh", st), ("acc", st), "fcw"], writes=[("acc", st)])

                        def f_stage2(i):
                            jp, wb, sub, G = its[i]
                            st = fset[i]; bb = fbb[i]
                            jj = 2 * jp + sub - j0
                            gsl = slice(G * 512, (G + 1) * 512)
                            S.op("act", I("activation", out=gls[st], in_=accs[st], func=AF.Gelu), reads=[("acc", st)], writes=[("gl", st)])
                            S.op("dve", I("tensor_tensor", out=Gp[:, jj, gsl], in0=pb[bb][:, 0:512], in1=gls[st], op=ALU.mult),
                                 reads=[("ps", bb), ("gl", st)], writes=[("Gp", jj, G)])

                        fset = {}; fbb = {}
                        S.dma("sp", f"wU{its[0][1]}", I("dma_start", out=wUflat[its[0][1]], in_=wup_b[l, its[0][0]]), writes=[("wU", its[0][1])])
                        for i in range(len(its) + 1):
                            if i < len(its):
                                f_stage1(i)
                            if i >= 1:
                                f_stage2(i - 1)
                        last = (pi == len(parts) - 1)
                        for t in range(NT):
                            G = t // 4
                            outproj_partial_reads = [("Gp", jj, G) for jj in range(nj)]
                            dbk = ((6, 7), (0, 1), (2, 3))[dpi[0] % 3]; dpi[0] += 1
                            for hf in range(2):
                                for jj in range(nj):
                                    S.op("pe", I("matmul", out=pb[dbk[hf]][:, 0:512], lhsT=Gp[:, jj, t * 128:(t + 1) * 128], rhs=wD3[:, jj, hf * 512:(hf + 1) * 512],
                                                 start=(jj == 0), stop=(jj == nj - 1)),
                                         reads=outproj_partial_reads + ["wD"], writes=[("ps", dbk[hf])])
                            for hf in range(2):
                                xsl = xs[:, t, hf * 512:(hf + 1) * 512]
                                if pi == 0:
                                    S.op("dve", I("scalar_tensor_tensor", out=xsl, in0=xsl, scalar=ALPHA, in1=pb[dbk[hf]][:, 0:512], op0=ALU.mult, op1=ALU.add),
                                         reads=[("ps", dbk[hf]), ("xs", t)], writes=[("xs", t)])
                                else:
                                    S.op("dve", I("tensor_tensor", out=xsl, in0=xsl, in1=pb[dbk[hf]][:, 0:512], op=ALU.add),
                                         reads=[("ps", dbk[hf]), ("xs", t)], writes=[("xs", t)])
                            if last and t % 4 == 3:
                                for tl in ([[t - 7, t - 6, t - 5, t - 4]] if t >= 7 else []) + ([[t - 3, t - 2, t - 1, t]] if t == NT - 1 else []):
                                    emit_ln_batch(tl)
                                    for t2 in tl:
                                        if l == depth - 1:
                                            S.dma("sp", "ostore", I("dma_start", out=out_d[s, t2 * 128:(t2 + 1) * 128, :], in_=xs[:, t2, :]),
                                                  reads=[("xs", t2)])
                                        else:
                                            emit_hT(t2)
                S.barrier()

        except _Stop:
            S.barrier()
            for t in range(NT):
                S.dma("sp", "ostore", I("dma_start", out=out_d[0, t * 128:(t + 1) * 128, :], in_=xs[:, t, :]), reads=[("xs", t)])
        S.barrier()
        S.emit()
    return nc


_NC_CACHE = {}


def _get_nc(nseq, depth):
    key = (nseq, depth)
    if key not in _NC_CACHE:
        _NC_CACHE[key] = build(nseq, depth)
    return _NC_CACHE[key]


def host_inputs(x_shard, w_in, conv_w, w_out, ln1_g, ln1_b, w_up, ffn_conv_w, w_down, ln2_g, ln2_b, rel_bias):
    cs_h, small_h, gC, ind_h, bidx_d, bidx_s, causal = _host_consts()
    f = lambda a: np.ascontiguousarray(a, dtype=np.float32)
    lnp = np.broadcast_to(np.stack([ln1_g, ln1_b, ln2_g, ln2_b], axis=1)[:, :, None, :], (DEPTH, 4, 128, D))
    cw = conv_w.reshape(DEPTH, 3, 2, 128).transpose(0, 3, 2, 1).reshape(DEPTH, 128, 6)
    fcw = ffn_conv_w.reshape(DEPTH, 3, 22, 128).transpose(0, 3, 2, 1).reshape(DEPTH, 128, 66)
    bd = np.where(causal[:, :, None], rel_bias[bidx_d], np.float32(NEG))
    bs = rel_bias[bidx_s]
    btile = np.stack([bd, bs], axis=0).transpose(1, 3, 0, 2).reshape(128, 6 * 2 * 128)
    cfar = np.broadcast_to(rel_bias[31][None, :], (128, 6))
    return {
        "x": f(x_shard), "w_in": f(w_in), "w_out": f(w_out), "w_up": f(w_up), "w_down": f(w_down),
        "lnp": f(lnp), "cw": f(cw), "fcw": f(fcw), "cs": f(cs_h.reshape(128, -1)), "btile": f(btile),
        "cfar": f(cfar), "small": f(small_h), "ind": f(ind_h),
    }


def kernel(x, w_in, conv_w, w_out, ln1_g, ln1_b, w_up, ffn_conv_w, w_down, ln2_g, ln2_b, rel_bias):
    args = [np.asarray(a) for a in (w_in, conv_w, w_out, ln1_g, ln1_b, w_up, ffn_conv_w, w_down, ln2_g, ln2_b, rel_bias)]
    x = np.asarray(x)
    B = x.shape[0]
    per = B // N_CORES
    nc = _get_nc(per, DEPTH)
    in_maps = []
    base = None
    for c in range(N_CORES):
        m = host_inputs(x[c * per:(c + 1) * per], *args) if base is None else dict(base, x=np.ascontiguousarray(x[c * per:(c + 1) * per], dtype=np.float32))
        if base is None:
            base = m
        in_maps.append(m)
    res = run_bass_kernel_spmd(nc, in_maps, core_ids=list(range(N_CORES)))
    return np.concatenate([np.asarray(r["out"]) for r in res.results], axis=0).astype(np.float32)
```

```python
import contextlib
import math
import numpy as np
import concourse.bass as bass
import concourse.mybir as mybir
from concourse.bass_utils import run_bass_kernel_spmd

F32 = mybir.dt.float32
BF16 = mybir.dt.bfloat16
AF = mybir.ActivationFunctionType
ALU = mybir.AluOpType
AX = mybir.AxisListType

N_CORES = 8
SEQ = 2048
D = 1024
NT = SEQ // 128
DEPTH = 2
IN_COLS = 3456
D_FF = 2816
ALPHA = (2.0 * DEPTH) ** 0.25
LN_EPS = 1e-5
NEG = -30000.0
BIG = 3.0e38

ENGS = ["pe", "act", "dve", "pool", "sp"]
SEM_CH = 12000


def I(name, *a, **k):
    return (name, a, k)


class Sched:
    def __init__(self, nc):
        self.nc = nc
        self.ops = {e: [] for e in ENGS}
        self.count = {e: 0 for e in ENGS}
        self.seen = {e: {} for e in ENGS}
        self.res = {}
        self.dma_cnt = {}

    def _deps(self, eng, reads, writes, skip_dma=None):
        deps = []
        for r in reads:
            st = self.res.get(r)
            if st and st[0] is not None:
                deps.append((st[0], True))
            if st and isinstance(r, tuple) and r[0] == "ps":
                for t in st[1]:
                    if t[1] != eng:
                        deps.append((t, True))
        for w in writes:
            st = self.res.get(w)
            if st:
                if st[0] is not None:
                    deps.append((st[0], False))
                for t in st[1]:
                    deps.append((t, False))
        seen = self.seen[eng]
        best = {}
        for tok, raw in deps:
            kind, key, val = tok
            if kind == "eng" and key == eng and not raw and eng == "pe":
                continue
            if kind == "dma" and key == skip_dma:
                continue
            k = (kind, key)
            if seen.get(k, 0) >= val:
                continue
            best[k] = max(best.get(k, 0), val)
        for k, v in best.items():
            seen[k] = v
        return [(k[0], k[1], v) for k, v in best.items()]

    def _commit(self, tok, reads, writes):
        for r in reads:
            st = self.res.setdefault(r, [None, []])
            st[1].append(tok)
        for w in writes:
            self.res[w] = [tok, []]

    def op(self, eng, fn, reads=(), writes=()):
        waits = self._deps(eng, reads, writes)
        self.count[eng] += 1
        tok = ("eng", eng, self.count[eng])
        self.ops[eng].append((waits, fn, tok))
        self._commit(tok, reads, writes)
        return tok

    def dma(self, q, sem, fn, reads=(), writes=()):
        waits = self._deps(q, reads, writes, skip_dma=sem)
        self.dma_cnt[sem] = self.dma_cnt.get(sem, 0) + 16
        tok = ("dma", sem, self.dma_cnt[sem])
        self.ops[q].append((waits, fn, tok))
        self._commit(tok, reads, writes)
        return tok

    def barrier(self):
        for e in ENGS:
            waits = []
            seen = self.seen[e]
            for e2 in ENGS:
                if e2 != e and self.count[e2] > seen.get(("eng", e2), 0):
                    waits.append(("eng", e2, self.count[e2]))
                    seen[("eng", e2)] = self.count[e2]
            for s, c in self.dma_cnt.items():
                if c > seen.get(("dma", s), 0):
                    waits.append(("dma", s, c))
                    seen[("dma", s)] = c
            if waits:
                self.ops[e].append((waits, None, None))
        self.res = {}

    def emit(self):
        nc = self.nc
        needed = {e: set() for e in ENGS}
        for e in ENGS:
            for waits, fn, tok in self.ops[e]:
                for kind, key, val in waits:
                    if kind == "eng":
                        needed[key].add(val)
        rank = {}
        for e in ENGS:
            rank[e] = {s: i + 1 for i, s in enumerate(sorted(needed[e]))}
        with contextlib.ExitStack() as es:
            esem = {}
            for e in ENGS:
                n = (len(rank[e]) + SEM_CH - 1) // SEM_CH
                esem[e] = [es.enter_context(nc.semaphore(f"s_{e}_{i}")) for i in range(max(n, 1))]
            dsem = {name: es.enter_context(nc.semaphore(f"d_{name}")) for name in self.dma_cnt}

            def lower(tok):
                kind, key, val = tok
                if kind == "eng":
                    r = rank[key][val]
                    return esem[key][(r - 1) // SEM_CH], (r - 1) % SEM_CH + 1
                return dsem[key], val

            def run(engobj, name):
                for waits, fn, tok in self.ops[name]:
                    for w in waits:
                        s, v = lower(w)
                        engobj.wait_ge(s, v)
                    if fn is None:
                        continue
                    ins = getattr(engobj, fn[0])(*fn[1], **fn[2])
                    if tok[0] == "eng":
                        if tok[2] in rank[name]:
                            s, v = lower(tok)
                            ins.then_inc(s, 1)
                    else:
                        s, v = lower(tok)
                        ins.then_inc(s, 16)

            with nc.Block() as block:
                @block.tensor
                def _(e):
                    run(e, "pe")

                @block.scalar
                def _(e):
                    run(e, "act")

                @block.vector
                def _(e):
                    run(e, "dve")

                @block.gpsimd
                def _(e):
                    run(e, "pool")

                @block.sync
                def _(e):
                    run(e, "sp")


class Arena:
    def __init__(self, t, words):
        self.t = t
        self.words = words
        self.off = 0

    def reset(self):
        self.off = 0

    def f32(self, n):
        assert self.off + n <= self.words, ("arena overflow", self.off + n, self.words)
        v = self.t[:, self.off:self.off + n]
        self.off += n
        return v

    def bf16(self, n):
        w = (n + 1) // 2
        assert self.off + w <= self.words, ("arena overflow", self.off + w, self.words)
        v = self.t[:, self.off:self.off + w].bitcast(BF16)
        self.off += w
        return v


SM_OWN = 0
SM_MASK = 128
SM_VSC = 256
SM_XI = 262
SM_ZETA = 268
SM_EPS = 274
SM_GC = 275
SM_W = 280


def _t5_bucket(dist):
    n = np.maximum(dist, 0)
    nf = np.maximum(n, 1).astype(np.float32)
    large = 16 + (np.log(nf / np.float32(16)) / np.float32(math.log(128 / 16)) * np.float32(16)).astype(np.int32)
    large = np.minimum(large, 31)
    return np.where(n < 16, n, large)


def _host_consts():
    p = np.arange(128)
    inv = (10000.0 ** (-np.arange(0, 64, 2, dtype=np.float32) / np.float32(64))).astype(np.float32)
    pos = (np.arange(NT)[None, :] * 128 + p[:, None]).astype(np.float32)
    ang = pos[:, :, None] * inv[None, None, :]
    cs = np.stack([np.concatenate([np.cos(ang), np.cos(ang)], axis=-1), np.concatenate([-np.sin(ang), np.sin(ang)], axis=-1)], axis=1).astype(np.float32)
    small = np.zeros((128, SM_W), np.float32)
    own = np.zeros((NT, 8), np.float32)
    for qt in range(NT):
        o = qt // 2
        own[qt, o] = BIG
        own[qt, o + 1:] = -BIG
    small[:, SM_OWN:SM_OWN + 128] = own.reshape(1, 128)
    e = p[:, None]; c = p[None, :]
    small[:, SM_MASK:SM_MASK + 128] = np.where(c >= e, 0.125, 0.0)
    g = 1.0 - 2.0 ** (-5.0 - np.arange(6))
    small[:, SM_VSC:SM_VSC + 6] = g[None, :] ** (-(p[:, None] + 1.0))
    small[:, SM_XI:SM_XI + 6] = g[None, :] ** (p[:, None] + 1.0)
    small[:, SM_ZETA:SM_ZETA + 6] = 0.125 * g[None, :] ** (127.0 - p[:, None])
    small[:, SM_EPS] = LN_EPS
    for j in range(3):
        small[0:64, SM_GC + j] = g[2 * j] ** 128.0
        small[64:128, SM_GC + j] = g[2 * j + 1] ** 128.0
    gC = [float(x) for x in g ** 128.0]
    ind = (np.arange(SEQ)[None, :] // 256 == np.arange(8)[:, None]).astype(np.float32)
    bidx_d = _t5_bucket(c - e)
    bidx_s = _t5_bucket(128 + c - e)
    causal = (c >= e)
    return cs, small, gC, ind, bidx_d, bidx_s, causal


class _Stop(Exception):
    pass


def build(nseq=4, depth=DEPTH, stop=None):
    cs_h, small_h, gC, ind_h, _, _, _ = _host_consts()
    nc = bass.Bass("TRN2", target_bir_lowering=False)
    x_d = nc.dram_tensor("x", [nseq, SEQ, D], F32, kind="ExternalInput").ap()
    w_in_d = nc.dram_tensor("w_in", [DEPTH, D, IN_COLS], F32, kind="ExternalInput").ap()
    w_out_d = nc.dram_tensor("w_out", [DEPTH, D, D], F32, kind="ExternalInput").ap()
    w_up_d = nc.dram_tensor("w_up", [DEPTH, D, 2 * D_FF], F32, kind="ExternalInput").ap()
    w_dn_d = nc.dram_tensor("w_down", [DEPTH, D_FF, D], F32, kind="ExternalInput").ap()
    lnp_d = nc.dram_tensor("lnp", [DEPTH, 4, 128, D], F32, kind="ExternalInput").ap()
    cw_d = nc.dram_tensor("cw", [DEPTH, 128, 6], F32, kind="ExternalInput").ap()
    fcw_d = nc.dram_tensor("fcw", [DEPTH, 128, 66], F32, kind="ExternalInput").ap()
    cs_d = nc.dram_tensor("cs", [128, 2 * NT * 64], F32, kind="ExternalInput").ap()
    bt_d = nc.dram_tensor("btile", [128, 6 * 2 * 128], F32, kind="ExternalInput").ap()
    cfar_d = nc.dram_tensor("cfar", [128, 6], F32, kind="ExternalInput").ap()
    small_d = nc.dram_tensor("small", [128, SM_W], F32, kind="ExternalInput").ap()
    ind_d = nc.dram_tensor("ind", [8, SEQ], F32, kind="ExternalInput").ap()
    out_d = nc.dram_tensor("out", [nseq, SEQ, D], F32, kind="ExternalOutput").ap()
    win_b = nc.dram_tensor("win_b", [DEPTH, D, IN_COLS], BF16).ap()
    wout_b = nc.dram_tensor("wout_b", [DEPTH, D, D], BF16).ap()
    wup_b = nc.dram_tensor("wup_b", [DEPTH, 11, 128, 8 * 512], BF16).ap()
    wdn_b = nc.dram_tensor("wdn_b", [DEPTH, D_FF, D], BF16).ap()
    ind_b = nc.dram_tensor("ind_b", [8, SEQ], BF16).ap()

    S = Sched(nc)

    def ck(tag):
        if stop == tag:
            raise _Stop()
    ARENA_W = 20992
    with contextlib.ExitStack() as es:
        def sb(name, shape, dt=F32):
            return es.enter_context(nc.sbuf_tensor("sb_" + name, shape, dt))

        xs = sb("xs", [128, NT, D])
        hT = sb("hT", [128, 8, SEQ], BF16)
        cs = sb("cs", [128, 2, NT, 64])
        bt = sb("bt", [128, 6, 2, 128])
        cfar = sb("cfar", [128, 6])
        small = sb("small", [128, SM_W])
        lnp = sb("lnp", [128, 2, D])
        cw = sb("cw", [128, 6])
        fcw = sb("fcw", [128, 66])
        ident = sb("ident", [128, 128], BF16)
        ones_f = sb("ones_f", [128, 64])
        xb = [sb("xb0", [128, D], BF16), sb("xb1", [128, D], BF16)]
        lnst = sb("lnst", [128, 4, 16])
        arena_t = sb("arena", [128, ARENA_W])
        A = Arena(arena_t, ARENA_W)
        pb = [es.enter_context(nc.psum_tensor(f"pb{i}", [128, 512], F32)) for i in range(8)]

        def pbb(i):
            return pb[i][:, :].bitcast(BF16)

        S.dma("sp", "c_cs", I("dma_start", out=cs[:].rearrange("p a t j -> p (a t j)"), in_=cs_d), writes=["cs"])
        S.dma("sp", "c_bt", I("dma_start", out=bt[:].rearrange("p h a q -> p (h a q)"), in_=bt_d), writes=["bt"])
        S.dma("sp", "c_cf", I("dma_start", out=cfar[:], in_=cfar_d), writes=["cfar"])
        S.dma("sp", "c_sm", I("dma_start", out=small[:], in_=small_d), writes=["small"])
        identf = A.f32(128)
        S.op("pool", I("memset", identf[:], 0.0), writes=["identf"])
        S.op("pool", I("affine_select", out=identf[:], in_=identf[:], pattern=[[-1, 128]],
                                              compare_op=ALU.not_equal, fill=1.0, base=0, channel_multiplier=1),
             reads=["identf"], writes=["identf"])
        S.op("dve", I("tensor_copy", out=ident[:], in_=identf[:]), reads=["identf"], writes=["ident"])
        S.op("pool", I("memset", ones_f[:], 1.0), writes=["ones_f"])
        for h in range(6):
            S.op("dve", I("tensor_scalar", out=bt[:, h, :, :], in0=bt[:, h, :, :], scalar1=cfar[:, h:h + 1], scalar2=None, op0=ALU.subtract),
                 reads=["bt", "cfar"], writes=["bt"])
        S.barrier()

        ownmask = small[:, SM_OWN:SM_OWN + 128].rearrange("p (t n) -> p t n", n=8)
        mask8 = small[:, SM_MASK:SM_MASK + 128]
        vsc1 = small[:, SM_VSC:SM_VSC + 6]
        xi = small[:, SM_XI:SM_XI + 6]
        zeta8 = small[:, SM_ZETA:SM_ZETA + 6]
        eps_t = small[:, SM_EPS:SM_EPS + 1]
        gct = small[:, SM_GC:SM_GC + 3]

        def bc3(ap2, n):
            a = ap2.shape[1]
            return ap2.unsqueeze(2).to_broadcast([128, a, n])

        def bcm(ap2, m):
            n = ap2.shape[1]
            return ap2.unsqueeze(1).to_broadcast([128, m, n])

        def wload(sem, dst3, scr2, r0, nrow_chunks, c0, c1, res):
            for k in range(nrow_chunks):
                S.dma("sp", sem, I("dma_start", out=dst3[:, k, :], in_=scr2[r0 + k * 128:r0 + (k + 1) * 128, c0:c1]), writes=[res])

        def emit_hT(t):
            b = t % 2
            S.op("act", I("activation", out=xb[b][:], in_=xs[:, t, :], func=AF.Copy),
                 reads=[("xs", t)], writes=[("xb", b)])
            p4 = pbb(4)
            for c in range(8):
                S.op("pe", I("transpose", out=p4[:, c * 128:(c + 1) * 128], in_=xb[b][:, c * 128:(c + 1) * 128],
                                                     identity=ident[:]),
                     reads=[("xb", b), "ident"], writes=[("ps", 4)])
            S.op("dve", I("tensor_copy", out=hT[:, :, t * 128:(t + 1) * 128],
                                                in_=p4.rearrange("p (c k) -> p c k", k=128)),
                 reads=[("ps", 4)], writes=[("hT", t)])

        def emit_ln_batch(tiles):
            n = len(tiles)
            for i, t in enumerate(tiles):
                S.op("dve", I("bn_stats", lnst[:, i, 0:6], xs[:, t, 0:512]), reads=[("xs", t)], writes=[("lnst0", i)])
                S.op("dve", I("bn_stats", lnst[:, i, 6:12], xs[:, t, 512:1024]), reads=[("xs", t)], writes=[("lnst1", i)])
                S.op("dve", I("bn_aggr", lnst[:, i, 12:14], lnst[:, i, 0:12]), reads=[("lnst0", i), ("lnst1", i)], writes=[("lnmv", i)])
            mvr = [("lnmv", i) for i in range(n)]
            S.op("act", I("activation", out=lnst[:, 0:n, 14], in_=lnst[:, 0:n, 13], func=AF.Ln, bias=eps_t, scale=1.0), reads=mvr, writes=["lnr"])
            S.op("act", I("activation", out=lnst[:, 0:n, 14], in_=lnst[:, 0:n, 14], func=AF.Exp, scale=-0.5), reads=["lnr"], writes=["lnr"])
            S.op("dve", I("scalar_tensor_tensor", out=lnst[:, 0:n, 15], in0=lnst[:, 0:n, 12], scalar=-1.0, in1=lnst[:, 0:n, 14], op0=ALU.mult, op1=ALU.mult),
                 reads=["lnr"] + mvr, writes=["lnn"])
            for i, t in enumerate(tiles):
                xt = xs[:, t, :]
                S.op("act", I("activation", out=xt, in_=xt, func=AF.Identity, bias=lnst[:, i, 15:16], scale=lnst[:, i, 14:15]),
                     reads=[("xs", t), "lnn", "lnr"], writes=[("xs", t)])
            for i, t in enumerate(tiles):
                xt = xs[:, t, :]
                S.op("dve", I("tensor_tensor", out=xt, in0=xt, in1=lnp[:, 0, :], op=ALU.mult), reads=[("xs", t), "lnp"], writes=[("xs", t)])
                S.op("dve", I("tensor_tensor", out=xt, in0=xt, in1=lnp[:, 1, :], op=ALU.add), reads=[("xs", t), "lnp"], writes=[("xs", t)])

        def load_lnp(l, gi):
            S.dma("sp", "lnp", I("dma_start", out=lnp[:, 0, :], in_=lnp_d[l, gi]),
                  writes=["lnp"])
            S.dma("sp", "lnp", I("dma_start", out=lnp[:, 1, :], in_=lnp_d[l, gi + 1]),
                  writes=["lnp"])

        def outproj_partial(t, lhs_fn, nk, wo3, first, lhs_reads, banks=(6, 7)):
            for hf in range(2):
                for j in range(nk):
                    S.op("pe", I("matmul", out=pb[banks[hf]][:, 0:512], lhsT=lhs_fn(j),
                                                               rhs=wo3[:, j, hf * 512:(hf + 1) * 512],
                                                               start=(j == 0), stop=(j == nk - 1)),
                         reads=list(lhs_reads) + ["wo"], writes=[("ps", banks[hf])])
            for hf in range(2):
                xsl = xs[:, t, hf * 512:(hf + 1) * 512]
                if first:
                    S.op("dve", I("scalar_tensor_tensor", out=xsl, in0=xsl, scalar=ALPHA, in1=pb[banks[hf]][:, 0:512],
                                                                                op0=ALU.mult, op1=ALU.add),
                         reads=[("ps", banks[hf]), ("xs", t)], writes=[("xs", t)])
                else:
                    S.op("dve", I("tensor_tensor", out=xsl, in0=xsl, in1=pb[banks[hf]][:, 0:512], op=ALU.add),
                         reads=[("ps", banks[hf]), ("xs", t)], writes=[("xs", t)])

        NSL, LEAD = 12, 6
        pf = [A.f32(768) for _ in range(NSL)]
        pbf = [A.bf16(768) for _ in range(NSL)]
        pieces = []
        for l in range(DEPTH):
            for src, dstb, nrows, ncols in ((w_in_d[l], win_b[l], D, IN_COLS), (w_out_d[l], wout_b[l], D, D)):
                for k in range(nrows // 128):
                    for p0 in range(0, ncols, 768):
                        n = min(768, ncols - p0)
                        rs = slice(k * 128, (k + 1) * 128)
                        pieces.append(([(0, n, src[rs, p0:p0 + n])], n, dstb[rs, p0:p0 + n]))
            for jp in range(11):
                for c in range(8):
                    rs = slice(c * 128, (c + 1) * 128)
                    pieces.append(([(0, 256, w_up_d[l][rs, jp * 256:(jp + 1) * 256]),
                                    (256, 256, w_up_d[l][rs, D_FF + jp * 256:D_FF + (jp + 1) * 256])], 512,
                                   wup_b[l, jp, :, c * 512:(c + 1) * 512]))
            for k in range(D_FF // 128):
                for p0 in range(0, D, 768):
                    n = min(768, D - p0)
                    rs = slice(k * 128, (k + 1) * 128)
                    pieces.append(([(0, n, w_dn_d[l][rs, p0:p0 + n])], n, wdn_b[l][rs, p0:p0 + n]))
        cast_eng = ["pool", "act", "dve"]
        npc = len(pieces)
        for i in range(npc + LEAD):
            if i < npc:
                sl = i % NSL
                for c0_, n_, src_ in pieces[i][0]:
                    S.dma("sp", f"pl{sl}", I("dma_start", out=pf[sl][:, c0_:c0_ + n_], in_=src_), writes=[("pf", sl)])
            j = i - LEAD
            if j >= 0:
                sl = j % NSL
                n_ = pieces[j][1]
                ce = cast_eng[j % 3]
                if ce == "act":
                    S.op("act", I("activation", out=pbf[sl][:, 0:n_], in_=pf[sl][:, 0:n_], func=AF.Copy), reads=[("pf", sl)], writes=[("pbf", sl)])
                else:
                    S.op(ce, I("tensor_copy", out=pbf[sl][:, 0:n_], in_=pf[sl][:, 0:n_]), reads=[("pf", sl)], writes=[("pbf", sl)])
                S.dma("sp", f"ps{sl}", I("dma_start", out=pieces[j][2], in_=pbf[sl][:, 0:n_]), reads=[("pbf", sl)])
        S.dma("sp", "pl0", I("dma_start", out=pf[0][64:72, 0:768], in_=ind_d[:, 0:768]), writes=[("pf", 0)])
        S.dma("sp", "pl1", I("dma_start", out=pf[1][64:72, 0:768], in_=ind_d[:, 768:1536]), writes=[("pf", 1)])
        S.dma("sp", "pl2", I("dma_start", out=pf[2][64:72, 0:512], in_=ind_d[:, 1536:2048]), writes=[("pf", 2)])
        for q_, (o_, n_) in enumerate(((0, 768), (768, 768), (1536, 512))):
            S.op("dve", I("tensor_copy", out=pbf[q_][64:72, 0:n_], in_=pf[q_][64:72, 0:n_]), reads=[("pf", q_)], writes=[("pbf", q_)])
            S.dma("sp", f"ps{q_}", I("dma_start", out=ind_b[:, o_:o_ + n_], in_=pbf[q_][64:72, 0:n_]), reads=[("pbf", q_)])
        S.barrier()

        try:
            for s in range(nseq):
                for t in range(NT):
                    S.dma("sp", f"xload{t}", I("dma_start", out=xs[:, t, :], in_=x_d[s, t * 128:(t + 1) * 128, :]),
                          writes=[("xs", t)])
                for t in range(NT):
                    emit_hT(t)
                if stop == "hT":
                    raise _Stop()
                for l in range(depth):
                    w_in = win_b[l]; w_out = wout_b[l]; w_dn = wdn_b[l]
                    S.barrier(); A.reset()
                    wR3 = A.bf16(8 * 1536).rearrange("p (c n) -> p c n", n=1536)
                    woR3 = A.bf16(3 * 1024).rearrange("p (c n) -> p c n", n=1024)
                    qrot = A.bf16(384)
                    krot = [A.bf16(384) for _ in range(2)]
                    vs1 = [A.bf16(384).rearrange("p (h d) -> p h d", d=64) for _ in range(2)]
                    vz = [A.bf16(384).rearrange("p (h d) -> p h d", d=64) for _ in range(2)]
                    qkT = [A.bf16(768) for _ in range(2)]
                    sg = [A.f32(384) for _ in range(2)]
                    scT = A.bf16(768).rearrange("p (par j c) -> p par j c", par=2, c=128)
                    ycat = [A.bf16(384) for _ in range(2)]; catT = A.bf16(384)
                    stb = [A.bf16(192).rearrange("p (j v) -> p j v", v=64) for _ in range(2)]
                    stf = A.f32(192).rearrange("p (j v) -> p j v", v=64)
                    tq = [A.f32(384).rearrange("p (h d) -> p h d", d=64) for _ in range(4)]
                    eg = A.f32(384)
                    yr = A.f32(384).rearrange("p (h d) -> p h d", d=64)
                    sq = A.f32(384).rearrange("p (h d) -> p h d", d=64)
                    st6 = A.f32(48)
                    s1 = st6[:, 0:6]; s2 = st6[:, 8:14]; mean = st6[:, 16:22]; msq = st6[:, 24:30]; var = st6[:, 32:38]; rstd = st6[:, 40:46]
                    wload("wR", wR3, w_in, 0, 8, 0, 1536, "wR")
                    wload("wo", woR3, w_out, 0, 3, 0, 1024, "wo")
                    S.op("dve", I("memset", stf, 0.0), writes=["stf"] + [("stf", h) for h in range(6)])
                    S.op("dve", I("memset", stb[0], 0.0), writes=[("stb", 0)])

                    def r_proj(t):
                        tsl = slice(t * 128, (t + 1) * 128)
                        for c in range(8):
                            for g4 in range(4):
                                S.op("pe", I("matmul", out=pb[g4][:, 0:384], lhsT=hT[:, c, tsl], rhs=wR3[:, c, g4 * 384:(g4 + 1) * 384],
                                             start=(c == 0), stop=(c == 7)),
                                     reads=[("hT", t), "wR"], writes=[("ps", g4)])

                    def r_front_ew(t):
                        p = t % 2
                        cos2 = bcm(cs[:, 0, t, :], 6)
                        nsin = bcm(cs[:, 1, t, 0:32], 6); psin = bcm(cs[:, 1, t, 32:64], 6)
                        for bank, dst, nm, o in ((0, qrot, "qrot", 0), (1, krot[p], ("krot", p), 2)):
                            pq = pb[bank][:, 0:384].rearrange("p (h d) -> p h d", d=64)
                            ra = tq[o]; rb = tq[o + 1]
                            S.op("dve", I("tensor_tensor", out=ra, in0=pq, in1=cos2, op=ALU.mult), reads=[("ps", bank), "cs"], writes=[("tq", o)])
                            S.op("dve", I("tensor_tensor", out=rb[:, :, 0:32], in0=pq[:, :, 32:64], in1=nsin, op=ALU.mult), reads=[("ps", bank), "cs"], writes=[("tq", o + 1, 0)])
                            S.op("dve", I("tensor_tensor", out=rb[:, :, 32:64], in0=pq[:, :, 0:32], in1=psin, op=ALU.mult), reads=[("ps", bank), "cs"], writes=[("tq", o + 1, 1)])
                            S.op("dve", I("tensor_tensor", out=dst.rearrange("p (h d) -> p h d", d=64), in0=ra, in1=rb, op=ALU.add),
                                 reads=[("tq", o), ("tq", o + 1, 0), ("tq", o + 1, 1)], writes=[(nm, 1), (nm, 2)])
                        pv = pb[2][:, 0:384].rearrange("p (h d) -> p h d", d=64)
                        S.op("dve", I("tensor_tensor", out=vs1[p], in0=pv, in1=bc3(vsc1, 64), op=ALU.mult), reads=[("ps", 2), "small"], writes=[("vs1", p)])
                        S.op("dve", I("tensor_tensor", out=vz[p], in0=pv, in1=bc3(zeta8, 64), op=ALU.mult), reads=[("ps", 2), "small"], writes=[("vz", p)])
                        S.op("act", I("activation", out=eg, in_=pb[3][:, 0:384], func=AF.Exp, scale=-1.0), reads=[("ps", 3)], writes=["eg"])
                        S.op("act", I("activation", out=eg, in_=eg, func=AF.Identity, bias=ones_f[:, 0:1], scale=1.0), reads=["eg", "ones_f"], writes=["eg"])
                        S.op("dve", I("reciprocal", out=eg, in_=eg), reads=["eg"], writes=["eg"])
                        S.op("dve", I("tensor_tensor", out=sg[p], in0=pb[3][:, 0:384], in1=eg, op=ALU.mult), reads=[("ps", 3), "eg"], writes=[("sg", p)])

                    def r_front_tr(t):
                        p = t % 2
                        p4 = pbb(4)
                        for j in range(3):
                            S.op("pe", I("transpose", out=p4[:, j * 128:(j + 1) * 128], in_=qrot[:, j * 128:(j + 1) * 128], identity=ident[:]),
                                 reads=[("qrot", 1), ("qrot", 2), "ident"], writes=[("ps", 4)])
                            S.op("pe", I("transpose", out=p4[:, (3 + j) * 128:(4 + j) * 128], in_=krot[p][:, j * 128:(j + 1) * 128], identity=ident[:]),
                                 reads=[(("krot", p), 1), (("krot", p), 2), "ident"], writes=[("ps", 4)])
                        S.op("act", I("activation", out=qkT[p], in_=p4[:, 0:768], func=AF.Copy), reads=[("ps", 4)], writes=[("qkT", p)])

                    def r_scores(t):
                        p = t % 2
                        for h in range(6):
                            j, hf = h // 2, h % 2
                            pr = slice(hf * 64, hf * 64 + 64)
                            S.op("pe", I("matmul", out=pb[5 + hf][:, j * 128:(j + 1) * 128], lhsT=qkT[p][pr, (3 + j) * 128:(4 + j) * 128],
                                         rhs=qkT[p][pr, j * 128:(j + 1) * 128], start=True, stop=True),
                                 reads=[("qkT", p)], writes=[("ps", 5 + hf)])

                    def r_mask(t):
                        for hf in range(2):
                            S.op("dve", I("tensor_tensor", out=scT[:, hf, :, :], in0=pb[5 + hf][:, 0:384].rearrange("p (h c) -> p h c", c=128),
                                          in1=bcm(mask8, 3), op=ALU.mult),
                                 reads=[("ps", 5 + hf), "small"], writes=[("scT", hf)])

                    def r_o_ds(t):
                        p = t % 2
                        sb_cur = stb[t % 2]
                        for h in range(6):
                            j, hf = h // 2, h % 2
                            pr = slice(hf * 64, hf * 64 + 64)
                            S.op("pe", I("matmul", out=pb[5][:, h * 64:(h + 1) * 64], lhsT=scT[:, hf, j, :], rhs=vs1[p][:, h, :], start=True, stop=False),
                                 reads=[("scT", 0), ("scT", 1), ("vs1", p)], writes=[("ps", 5)])
                            S.op("pe", I("matmul", out=pb[5][:, h * 64:(h + 1) * 64], lhsT=qkT[p][pr, j * 128:(j + 1) * 128],
                                         rhs=sb_cur[pr, j, :], start=False, stop=True),
                                 reads=[("qkT", p), ("stb", t % 2)], writes=[("ps", 5)])
                        for h in range(6):
                            j, hf = h // 2, h % 2
                            S.op("pe", I("matmul", out=pb[7][hf * 64:hf * 64 + 64, j * 64:(j + 1) * 64], lhsT=krot[p][:, h * 64:(h + 1) * 64], rhs=vz[p][:, h, :],
                                         start=True, stop=True),
                                 reads=[(("krot", p), 1), (("krot", p), 2), ("vz", p)], writes=[("ps", 7)])

                    def r_state_norm(t):
                        p = t % 2
                        sb_nxt = stb[(t + 1) % 2]
                        for j in range(3):
                            S.op("dve", I("scalar_tensor_tensor", out=stf[:, j, :], in0=stf[:, j, :], scalar=gct[:, j:j + 1],
                                          in1=pb[7][:, j * 64:(j + 1) * 64], op0=ALU.mult, op1=ALU.add),
                                 reads=[("ps", 7), ("stf", 2 * j), "small"], writes=[("stf", 2 * j), ("stf", 2 * j + 1)])
                        S.op("act", I("activation", out=sb_nxt, in_=stf, func=AF.Copy),
                             reads=[("stf", h) for h in range(6)], writes=[("stb", (t + 1) % 2)])
                        po = pb[5][:, 0:384].rearrange("p (h d) -> p h d", d=64)
                        S.op("dve", I("tensor_tensor", out=yr, in0=po, in1=bc3(xi, 64), op=ALU.mult), reads=[("ps", 5), "small"], writes=["yr"])
                        S.op("dve", I("tensor_reduce", out=s1, in_=yr, axis=AX.X, op=ALU.add), reads=["yr"], writes=["s1"])
                        S.op("dve", I("tensor_tensor", out=sq, in0=yr, in1=yr, op=ALU.mult), reads=["yr"], writes=["sq"])
                        S.op("dve", I("tensor_reduce", out=s2, in_=sq, axis=AX.X, op=ALU.add), reads=["sq"], writes=["s2"])
                        S.op("dve", I("tensor_scalar", out=mean, in0=s1, scalar1=1.0 / 64, scalar2=None, op0=ALU.mult), reads=["s1"], writes=["mean"])
                        S.op("dve", I("tensor_tensor", out=msq, in0=mean, in1=mean, op=ALU.mult), reads=["mean"], writes=["msq"])
                        S.op("dve", I("tensor_scalar", out=var, in0=s2, scalar1=1.0 / 64, scalar2=None, op0=ALU.mult), reads=["s2"], writes=["var"])
                        S.op("dve", I("tensor_tensor", out=var, in0=var, in1=msq, op=ALU.subtract), reads=["var", "msq"], writes=["var"])
                        S.op("dve", I("tensor_scalar", out=var, in0=var, scalar1=0.0, scalar2=None, op0=ALU.max), reads=["var"], writes=["var"])
                        S.op("act", I("activation", out=rstd, in_=var, func=AF.Ln, bias=eps_t, scale=1.0), reads=["var", "small"], writes=["rstd"])
                        S.op("act", I("activation", out=rstd, in_=rstd, func=AF.Exp, scale=-0.5), reads=["rstd"], writes=["rstd"])
                        S.op("dve", I("tensor_tensor", out=yr, in0=yr, in1=bc3(mean, 64), op=ALU.subtract), reads=["yr", "mean", "s1", "sq"], writes=["yr"])
                        S.op("dve", I("tensor_tensor", out=yr, in0=yr, in1=bc3(rstd, 64), op=ALU.mult), reads=["yr", "rstd"], writes=["yr"])
                        S.op("dve", I("tensor_tensor", out=ycat[p], in0=yr.rearrange("p h d -> p (h d)"), in1=sg[p], op=ALU.mult), reads=["yr", ("sg", p)], writes=[("ycat", p)])

                    def r_out(t):
                        p4 = pbb(4)
                        for j in range(3):
                            S.op("pe", I("transpose", out=p4[:, j * 128:(j + 1) * 128], in_=ycat[t % 2][:, j * 128:(j + 1) * 128], identity=ident[:]),
                                 reads=[("ycat", t % 2), "ident"], writes=[("ps", 4)])
                        S.op("act", I("activation", out=catT, in_=p4[:, 0:384], func=AF.Copy), reads=[("ps", 4)], writes=["catT"])
                        outproj_partial(t, lambda j: catT[:, j * 128:(j + 1) * 128], 3, woR3, True, ["catT"])

                    for t in range(NT + 2):
                        back = 1 <= t <= NT
                        if t < NT:
                            r_proj(t)
                        if back:
                            r_scores(t - 1)
                        if t < NT:
                            r_front_ew(t)
                        if back:
                            r_mask(t - 1)
                        if t < NT:
                            r_front_tr(t)
                        if t >= 2:
                            r_out(t - 2)
                        if back:
                            r_o_ds(t - 1)
                            r_state_norm(t - 1)

                    if stop == "R":
                        raise _Stop()
                    S.barrier(); A.reset()
                    wM3 = A.bf16(8 * 1152).rearrange("p (c n) -> p c n", n=1152)
                    woM3 = A.bf16(3 * 1024).rearrange("p (c n) -> p c n", n=1024)
                    KT = A.bf16(6 * SEQ).rearrange("p (h k) -> p h k", k=SEQ)
                    QT = A.bf16(6 * 512).rearrange("p (h k) -> p h k", k=512)
                    Va = A.bf16(NT * 6 * 65 + 2).rearrange("p (t h d) -> p t h d", h=6, d=65) if False else None
                    Vraw = A.bf16(NT * 6 * 66)
                    Va = Vraw.rearrange("p (t h d) -> p t h d", h=6, d=66)
                    NPT = 4
                    PT = [A.bf16(512) for _ in range(NPT)]
                    OTs = A.f32(512); rec = OTs
                    catM = A.bf16(3 * 512).rearrange("p (j k) -> p j k", k=512)
                    NM = [A.bf16(6 * 72).rearrange("p (h n) -> p h n", n=72) for _ in range(4)]
                    gsb = [A.f32(48).rearrange("p (h n) -> p h n", n=8) for _ in range(4)]
                    m8 = [A.f32(48).rearrange("p (h n) -> p h n", n=8) for _ in range(4)]
                    ge = [A.f32(48).rearrange("p (h n) -> p h n", n=8) for _ in range(4)]
                    km = A.f32(48).rearrange("p (h n) -> p h n", n=8)
                    kmT = A.bf16(48).rearrange("p (h n) -> p h n", n=8)
                    wload("wM", wM3, w_in, 0, 8, 1536, 2688, "wM")
                    wload("wo", woM3, w_out, 384, 3, 0, 1024, "wo")
                    for h in range(6):
                        S.dma("sp", f"ind{h}", I("dma_start", out=KT[64:72, h, :], in_=ind_b), writes=[("KTi", h)])
                    S.op("dve", I("memset", Vraw, 1.0), writes=["Va1"])
                    for i in range(4):
                        S.op("dve", I("memset", NM[i], 0.0), writes=[("NM", i)])
                    S.op("dve", I("memset", kmT, 0.0), writes=["kmT"])
                    S.op("dve", I("memset", km, 0.0), writes=["km"])
                    sidx = [0]; pidx = [0]; oidx = [0]
                    for G in range(4):
                        gsl = slice(G * 512, (G + 1) * 512)
                        hTr = [("hT", 4 * G + i) for i in range(4)]
                        def m_proj_q(G_):
                            gs_ = slice(G_ * 512, (G_ + 1) * 512)
                            hr_ = [("hT", 4 * G_ + q) for q in range(4)]
                            for h in range(6):
                                bank = h % 2
                                for c in range(8):
                                    S.op("pe", I("matmul", out=pb[bank][0:64, 0:512], lhsT=wM3[:, c, h * 64:h * 64 + 64], rhs=hT[:, c, gs_],
                                                 start=(c == 0), stop=(c == 7)),
                                         reads=hr_ + ["wM"], writes=[("ps", bank)])
                                S.op("act", I("activation", out=QT[0:64, h, :], in_=pb[bank][0:64, 0:512], func=AF.Identity, scale=0.125),
                                     reads=[("ps", bank)], writes=[("QT", h)])

                        def m_proj_kv(G_):
                            gs_ = slice(G_ * 512, (G_ + 1) * 512)
                            hr_ = [("hT", 4 * G_ + q) for q in range(4)]
                            for h in range(6):
                                bank = 2 + (h % 2)
                                c0 = 384 + h * 64
                                for c in range(8):
                                    S.op("pe", I("matmul", out=pb[bank][0:64, 0:512], lhsT=wM3[:, c, c0:c0 + 64], rhs=hT[:, c, gs_],
                                                 start=(c == 0), stop=(c == 7)),
                                         reads=hr_ + ["wM"], writes=[("ps", bank)])
                                S.op("act", I("activation", out=KT[0:64, h, gs_], in_=pb[bank][0:64, 0:512], func=AF.Copy),
                                     reads=[("ps", bank)], writes=[("KT", h, G_)])
                            for i in range(4):
                                tt = 4 * G_ + i
                                bank = i % 2
                                for c in range(8):
                                    S.op("pe", I("matmul", out=pb[bank][:, 0:384], lhsT=hT[:, c, tt * 128:(tt + 1) * 128], rhs=wM3[:, c, 768:1152],
                                                 start=(c == 0), stop=(c == 7)),
                                         reads=[("hT", tt), "wM"], writes=[("ps", bank)])
                                S.op("act", I("activation", out=Va[:, tt, :, 0:64], in_=pb[bank][:, 0:384].rearrange("p (h d) -> p h d", d=64), func=AF.Copy),
                                     reads=[("ps", bank), "Va1"], writes=[("Va", tt)])

                        if G == 0:
                            m_proj_kv(0)
                        m_proj_q(G)
                        for h in range(6):
                            S.op("dve", I("tensor_reduce", out=km[0:64, h, 2 * G:2 * G + 2], in_=KT[0:64, h, gsl].rearrange("p (n k) -> p n k", k=256),
                                                                      axis=AX.X, op=ALU.add),
                                 reads=[("KT", h, G), "km"], writes=[("km", h)])
                        S.op("dve", I("tensor_scalar", out=kmT[0:64, :, 2 * G:2 * G + 2], in0=km[0:64, :, 2 * G:2 * G + 2], scalar1=1.0 / 256, scalar2=None, op0=ALU.mult),
                             reads=[("km", h) for h in range(6)] + ["kmT"], writes=["kmT"])
                        for i in range(4):
                            for h in range(6):
                                S.op("pe", I("matmul", out=pb[5][:, i * 48 + h * 8:i * 48 + (h + 1) * 8], lhsT=QT[0:64, h, i * 128:(i + 1) * 128], rhs=kmT[0:64, h, :],
                                             start=True, stop=True),
                                     reads=[("QT", h), "kmT"], writes=[("ps", 5)])
                        if G < 3:
                            m_proj_kv(G + 1)
                        for i in range(4):
                            qt = 4 * G + i
                            nb = NM[i]
                            S.op("dve", I("tensor_tensor", out=gsb[i], in0=pb[5][:, i * 48:(i + 1) * 48].rearrange("p (h n) -> p h n", n=8), in1=bcm(ownmask[:, qt, :], 6), op=ALU.add),
                                 reads=[("ps", 5), "small"], writes=[("gsb", i)])
                            for h in range(6):
                                S.op("dve", I("max", out=m8[i][:, h, :], in_=gsb[i][:, h, :]), reads=[("gsb", i)], writes=[("m8", i, h)])
                            S.op("dve", I("tensor_tensor", out=ge[i], in0=gsb[i], in1=m8[i][:, :, 3:4].to_broadcast([128, 6, 8]), op=ALU.is_ge),
                                 reads=[("gsb", i)] + [("m8", i, h) for h in range(6)], writes=[("ge", i)])
                            S.op("dve", I("tensor_scalar", out=nb[:, :, 64:72], in0=ge[i], scalar1=-1.0, scalar2=-NEG, op0=ALU.add, op1=ALU.mult),
                                 reads=[("ge", i), ("NM", i)], writes=[("NM", i)])
                        for i in range(4):
                            nb = NM[i]
                            p4 = pbb(4)
                            for h in range(6):
                                S.op("pe", I("transpose", out=p4[0:72, h * 128:(h + 1) * 128], in_=nb[:, h, :], identity=ident[:]),
                                     reads=[("NM", i), "ident"], writes=[("ps", 4)])
                            S.op("act", I("activation", out=QT[64:72, :, i * 128:(i + 1) * 128], in_=p4[64:72, 0:768].rearrange("p (h k) -> p h k", k=128), func=AF.Copy),
                                 reads=[("ps", 4)] + [("QT", h) for h in range(6)], writes=[("QTn", i)])
                        nkt = 4 * G + 4
                        seq = [(h, kt) for h in range(6) for kt in range(nkt)]
                        LA = 3
                        slot = {}
                        pend = []
                        obs = {}
                        for h in range(6):
                            obs[h] = 6 + (oidx[0] % 2); oidx[0] += 1
                        for idx in range(len(seq) + LA):
                            if idx < len(seq):
                                h, kt = seq[idx]
                                rel = kt - 4 * G
                                c0 = max(0, rel) * 128
                                sbk = sidx[0] % 4; sidx[0] += 1
                                ptb = pidx[0] % NPT; pidx[0] += 1
                                slot[idx] = ptb
                                Pt = PT[ptb]
                                S.op("pe", I("matmul", out=pb[sbk][:, c0:512], lhsT=KT[0:72, h, kt * 128:(kt + 1) * 128], rhs=QT[0:72, h, c0:512],
                                             start=True, stop=True),
                                     reads=[("KT", h, kt // 4), ("KTi", h), ("QT", h)] + [("QTn", i) for i in range(4)], writes=[("ps", sbk)])
                                if -1 <= rel <= 3:
                                    if rel == -1:
                                        a0, a1, bsl = 0, 128, bt[:, h, 1, :]
                                    elif rel == 3:
                                        a0, a1, bsl = 384, 512, bt[:, h, 0, :]
                                    else:
                                        a0, a1, bsl = rel * 128, rel * 128 + 256, bt[:, h, :, :].rearrange("p a q -> p (a q)")
                                    S.op("dve", I("tensor_tensor", out=pb[sbk][:, a0:a1], in0=pb[sbk][:, a0:a1], in1=bsl, op=ALU.add),
                                         reads=[("ps", sbk), "bt"], writes=[("ps", sbk)])
                                S.op("act", I("activation", out=Pt[:, c0:512], in_=pb[sbk][:, c0:512], func=AF.Exp, bias=cfar[:, h:h + 1], scale=1.0),
                                     reads=[("ps", sbk), "cfar", ("PT", ptb)], writes=[("PTf", ptb)])
                            jdx = idx - LA
                            if jdx >= 0:
                                h, kt = seq[jdx]
                                j, hf = h // 2, h % 2
                                ob = obs[h]
                                rel = kt - 4 * G
                                c0 = max(0, rel) * 128
                                ptb = slot[jdx]
                                Pt = PT[ptb]
                                S.op("pe", I("matmul", out=pb[ob][0:65, c0:512], lhsT=Va[:, kt, h, 0:65], rhs=Pt[:, c0:512],
                                             start=(kt == 0), stop=(kt == nkt - 1)),
                                     reads=[("Va", kt), "Va1", ("PTf", ptb)], writes=[("ps", ob), ("PT", ptb)])
                                if kt == nkt - 1:
                                    S.op("dve", I("reciprocal", out=rec[64:65, :], in_=pb[ob][64:65, 0:512]), reads=[("ps", ob)], writes=["rec"])
                                    pend.append((idx + 2, h, ob))
                            while pend and (pend[0][0] <= idx or idx == len(seq) + LA - 1):
                                _, h, ob = pend.pop(0)
                                j, hf = h // 2, h % 2
                                S.op("pe", I("matmul", out=pb[5][0:64, 0:512], lhsT=ones_f[64:65, 0:64], rhs=rec[64:65, :], start=True, stop=True),
                                     reads=["rec", "ones_f"], writes=[("ps", 5)])
                                S.op("act", I("activation", out=OTs[0:64, :], in_=pb[ob][0:64, 0:512], func=AF.Copy), reads=[("ps", ob)], writes=["OTs"])
                                S.op("dve", I("tensor_tensor", out=catM[hf * 64:hf * 64 + 64, j, :], in0=OTs[0:64, :], in1=pb[5][0:64, 0:512], op=ALU.mult),
                                     reads=["OTs", ("ps", 5)], writes=[("catM", h)])
                        for i in range(4):
                            tt = 4 * G + i
                            outproj_partial(tt, lambda j, i=i: catM[:, j, i * 128:(i + 1) * 128], 3, woM3, False, [("catM", h) for h in range(6)],
                                            banks=((6, 7), (0, 1), (2, 3))[i % 3])

                    if stop == "M":
                        raise _Stop()
                    S.barrier(); A.reset()
                    wC3 = A.bf16(8 * 768).rearrange("p (c n) -> p c n", n=768)
                    woC3 = A.bf16(2 * 1024).rearrange("p (c n) -> p c n", n=1024)
                    catC = A.bf16(2 * SEQ).rearrange("p (j k) -> p j k", k=SEQ)
                    pbuf = [A.f32(514) for _ in range(2)]
                    ccsj = [A.f32(512) for _ in range(2)]; accj = [A.f32(512) for _ in range(2)]
                    wload("wC", wC3, w_in, 0, 8, 2688, 3456, "wC")
                    wload("wo", woC3, w_out, 768, 2, 0, 1024, "wo")
                    S.dma("sp", "cw", I("dma_start", out=cw[:], in_=cw_d[l]), writes=["cw"])
                    load_lnp(l, 0)
                    for j in range(2):
                        S.op("dve", I("memset", pbuf[j][:, 0:2], 0.0), writes=[("pbuf", j)])
                    def c_conv(G):
                        gsl = slice(G * 512, (G + 1) * 512)
                        hTr = [("hT", 4 * G + i) for i in range(4)]
                        for j in range(2):
                            bk = ((0, 1, 2), (3, 5, 7))[j]
                            for part in range(3):
                                c0 = part * 256 + j * 128
                                for c in range(8):
                                    S.op("pe", I("matmul", out=pb[bk[part]][:, 0:512], lhsT=wC3[:, c, c0:c0 + 128], rhs=hT[:, c, gsl],
                                                 start=(c == 0), stop=(c == 7)),
                                         reads=hTr + ["wC"], writes=[("ps", bk[part])])
                        for j in range(2):
                            bk = ((0, 1, 2), (3, 5, 7))[j]
                            pbj = pbuf[j]; acj = accj[j]
                            S.op("act", I("activation", out=ccsj[j], in_=pb[bk[1]][:, 0:512], func=AF.Copy), reads=[("ps", bk[1])], writes=[("ccs", j)])
                            S.op("dve", I("tensor_tensor", out=pbj[:, 2:514], in0=pb[bk[2]][:, 0:512], in1=ccsj[j], op=ALU.mult),
                                 reads=[("ps", bk[2]), ("ccs", j), ("pbuf", j)], writes=[("pbufm", j)])
                            S.op("dve", I("tensor_scalar", out=acj, in0=pbj[:, 2:514], scalar1=cw[:, j * 3 + 2:j * 3 + 3], scalar2=None, op0=ALU.mult),
                                 reads=[("pbufm", j), "cw"], writes=[("acc", j)])
                            S.op("dve", I("scalar_tensor_tensor", out=acj, in0=pbj[:, 1:513], scalar=cw[:, j * 3 + 1:j * 3 + 2], in1=acj, op0=ALU.mult, op1=ALU.add),
                                 reads=[("pbufm", j), ("pbuf", j), ("acc", j), "cw"], writes=[("acc", j)])
                            S.op("dve", I("scalar_tensor_tensor", out=acj, in0=pbj[:, 0:512], scalar=cw[:, j * 3:j * 3 + 1], in1=acj, op0=ALU.mult, op1=ALU.add),
                                 reads=[("pbufm", j), ("pbuf", j), ("acc", j), "cw"], writes=[("acc", j)])
                            S.op("dve", I("tensor_tensor", out=catC[:, j, gsl], in0=pb[bk[0]][:, 0:512], in1=acj, op=ALU.mult),
                                 reads=[("ps", bk[0]), ("acc", j)], writes=[("catC", j, G)])
                            S.op("dve", I("tensor_copy", out=pbj[:, 0:2], in_=pbj[:, 512:514]),
                                 reads=[("pbufm", j)], writes=[("pbuf", j)])

                    def c_tail(G):
                        for i in range(4):
                            tt = 4 * G + i
                            outproj_partial(tt, lambda j, tt=tt: catC[:, j, tt * 128:(tt + 1) * 128], 2, woC3, False, [("catC", 0, G), ("catC", 1, G)],
                                            banks=(6, 4))
                        emit_ln_batch([4 * G + i for i in range(4)])
                        for i in range(4):
                            emit_hT(4 * G + i)

                    for G in range(5):
                        if G < 4:
                            c_conv(G)
                        if G >= 1:
                            c_tail(G - 1)

                    if stop == "C":
                        raise _Stop()
                    S.barrier(); A.reset()
                    Gp = A.bf16(8 * SEQ).rearrange("p (j k) -> p j k", k=SEQ)
                    wD3 = A.bf16(8 * 1024).rearrange("p (j n) -> p j n", n=1024)
                    wUflat = [A.bf16(8 * 512) for _ in range(2)]
                    wU = [w_.rearrange("p (c n) -> p c n", n=512) for w_ in wUflat]
                    abuf = [A.f32(514) for _ in range(2)]
                    accs = [A.f32(512) for _ in range(2)]
                    gls = [A.f32(512) for _ in range(2)]
                    S.dma("sp", "fcw", I("dma_start", out=fcw[:], in_=fcw_d[l]), writes=["fcw"])
                    load_lnp(l, 2)
                    parts = [(0, 8), (8, 16), (16, 22)]
                    upc = [0]; itc = [0]; bai = [0]; bbi = [0]; dpi = [0]
                    for pi, (j0, j1) in enumerate(parts):
                        nj = j1 - j0
                        wload("wD", wD3, w_dn, j0 * 128, nj, 0, 1024, "wD")
                        its = []
                        for jp in range(j0 // 2, j1 // 2):
                            wb = upc[0] % 2; upc[0] += 1
                            for sub in range(2):
                                for G in range(4):
                                    its.append((jp, wb, sub, G))

                        def f_stage1(i):
                            jp, wb, sub, G = its[i]
                            wu = wU[wb]
                            if sub == 0 and G == 0:
                                nxt = [(jp2, wb2) for (jp2, wb2, s2, g2) in its if jp2 == jp + 1 and s2 == 0 and g2 == 0]
                                if nxt:
                                    S.dma("sp", f"wU{nxt[0][1]}", I("dma_start", out=wUflat[nxt[0][1]], in_=wup_b[l, nxt[0][0]]), writes=[("wU", nxt[0][1])])
                            jg = 2 * jp + sub
                            st = itc[0] % 2; itc[0] += 1
                            fset[i] = st
                            ab = abuf[st]; ac = accs[st]
                            gsl = slice(G * 512, (G + 1) * 512)
                            hTr = [("hT", 4 * G + q) for q in range(4)]
                            ba = bai[0] % 2; bai[0] += 1
                            bb = (2, 3, 5)[bbi[0] % 3]; bbi[0] += 1
                            fbb[i] = bb
                            for c in range(8):
                                S.op("pe", I("matmul", out=pb[ba][:, 0:512], lhsT=wu[:, c, sub * 128:(sub + 1) * 128], rhs=hT[:, c, gsl],
                                             start=(c == 0), stop=(c == 7)),
                                     reads=hTr + [("wU", wb)], writes=[("ps", ba)])
                            for c in range(8):
                                S.op("pe", I("matmul", out=pb[bb][:, 0:512], lhsT=wu[:, c, 256 + sub * 128:256 + (sub + 1) * 128], rhs=hT[:, c, gsl],
                                             start=(c == 0), stop=(c == 7)),
                                     reads=hTr + [("wU", wb)], writes=[("ps", bb)])
                            if G == 0:
                                S.op("pool", I("memset", ab[:, 0:2], 0.0), writes=[("abh", st)])
                            else:
                                S.op("pool", I("tensor_copy", out=ab[:, 0:2], in_=abuf[1 - st][:, 512:514]), reads=[("abm", 1 - st)], writes=[("abh", st)])
                            S.op("act", I("activation", out=ab[:, 2:514], in_=pb[ba][:, 0:512], func=AF.Copy),
                                 reads=[("ps", ba)], writes=[("abm", st)])
                            k3 = jg * 3
                            S.op("act", I("activation", out=ac, in_=pb[ba][:, 0:512], func=AF.Identity, scale=fcw[:, k3 + 2:k3 + 3]),
                                 reads=[("ps", ba), "fcw"], writes=[("acc", st)])
                            S.op("dve", I("scalar_tensor_tensor", out=ac, in0=ab[:, 1:513], scalar=fcw[:, k3 + 1:k3 + 2], in1=ac, op0=ALU.mult, op1=ALU.add),
                                 reads=[("abm", st), ("abh", st), ("acc", st), "fcw"], writes=[("acc", st)])
                            S.op("dve", I("scalar_tensor_tensor", out=ac, in0=ab[:, 0:512], scalar=fcw[:, k3:k3 + 1], in1=ac, op0=ALU.mult, op1=ALU.add),
                                 reads=[("abm", st), ("abh", st), ("acc", st), "fcw"], writes=[("acc", st)])

                        def f_stage2(i):
                            jp, wb, sub, G = its[i]
                            st = fset[i]; bb = fbb[i]
                            jj = 2 * jp + sub - j0
                            gsl = slice(G * 512, (G + 1) * 512)
                            S.op("act", I("activation", out=gls[st], in_=accs[st], func=AF.Gelu), reads=[("acc", st)], writes=[("gl", st)])
                            S.op("dve", I("tensor_tensor", out=Gp[:, jj, gsl], in0=pb[bb][:, 0:512], in1=gls[st], op=ALU.mult),
                                 reads=[("ps", bb), ("gl", st)], writes=[("Gp", jj, G)])

                        fset = {}; fbb = {}
                        S.dma("sp", f"wU{its[0][1]}", I("dma_start", out=wUflat[its[0][1]], in_=wup_b[l, its[0][0]]), writes=[("wU", its[0][1])])
                        for i in range(len(its) + 1):
                            if i < len(its):
                                f_stage1(i)
                            if i >= 1:
                                f_stage2(i - 1)
                        last = (pi == len(parts) - 1)
                        for t in range(NT):
                            G = t // 4
                            outproj_partial_reads = [("Gp", jj, G) for jj in range(nj)]
                            dbk = ((6, 7), (0, 1), (2, 3))[dpi[0] % 3]; dpi[0] += 1
                            for hf in range(2):
                                for jj in range(nj):
                                    S.op("pe", I("matmul", out=pb[dbk[hf]][:, 0:512], lhsT=Gp[:, jj, t * 128:(t + 1) * 128], rhs=wD3[:, jj, hf * 512:(hf + 1) * 512],
                                                 start=(jj == 0), stop=(jj == nj - 1)),
                                         reads=outproj_partial_reads + ["wD"], writes=[("ps", dbk[hf])])
                            for hf in range(2):
                                xsl = xs[:, t, hf * 512:(hf + 1) * 512]
                                if pi == 0:
                                    S.op("dve", I("scalar_tensor_tensor", out=xsl, in0=xsl, scalar=ALPHA, in1=pb[dbk[hf]][:, 0:512], op0=ALU.mult, op1=ALU.add),
                                         reads=[("ps", dbk[hf]), ("xs", t)], writes=[("xs", t)])
                                else:
                                    S.op("dve", I("tensor_tensor", out=xsl, in0=xsl, in1=pb[dbk[hf]][:, 0:512], op=ALU.add),
                                         reads=[("ps", dbk[hf]), ("xs", t)], writes=[("xs", t)])
                            if last and t % 4 == 3:
                                for tl in ([[t - 7, t - 6, t - 5, t - 4]] if t >= 7 else []) + ([[t - 3, t - 2, t - 1, t]] if t == NT - 1 else []):
                                    emit_ln_batch(tl)
                                    for t2 in tl:
                                        if l == depth - 1:
                                            S.dma("sp", "ostore", I("dma_start", out=out_d[s, t2 * 128:(t2 + 1) * 128, :], in_=xs[:, t2, :]),
                                                  reads=[("xs", t2)])
                                        else:
                                            emit_hT(t2)
                S.barrier()

        except _Stop:
            S.barrier()
            for t in range(NT):
                S.dma("sp", "ostore", I("dma_start", out=out_d[0, t * 128:(t + 1) * 128, :], in_=xs[:, t, :]), reads=[("xs", t)])
        S.barrier()
        S.emit()
    return nc


_NC_CACHE = {}


def _get_nc(nseq, depth):
    key = (nseq, depth)
    if key not in _NC_CACHE:
        _NC_CACHE[key] = build(nseq, depth)
    return _NC_CACHE[key]


def host_inputs(x_shard, w_in, conv_w, w_out, ln1_g, ln1_b, w_up, ffn_conv_w, w_down, ln2_g, ln2_b, rel_bias):
    cs_h, small_h, gC, ind_h, bidx_d, bidx_s, causal = _host_consts()
    f = lambda a: np.ascontiguousarray(a, dtype=np.float32)
    lnp = np.broadcast_to(np.stack([ln1_g, ln1_b, ln2_g, ln2_b], axis=1)[:, :, None, :], (DEPTH, 4, 128, D))
    cw = conv_w.reshape(DEPTH, 3, 2, 128).transpose(0, 3, 2, 1).reshape(DEPTH, 128, 6)
    fcw = ffn_conv_w.reshape(DEPTH, 3, 22, 128).transpose(0, 3, 2, 1).reshape(DEPTH, 128, 66)
    bd = np.where(causal[:, :, None], rel_bias[bidx_d], np.float32(NEG))
    bs = rel_bias[bidx_s]
    btile = np.stack([bd, bs], axis=0).transpose(1, 3, 0, 2).reshape(128, 6 * 2 * 128)
    cfar = np.broadcast_to(rel_bias[31][None, :], (128, 6))
    return {
        "x": f(x_shard), "w_in": f(w_in), "w_out": f(w_out), "w_up": f(w_up), "w_down": f(w_down),
        "lnp": f(lnp), "cw": f(cw), "fcw": f(fcw), "cs": f(cs_h.reshape(128, -1)), "btile": f(btile),
        "cfar": f(cfar), "small": f(small_h), "ind": f(ind_h),
    }


def kernel(x, w_in, conv_w, w_out, ln1_g, ln1_b, w_up, ffn_conv_w, w_down, ln2_g, ln2_b, rel_bias):
    args = [np.asarray(a) for a in (w_in, conv_w, w_out, ln1_g, ln1_b, w_up, ffn_conv_w, w_down, ln2_g, ln2_b, rel_bias)]
    x = np.asarray(x)
    B = x.shape[0]
    per = B // N_CORES
    nc = _get_nc(per, DEPTH)
    in_maps = []
    base = None
    for c in range(N_CORES):
        m = host_inputs(x[c * per:(c + 1) * per], *args) if base is None else dict(base, x=np.ascontiguousarray(x[c * per:(c + 1) * per], dtype=np.float32))
        if base is None:
            base = m
        in_maps.append(m)
    res = run_bass_kernel_spmd(nc, in_maps, core_ids=list(range(N_CORES)))
    return np.concatenate([np.asarray(r["out"]) for r in res.results], axis=0).astype(np.float32)
```
